# Optimizing a Trainium2 kernel written in Bass

```python
import math
import jax, jax.numpy as jnp
from jax import lax
import numpy as np

D_MODEL = 2048
BATCH = 4
SEQ = 2048
DEPTH = 1
DEC_BATCH = 128
DEC_SEQ = 1
PAST_LEN = 16384
PAGE_SIZE = 128

A_CHUNK = 128
A_GROUP_DIM = 128
A_GROUPS = D_MODEL // A_GROUP_DIM
D_A = A_GROUPS * A_GROUP_DIM
B_HEADS = 16
B_HEAD_DIM = 128
D_B = B_HEADS * B_HEAD_DIM
CONV_W = 4
DELTA_CHUNK = 64
D_FF = 4 * D_MODEL
D_IN = 2 * D_A + 4 * D_B + 2 * B_HEADS + 2 * D_MODEL
ALPHA = (2.0 * DEPTH) ** 0.25
BETA_INIT = (8.0 * DEPTH) ** -0.25
LN_EPS = 1e-5
RMS_EPS = 1e-6

kernel_name = "hybrid_gmlp_gated_deltanet_deepnorm_step"


def layer_norm(x, g, b):
    xf = x.astype(jnp.float32)
    mu = jnp.mean(xf, axis=-1, keepdims=True)
    var = jnp.mean(jnp.square(xf - mu), axis=-1, keepdims=True)
    return ((xf - mu) * lax.rsqrt(var + LN_EPS) * g + b).astype(x.dtype)


def l2norm(x):
    xf = x.astype(jnp.float32)
    return (xf * lax.rsqrt(jnp.sum(jnp.square(xf), axis=-1, keepdims=True) + RMS_EPS)).astype(x.dtype)


def gated_rms_norm(o, z, w):
    of = o.astype(jnp.float32)
    of = of * lax.rsqrt(jnp.mean(jnp.square(of), axis=-1, keepdims=True) + RMS_EPS)
    return (of * w * jax.nn.silu(z.astype(jnp.float32))).astype(o.dtype)


def chunk_spatial_gate(u, v, w_s, b_s):
    b, t, _ = v.shape
    pad = (-t) % A_CHUNK
    nc = (t + pad) // A_CHUNK
    vp = jnp.pad(v, ((0, 0), (0, pad), (0, 0))).reshape(b, nc, A_CHUNK, A_GROUPS, A_GROUP_DIM)
    idx = jnp.arange(A_CHUNK)
    causal = idx[:, None] >= idx[None, :]
    w = jnp.where(causal[None], w_s, 0.0)
    mixed = jnp.einsum('gts,bcsgd->bctgd', w, vp) + jnp.transpose(b_s)[None, None, :, :, None]
    mixed = mixed.reshape(b, nc * A_CHUNK, D_A)[:, :t]
    return u * mixed.astype(u.dtype)


def gated_delta_rule(q, k, v, beta, g, s0):
    b, t, h, dk = q.shape
    dv = v.shape[-1]
    out_dtype = v.dtype
    pad = (-t) % DELTA_CHUNK
    nc = (t + pad) // DELTA_CHUNK

    def blocks(a):
        a = a.astype(jnp.float32)
        a = jnp.pad(a, [(0, 0), (0, pad)] + [(0, 0)] * (a.ndim - 2))
        a = a.reshape((b, nc, DELTA_CHUNK) + a.shape[2:])
        return jnp.moveaxis(a, 3, 1)

    q, k, v, beta, g = blocks(q), blocks(k), blocks(v), blocks(beta), blocks(g)
    gc = jnp.cumsum(g, axis=-1)
    idx = jnp.arange(DELTA_CHUNK)
    incl = idx[:, None] >= idx[None, :]
    strict = idx[:, None] > idx[None, :]
    decay = jnp.exp(jnp.where(incl, gc[..., :, None] - gc[..., None, :], -jnp.inf))
    kb = k * beta[..., None]
    vb = v * beta[..., None]
    n = jnp.where(strict, jnp.einsum('bhnid,bhnjd->bhnij', kb, k) * decay, 0.0)
    eye = jnp.eye(DELTA_CHUNK, dtype=jnp.float32)
    tmat = lax.linalg.triangular_solve(eye + n, jnp.broadcast_to(eye, n.shape),
                                       left_side=True, lower=True, unit_diagonal=True)
    w_val = jnp.einsum('bhnij,bhnje->bhnie', tmat, vb)
    k_cd = jnp.einsum('bhnij,bhnjd->bhnid', tmat, kb * jnp.exp(gc)[..., None])
    attn = jnp.einsum('bhnid,bhnjd->bhnij', q, k) * decay
    q_dec = q * jnp.exp(gc)[..., None]
    k_dec = k * jnp.exp(gc[..., -1:] - gc)[..., None]
    g_last = jnp.exp(gc[..., -1])
    xs = (jnp.moveaxis(w_val, 2, 0), jnp.moveaxis(k_cd, 2, 0), jnp.moveaxis(attn, 2, 0),
          jnp.moveaxis(q_dec, 2, 0), jnp.moveaxis(k_dec, 2, 0), jnp.moveaxis(g_last, 2, 0))

    def step(s, xc):
        w_c, kcd_c, attn_c, qd_c, kd_c, gl_c = xc
        v_new = w_c - jnp.einsum('bhid,bhde->bhie', kcd_c, s)
        o_c = jnp.einsum('bhid,bhde->bhie', qd_c, s) + jnp.einsum('bhij,bhje->bhie', attn_c, v_new)
        s = s * gl_c[..., None, None] + jnp.einsum('bhid,bhie->bhde', kd_c, v_new)
        return s, o_c

    s_fin, o = lax.scan(step, s0.astype(jnp.float32), xs)
    o = jnp.transpose(o, (1, 0, 3, 2, 4)).reshape(b, nc * DELTA_CHUNK, h, dv)[:, :t]
    return o.astype(out_dtype), s_fin.astype(s0.dtype)


def hybrid_layer(x, conv_buf, s0, w_in, w_s, b_s, ln_v_g, ln_v_b, w_conv, a_log, dt_bias, w_onorm,
                 w_proj_a, w_proj_b, w_o, ln1_g, ln1_b, w_up, w_down, ln2_g, ln2_b):
    b, t, _ = x.shape
    p = jnp.einsum('btd,de->bte', x, w_in)
    sizes = [D_A, D_A, 3 * D_B, D_B, B_HEADS, B_HEADS, D_MODEL, D_MODEL]
    points = [int(s) for s in np.cumsum(sizes)[:-1]]
    u, va, qkv, z, beta_logit, a_logit, gate_a, gate_b = jnp.split(p, points, axis=-1)
    u = jax.nn.gelu(u, approximate=False)
    va = layer_norm(jax.nn.gelu(va, approximate=False), ln_v_g, ln_v_b)
    y_a = chunk_spatial_gate(u, va, w_s, b_s)
    xp = jnp.concatenate([conv_buf.astype(qkv.dtype), qkv], axis=1)
    new_conv = xp[:, t:]
    acc = xp[:, 0:t] * w_conv[0]
    for j in range(1, CONV_W):
        acc = acc + xp[:, j:j + t] * w_conv[j]
    qkv_c = jax.nn.silu(acc)
    q, k, vb = jnp.split(qkv_c, 3, axis=-1)
    q = l2norm(q.reshape(b, t, B_HEADS, B_HEAD_DIM)) * (B_HEAD_DIM ** -0.5)
    k = l2norm(k.reshape(b, t, B_HEADS, B_HEAD_DIM))
    vb = vb.reshape(b, t, B_HEADS, B_HEAD_DIM)
    beta = jax.nn.sigmoid(beta_logit.astype(jnp.float32))
    g = -jnp.exp(a_log.astype(jnp.float32)) * jax.nn.softplus(a_logit.astype(jnp.float32) + dt_bias.astype(jnp.float32))
    o, s_new = gated_delta_rule(q, k, vb, beta, g, s0)
    y_b = gated_rms_norm(o, z.reshape(b, t, B_HEADS, B_HEAD_DIM), w_onorm).reshape(b, t, D_B)
    m = jax.nn.sigmoid(gate_a) * (y_a @ w_proj_a) + jax.nn.sigmoid(gate_b) * (y_b @ w_proj_b)
    x1 = layer_norm(ALPHA * x + m @ w_o, ln1_g, ln1_b)
    h = jnp.square(jax.nn.relu(x1 @ w_up)) @ w_down
    y = layer_norm(ALPHA * x1 + h, ln2_g, ln2_b)
    return y, va, new_conv, s_new


def setup_inputs(seed: int = 0) -> dict:
    key = jax.random.key(seed)
    ks = jax.random.split(key, 24)

    def nrm(k, shape, scale):
        return jax.random.normal(k, shape, jnp.float32) * scale

    x_prompt = nrm(ks[0], (BATCH, SEQ, D_MODEL), 1.0)
    x_sample = nrm(ks[1], (DEC_BATCH, DEC_SEQ, D_MODEL), 1.0)
    state_conv = nrm(ks[2], (DEPTH, DEC_BATCH, CONV_W - 1, 3 * D_B), 1.0)
    state_ssm = nrm(ks[3], (DEPTH, DEC_BATCH, B_HEADS, B_HEAD_DIM, B_HEAD_DIM), 0.1)
    w_in = nrm(ks[4], (DEPTH, D_MODEL, D_IN), D_MODEL ** -0.5)
    w_s = nrm(ks[5], (DEPTH, A_GROUPS, A_CHUNK, A_CHUNK), A_CHUNK ** -0.5)
    b_s = 1.0 + nrm(ks[6], (DEPTH, A_GROUPS, A_CHUNK), 0.01)
    ln_v_g = 1.0 + nrm(ks[7], (DEPTH, D_A), 0.02)
    ln_v_b = nrm(ks[8], (DEPTH, D_A), 0.02)
    w_conv = nrm(ks[9], (DEPTH, CONV_W, 3 * D_B), CONV_W ** -0.5)
    a_log = jnp.log(jax.random.uniform(ks[10], (DEPTH, B_HEADS), jnp.float32, minval=1.0, maxval=16.0))
    dt = jnp.exp(jax.random.uniform(ks[11], (DEPTH, B_HEADS), jnp.float32,
                                    minval=math.log(1e-3), maxval=math.log(1e-1)))
    dt_bias = dt + jnp.log(-jnp.expm1(-dt))
    w_onorm = 1.0 + nrm(ks[12], (DEPTH, B_HEAD_DIM), 0.02)
    w_proj_a = nrm(ks[13], (DEPTH, D_A, D_MODEL), D_A ** -0.5)
    w_proj_b = nrm(ks[14], (DEPTH, D_B, D_MODEL), D_B ** -0.5)
    w_o = nrm(ks[15], (DEPTH, D_MODEL, D_MODEL), BETA_INIT * D_MODEL ** -0.5)
    ln1_g = 1.0 + nrm(ks[16], (DEPTH, D_MODEL), 0.02)
    ln1_b = nrm(ks[17], (DEPTH, D_MODEL), 0.02)
    w_up = nrm(ks[18], (DEPTH, D_MODEL, D_FF), D_MODEL ** -0.5)
    w_down = nrm(ks[19], (DEPTH, D_FF, D_MODEL), BETA_INIT * D_FF ** -0.5)
    ln2_g = 1.0 + nrm(ks[20], (DEPTH, D_MODEL), 0.02)
    ln2_b = nrm(ks[21], (DEPTH, D_MODEL), 0.02)
    return {"x_prompt": x_prompt, "x_sample": x_sample, "state_conv": state_conv, "state_ssm": state_ssm,
            "w_in": w_in, "w_s": w_s, "b_s": b_s, "ln_v_g": ln_v_g, "ln_v_b": ln_v_b, "w_conv": w_conv,
            "a_log": a_log, "dt_bias": dt_bias, "w_onorm": w_onorm, "w_proj_a": w_proj_a,
            "w_proj_b": w_proj_b, "w_o": w_o, "ln1_g": ln1_g, "ln1_b": ln1_b, "w_up": w_up,
            "w_down": w_down, "ln2_g": ln2_g, "ln2_b": ln2_b}


def reference(x_prompt, x_sample, state_conv, state_ssm, w_in, w_s, b_s, ln_v_g, ln_v_b, w_conv, a_log,
              dt_bias, w_onorm, w_proj_a, w_proj_b, w_o, ln1_g, ln1_b, w_up, w_down, ln2_g, ln2_b):
    yp = x_prompt
    ys = x_sample
    bp = x_prompt.shape[0]
    conv_p, ssm_p, vrows_s, conv_s, ssm_s = [], [], [], [], []
    for l in range(DEPTH):
        lw = (w_in[l], w_s[l], b_s[l], ln_v_g[l], ln_v_b[l], w_conv[l], a_log[l], dt_bias[l], w_onorm[l],
              w_proj_a[l], w_proj_b[l], w_o[l], ln1_g[l], ln1_b[l], w_up[l], w_down[l], ln2_g[l], ln2_b[l])
        conv0 = jnp.zeros((bp, CONV_W - 1, 3 * D_B), x_prompt.dtype)
        ssm0 = jnp.zeros((bp, B_HEADS, B_HEAD_DIM, B_HEAD_DIM), x_prompt.dtype)
        yp, _, cp, sp = hybrid_layer(yp, conv0, ssm0, *lw)
        ys, vs, cs, ss = hybrid_layer(ys, state_conv[l], state_ssm[l], *lw)
        conv_p.append(cp)
        ssm_p.append(sp)
        vrows_s.append(vs)
        conv_s.append(cs)
        ssm_s.append(ss)
    new_conv_prompt = jnp.stack(conv_p)
    new_ssm_prompt = jnp.stack(ssm_p)
    new_gmlp_v_sample = jnp.stack(vrows_s)
    new_conv_sample = jnp.stack(conv_s)
    new_ssm_sample = jnp.stack(ssm_s)
    return (yp, ys, new_conv_prompt, new_ssm_prompt, new_gmlp_v_sample, new_conv_sample, new_ssm_sample)
```

```python
import os
import numpy as np
from contextlib import ExitStack
import concourse.bass as bass
import concourse.mybir as mybir
from concourse.bass_utils import run_bass_kernel_spmd

F32 = mybir.dt.float32
BF16 = mybir.dt.bfloat16
U8 = mybir.dt.uint8
AF = mybir.ActivationFunctionType
ALU = mybir.AluOpType
AX = mybir.AxisListType

ALPHA = 2.0 ** 0.25
LN_EPS = 1e-5
RMS_EPS = 1e-6
NOWN = 1040
NCO = 1043
NCP = 1027
BIGMASK = 30000.0
OU, OVA, OQ, OK_, OV, OZ, OBETA, OA, OGA, OGB = 0, 2048, 4096, 6144, 8192, 10240, 12288, 12304, 12320, 14368


class Buf:
    __slots__ = ("w", "r", "excl")

    def __init__(self):
        self.w = {}
        self.r = {}
        self.excl = False


class T:
    def __init__(self, ap, b=None):
        self.ap = ap
        self.b = b if b is not None else Buf()

    def __getitem__(self, k):
        return self.ap[k]


def _bufs(ts):
    return [t.b if isinstance(t, T) else t for t in ts]


class Ctx:
    NDS = {"sp": 8, "pool": 8}

    def __init__(self, nc, es):
        self.nc = nc
        self.E = {"pe": nc.tensor, "dve": nc.vector, "act": nc.scalar, "pool": nc.gpsimd, "sp": nc.sync}
        self.sems = {}
        self.cnt = {}
        self.seen = {k: {} for k in self.E}
        for k in self.E:
            self.sems[k] = es.enter_context(nc.semaphore("sm_" + k))
            self.cnt[k] = 0
        self.dq = {}
        for q, n in self.NDS.items():
            lst = []
            for i in range(n):
                key = "d_%s%d" % (q, i)
                self.sems[key] = es.enter_context(nc.semaphore(key))
                self.cnt[key] = 0
                lst.append(key)
            self.dq[q] = [lst, 0]
        self.nwait = 0
        self.nins = 0

    def wait(self, eng, key, val):
        if val <= 0:
            return
        if eng == "pe" and key == "pe":
            return
        if self.seen[eng].get(key, 0) >= val:
            return
        self.E[eng].wait_ge(self.sems[key], val)
        self.seen[eng][key] = val
        self.nwait += 1

    def _deps(self, eng, reads, writes):
        deps = {}
        for b in reads:
            for k, v in b.w.items():
                if deps.get(k, 0) < v:
                    deps[k] = v
            if b.excl:
                for k, v in b.r.items():
                    if k != eng and deps.get(k, 0) < v:
                        deps[k] = v
        for b in writes:
            for k, v in b.w.items():
                if deps.get(k, 0) < v:
                    deps[k] = v
            for k, v in b.r.items():
                if deps.get(k, 0) < v:
                    deps[k] = v
        for k, v in deps.items():
            self.wait(eng, k, v)

    def _mark(self, key, val, reads, writes):
        for b in reads:
            if b.r.get(key, 0) < val:
                b.r[key] = val
        for b in writes:
            b.w = {key: val}
            b.r = {}

    def op(self, eng, fn, reads=(), writes=(), inc=True):
        reads = _bufs(reads)
        writes = _bufs(writes)
        self._deps(eng, reads, writes)
        ins = fn(self.E[eng])
        val = self.cnt[eng] + 1
        if inc:
            ins.then_inc(self.sems[eng], 1)
            self.cnt[eng] = val
        self._mark(eng, val, reads, writes)
        self.nins += 1
        return ins

    def dma(self, q, out, in_, reads=(), writes=()):
        reads = _bufs(reads)
        writes = _bufs(writes)
        lst, i = self.dq[q]
        key = lst[i % len(lst)]
        self.dq[q][1] = i + 1
        self.wait(q, key, self.cnt[key])
        self._deps(q, reads, writes)
        ins = self.E[q].dma_start(out=out, in_=in_)
        val = self.cnt[key] + 16
        ins.then_inc(self.sems[key], 16)
        self.cnt[key] = val
        self._mark(key, val, reads, writes)
        self.nins += 1
        return ins

    def barrier(self):
        for e in self.E:
            for k in self.sems:
                self.wait(e, k, self.cnt[k])

    def finish(self):
        for k in self.sems:
            self.wait("sp", k, self.cnt[k])

    def mm(self, out, lhsT, rhs, R, W, start=True, stop=True):
        return self.op("pe", lambda e: e.matmul(out, lhsT=lhsT, rhs=rhs, start=start, stop=stop), R, [W], inc=stop)

    def tr(self, out, in_, ident, R, W):
        return self.op("pe", lambda e: e.transpose(out, in_, ident), R, [W])

    def act(self, out, in_, func, R, W, **kw):
        return self.op("act", lambda e: e.activation(out=out, in_=in_, func=func, **kw), R, W)

    def tt(self, eng, out, in0, in1, op, R, W):
        return self.op(eng, lambda e: e.tensor_tensor(out=out, in0=in0, in1=in1, op=op), R, W)

    def ts(self, eng, out, in0, s1, s2, op0, op1, R, W):
        if s2 is None:
            return self.op(eng, lambda e: e.tensor_scalar(out=out, in0=in0, scalar1=s1, scalar2=None, op0=op0), R, W)
        return self.op(eng, lambda e: e.tensor_scalar(out=out, in0=in0, scalar1=s1, scalar2=s2, op0=op0, op1=op1), R, W)

    def stt(self, eng, out, in0, scalar, in1, op0, op1, R, W):
        return self.op(eng, lambda e: e.scalar_tensor_tensor(out=out, in0=in0, scalar=scalar, in1=in1, op0=op0, op1=op1), R, W)

    def cp(self, eng, out, in_, R, W):
        if eng == "act":
            return self.act(out, in_, AF.Copy, R, W)
        return self.op(eng, lambda e: e.tensor_copy(out=out, in_=in_), R, W)


ESZ = {F32: 4, BF16: 2, U8: 1}


class Arena:
    def __init__(self, big, lo, hi):
        self.big = big
        self.lo = lo
        self.hi = hi
        self.p = lo

    def ap(self, shape, dtype):
        free = 1
        for s in shape[1:]:
            free *= s
        nb = free * ESZ[dtype]
        nba = (nb + 63) // 64 * 64
        assert self.p + nba <= self.hi, ("arena overflow", self.p, nba, self.hi)
        a = self.big[0:shape[0], self.p:self.p + nb].bitcast(dtype)
        if len(shape) > 2:
            names = ["a%d" % i for i in range(len(shape) - 1)]
            a = a.rearrange("p (%s) -> p %s" % (" ".join(names), " ".join(names)),
                            **{n: s for n, s in zip(names, shape[1:])})
        self.p += nba
        return a

    def t(self, shape, dtype):
        return T(self.ap(shape, dtype))

    def ring(self, n, shape, dtype):
        return Ring([self.t(shape, dtype) for _ in range(n)])


class Ring:
    def __init__(self, items):
        self.items = items
        self.i = 0

    def next(self):
        t = self.items[self.i % len(self.items)]
        self.i += 1
        return t


def bc(ap, shape, axis):
    return ap.unsqueeze(axis).to_broadcast(shape)


class Builder:
    def __init__(self, phases=("P", "B", "A", "M", "T"), dbg=()):
        self.phases = phases
        self.dbg = dbg
        self.nc = bass.Bass("TRN2", target_bir_lowering=False)
        self.es = ExitStack()
        self.build()

    def din(self, name, shape):
        return self.nc.dram_tensor(name, list(shape), F32, kind="ExternalInput").ap()

    def dout(self, name, shape):
        return self.nc.dram_tensor(name, list(shape), F32, kind="ExternalOutput").ap()

    def build(self):
        nc, es = self.nc, self.es
        d = self.d = {}
        for name, shape in [
            ("xTo", (2048, NCO)), ("xTp", (2048, NCP)), ("xtok", (NOWN, 2048)),
            ("w_in", (2048, 16416)), ("w_pa", (2048, 2048)), ("w_pb", (2048, 2048)), ("w_o", (2048, 2048)),
            ("w_up", (2048, 8192)), ("w_dn", (8192, 2048)),
            ("w_sT", (128, 16, 128)), ("bs_row", (1, 2048)), ("bs0", (1, 16, 16)), ("w00", (16, 16)),
            ("lnv_g", (128, 2048)), ("lnv_b", (128, 2048)), ("ln1_g", (128, 2048)), ("ln1_b", (128, 2048)),
            ("ln2_g", (128, 2048)), ("ln2_b", (128, 2048)),
            ("wconv", (128, 48, 4)), ("alog", (128, 16)), ("dtb", (128, 16)), ("wonorm", (128, 1)),
            ("scT", (128, 48, 3, 16)), ("ssm", (16, 16, 128, 128)),
            ("ident", (128, 128)), ("maskNb", (128, 128)), ("maskAb", (128, 128)), ("ucs", (128, 128)),
            ("bones", (128, 128)), ("ci", (128, 2, 128)), ("tri", (128, 128)), ("i16b", (128, 16, 16)),
        ]:
            d[name] = self.din(name, shape)
        for name, shape in [
            ("y_out", (NOWN, 2048)), ("ncp", (128, 48, 3)), ("ssm_p", (128, 16, 128)), ("vs", (16, 2048)),
            ("ncs", (128, 48, 3, 16)), ("ssm_s", (16, 16, 128, 128)),
        ]:
            d[name] = self.dout(name, shape)
        for name, shape in self.dbg:
            if name.startswith("in_"):
                d[name] = self.din(name, shape)
            else:
                d[name] = self.dout(name, shape)

        C = self.C = Ctx(nc, es)
        sb = lambda n, s, dt: es.enter_context(nc.sbuf_tensor("sb_" + n, list(s), dt))
        self.ident_f = T(sb("ident_f", (128, 128), F32)[:])
        self.ident_b = T(sb("ident_b", (128, 128), BF16)[:])
        self.ones_f = T(sb("ones_f", (128, 128), F32)[:])
        self.ones_b = T(sb("ones_b", (128, 128), BF16)[:])
        self.maskNb = T(sb("maskNb", (128, 128), F32)[:])
        self.maskAb = T(sb("maskAb", (128, 128), F32)[:])
        self.ucs = T(sb("ucs", (128, 128), F32)[:])
        self.bones = T(sb("bones", (128, 128), F32)[:])
        self.ci = T(sb("ci", (128, 2, 128), F32)[:])
        self.wconv = T(sb("wconv", (128, 48, 4), F32)[:])
        self.wonorm = T(sb("wonorm", (128, 1), F32)[:])
        self.nA = T(sb("nA", (128, 16), F32)[:])
        self.dtb = T(sb("dtb", (128, 16), F32)[:])
        self.eps_ln = T(sb("eps_ln", (128, 1), F32)[:])
        for t_, nm in [(self.ident_f, "ident"), (self.maskNb, "maskNb"), (self.maskAb, "maskAb"), (self.ucs, "ucs"),
                       (self.bones, "bones"), (self.ci, "ci"), (self.wconv, "wconv"), (self.wonorm, "wonorm"),
                       (self.dtb, "dtb"), (self.nA, "alog")]:
            C.dma("sp", t_.ap, d[nm], writes=[t_])
        C.cp("dve", self.ident_b.ap, self.ident_f.ap, [self.ident_f], [self.ident_b])
        C.op("dve", lambda e: e.memset(self.ones_f.ap, 1.0), [], [self.ones_f])
        C.op("dve", lambda e: e.memset(self.ones_b.ap, 1.0), [], [self.ones_b])
        C.act(self.nA.ap, self.nA.ap, AF.Exp, [self.nA], [self.nA])
        C.ts("dve", self.nA.ap, self.nA.ap, -1.0, None, ALU.mult, None, [self.nA], [self.nA])

        self.eps_rms = T(sb("eps_rms", (128, 1), F32)[:])
        C.op("dve", lambda e: e.memset(self.eps_rms.ap, RMS_EPS), [], [self.eps_rms])
        self.i16b = T(sb("i16b", (128, 16, 16), F32)[:])
        C.dma("sp", self.i16b.ap, d["i16b"], writes=[self.i16b])
        ps = lambda n, s, dt: es.enter_context(nc.psum_tensor(n, list(s), dt))
        self.bank = [T(ps("pb%d" % i, (128, 512), F32)[:]) for i in range(8)]
        for t_ in self.bank:
            t_.b.excl = True
        self.psbig = Ring(self.bank[0:5])
        self.pstr = Ring(self.bank[5:7])

        nbig = nc.sbuf_bytes_remaining - 256
        nbig = nbig // 64 * 64
        self.nbig = nbig
        self.big = sb("big", (128, nbig), U8)
        G = self.G = Arena(self.big, 0, nbig)
        self.ybT = G.t((128, 16, NOWN), BF16)
        self.baseB = G.p
        self.yaT = G.t((128, 16, NOWN), BF16)
        self.baseA = G.p
        self.topB = (nbig - 128 * 16 * 6 - 128) // 64 * 64
        GS = Arena(self.big, self.topB, nbig)
        self.S = GS.t((128, 16, 128), F32)
        self.Sb = GS.t((128, 16, 128), BF16)
        C.op("dve", lambda e: e.memset(self.S.ap, 0.0), [], [self.S])
        C.op("dve", lambda e: e.memset(self.Sb.ap, 0.0), [], [self.Sb])
        self.Sh = [T(self.S[:, h, :]) for h in range(16)]
        self.Sbh = [T(self.Sb[:, h, :]) for h in range(16)]
        for h in range(16):
            self.Sh[h].b.w = dict(self.S.b.w)
            self.Sbh[h].b.w = dict(self.Sb.b.w)

        if "P" in self.phases:
            self.mixerB(False)
        if "B" in self.phases:
            self.mixerB(True)
        elif "in_ybT" in d:
            self.load_dbg_bf(self.ybT, d["in_ybT"])
        if "A" in self.phases:
            self.mixerA()
        elif "in_yaT" in d:
            self.load_dbg_bf(self.yaT, d["in_yaT"])
        if "M" in self.phases:
            self.merge_tail()
        C.finish()
        es.close()

    def rsqrt(self, out, in_, eps, R, W, scale=1.0):
        C = self.C
        C.ts("dve", out, in_, scale, eps, ALU.mult, ALU.add, R, [W])
        C.act(out, out, AF.Sqrt, [W], [W])
        C.op("dve", lambda e: e.reciprocal(out=out, in_=out), [W], [W])

    def load_dbg_bf(self, t, src):
        self.C.dma("pool", t.ap, src.rearrange("(k p) n -> p k n", p=128), writes=[t])

    def dump_fm(self, t, name, ncols, at=None):
        C = self.C
        C.barrier()
        if at is None:
            at = self.nbig - 8 * 1024 - 64
        A = Arena(self.big, at, self.nbig)
        tmp = A.t((128, ncols), F32)
        dst = self.d[name].rearrange("(k p) n -> p k n", p=128)
        for k in range(16):
            C.cp("dve", tmp.ap, t[:, k, 0:ncols], [t], [tmp])
            C.dma("sp", dst[:, k, :], tmp.ap, reads=[tmp])

    def wload(self, ring, src2d, ncols, k=16):
        slot = ring.next()
        view = slot.ap[:, 0:k * ncols].rearrange("p (k n) -> p k n", k=k)
        self.C.dma("pool", view, src2d.rearrange("(k p) n -> p k n", p=128), writes=[slot])
        return T(view, slot.b)

    def mixerA(self):
        C, d = self.C, self.d
        C.barrier()
        A = Arena(self.big, self.baseA, self.nbig)
        xT = A.t((128, 16, NCO), BF16)
        C.dma("pool", xT.ap, d["xTo"].rearrange("(k p) n -> p k n", p=128), writes=[xT])
        wring = Ring([T(A.ap((128, 4096), BF16)) for _ in range(3)])
        va_p0 = A.p
        va = A.t((128, 9, 2048), BF16)
        va32 = A.t((16, 2048), F32)
        lng = A.t((128, 2048), F32)
        lnb = A.t((128, 2048), F32)
        wsm = A.t((128, 16, 128), BF16)
        bsr = A.t((1, 2048), F32)
        bs0 = A.t((1, 16, 16), F32)
        w00 = A.t((16, 16), F32)
        w00I = A.t((16, 16, 16), BF16)
        stats = A.t((128, 4, 6), F32)
        mv = A.t((128, 2), F32)
        rstd = A.t((128, 1), F32)
        C.dma("sp", lng.ap, d["lnv_g"], writes=[lng])
        C.dma("sp", lnb.ap, d["lnv_b"], writes=[lnb])
        C.dma("sp", bsr.ap, d["bs_row"], writes=[bsr])
        C.dma("sp", bs0.ap, d["bs0"], writes=[bs0])
        C.dma("sp", w00.ap, d["w00"], writes=[w00])
        A2 = Arena(self.big, va_p0, self.nbig)
        wtmp = T(A2.ap((128, 16, 128), F32), va.b)
        C.dma("sp", wtmp.ap, d["w_sT"], writes=[wtmp])
        tri = T(A2.ap((128, 128), F32), va.b)
        C.dma("sp", tri.ap, d["tri"], writes=[tri])
        C.tt("dve", wsm.ap, wtmp.ap, bc(tri.ap, [128, 16, 128], 1), ALU.mult, [wtmp, tri], [wsm])
        C.tt("dve", w00I.ap, bc(self.ident_f[0:16, 0:16], [16, 16, 16], 1), bc(w00.ap, [16, 16, 16], 2), ALU.mult,
             [self.ident_f, w00], [w00I])
        yaT = self.yaT
        blocks = [(3, 350), (350, 697), (697, 1043)]
        for g in range(8):
            wt = self.wload(wring, d["w_in"][:, OU + g * 256: OU + (g + 1) * 256], 256)
            for j in range(2):
                for (c0, c1) in blocks:
                    ps = self.psbig.next()
                    n = c1 - c0
                    for k in range(16):
                        C.mm(ps[:, 0:n], wt[:, k, j * 128:(j + 1) * 128], xT[:, k, c0:c1], [wt, xT], ps, k == 0, k == 15)
                    C.act(yaT[:, 2 * g + j, c0 - 3:c1 - 3], ps[:, 0:n], AF.Gelu, [ps], [yaT])
        for g in range(8):
            wt = self.wload(wring, d["w_in"][:, OVA + g * 256: OVA + (g + 1) * 256], 256)
            for t in range(9):
                nt = 128 if t < 8 else 16
                c0 = 3 + 128 * t
                ps = self.psbig.next()
                for k in range(16):
                    C.mm(ps[0:nt, 0:256], xT[:, k, c0:c0 + nt], wt[:, k, :], [wt, xT], ps, k == 0, k == 15)
                if t < 8:
                    C.act(va[:, t, g * 256:(g + 1) * 256], ps[:, 0:256], AF.Gelu, [ps], [va])
                else:
                    C.act(va32[:, g * 256:(g + 1) * 256], ps[0:16, 0:256], AF.Gelu, [ps], [va32])
        tmpn = T(A.ap((128, 512), F32))
        for t in range(9):
            nt = 128 if t < 8 else 16
            src = va[0:nt, t, :] if t < 8 else va32.ap
            srcT = va if t < 8 else va32
            for q in range(4):
                C.op("dve", lambda e: e.bn_stats(out=stats[0:nt, q, :], in_=src[:, q * 512:(q + 1) * 512]), [srcT], [stats])
            C.op("dve", lambda e: e.bn_aggr(out=mv[0:nt, :], in_=stats[0:nt, :, :]), [stats], [mv])
            self.rsqrt(rstd[0:nt, :], mv[0:nt, 1:2], LN_EPS, [mv], rstd)
            for q in range(4):
                qs = slice(q * 512, (q + 1) * 512)
                C.ts("dve", tmpn[0:nt, :], src[:, qs], mv[0:nt, 0:1], rstd[0:nt, 0:1], ALU.subtract, ALU.mult,
                     [srcT, mv, rstd], [tmpn])
                C.tt("dve", tmpn[0:nt, :], tmpn[0:nt, :], lng[0:nt, qs], ALU.mult, [tmpn, lng], [tmpn])
                if t < 8:
                    C.tt("dve", va[:, t, qs], tmpn.ap, lnb[:, qs], ALU.add, [tmpn, lnb], [va])
                else:
                    C.tt("dve", va32[:, qs], tmpn[0:16, :], lnb[0:16, qs], ALU.add, [tmpn, lnb], [va32])
            if t == 8:
                C.dma("sp", d["vs"], va32.ap, reads=[va32])
                C.cp("dve", va[0:16, 8, :], va32.ap, [va32], [va])
        for t in range(9):
            for g4 in range(4):
                ps = self.psbig.next()
                nt = 128 if t < 8 else 16
                for gi in range(4):
                    g = g4 * 4 + gi
                    if t < 8:
                        C.mm(ps[:, gi * 128:(gi + 1) * 128], va[:, t, g * 128:(g + 1) * 128], wsm[:, g, :], [va, wsm], ps, True, False)
                        C.mm(ps[:, gi * 128:(gi + 1) * 128], self.ones_f[0:1, :], bsr[0:1, g * 128:(g + 1) * 128],
                             [self.ones_f, bsr], ps, False, True)
                    else:
                        C.mm(ps[:, gi * 128:gi * 128 + 16], va[0:16, 8, g * 128:(g + 1) * 128], w00I[:, g, :], [va, w00I], ps, True, False)
                        C.mm(ps[:, gi * 128:gi * 128 + 16], self.ones_f[0:1, :], bs0[0:1, g, :], [self.ones_f, bs0], ps, False, True)
                dst = yaT[:, g4 * 4:g4 * 4 + 4, t * 128:t * 128 + nt]
                src = ps.ap.rearrange("p (g n) -> p g n", g=4)[:, :, 0:nt]
                C.tt("dve", dst, dst, src, ALU.mult, [yaT, ps], [yaT])
        if "yaT" in d:
            self.dump_fm(self.yaT, "yaT", NOWN)
        C.barrier()

    def layernorm_rows(self, A, x, nt, g, b, dst_fn, stats, mv, rstd, tmp_ring):
        C = self.C
        for q in range(4):
            C.op("dve", lambda e: e.bn_stats(out=stats[0:nt, q, :], in_=x.ap[0:nt, q * 512:(q + 1) * 512]), [x], [stats])
        C.op("dve", lambda e: e.bn_aggr(out=mv[0:nt, :], in_=stats[0:nt, :, :]), [stats], [mv])
        self.rsqrt(rstd[0:nt, :], mv[0:nt, 1:2], LN_EPS, [mv], rstd)
        for q in range(4):
            qs = slice(q * 512, (q + 1) * 512)
            tmp = tmp_ring.next()
            C.ts("dve", tmp[0:nt, :], x.ap[0:nt, qs], mv[0:nt, 0:1], rstd[0:nt, 0:1], ALU.subtract, ALU.mult,
                 [x, mv, rstd], [tmp])
            C.tt("dve", tmp[0:nt, :], tmp[0:nt, :], g[0:nt, qs], ALU.mult, [tmp, g], [tmp])
            dst_fn(q, tmp)

    def merge_tail(self):
        C, d = self.C, self.d
        C.barrier()
        mT_lo = (self.nbig - 16 * NOWN * 2) // 64 * 64
        mT = T(Arena(self.big, mT_lo, self.nbig).ap((128, 16, NOWN), BF16))
        A = Arena(self.big, self.baseA, mT_lo)
        xT = A.t((128, 16, NCO), BF16)
        C.dma("pool", xT.ap, d["xTo"].rearrange("(k p) n -> p k n", p=128), writes=[xT])
        wring = Ring([T(A.ap((128, 4096), BF16)) for _ in range(4)])
        sgr = A.ring(3, (128, 512), F32)
        t1r = A.ring(2, (128, 512), F32)
        yaT, ybT = self.yaT, self.ybT
        blocks = [(0, 347), (347, 694), (694, NOWN)]
        for g in range(8):
            wa = self.wload(wring, d["w_pa"][:, g * 256:(g + 1) * 256], 256)
            wga = self.wload(wring, d["w_in"][:, OGA + g * 256:OGA + (g + 1) * 256], 256)
            wb = self.wload(wring, d["w_pb"][:, g * 256:(g + 1) * 256], 256)
            wgb = self.wload(wring, d["w_in"][:, OGB + g * 256:OGB + (g + 1) * 256], 256)
            for j in range(2):
                f = 2 * g + j
                js = slice(j * 128, (j + 1) * 128)
                for (c0, c1) in blocks:
                    n = c1 - c0
                    parts = []
                    for (wy, yT, wg_) in ((wa, yaT, wga), (wb, ybT, wgb)):
                        psg = self.psbig.next()
                        for k in range(16):
                            C.mm(psg[:, 0:n], wg_[:, k, js], xT[:, k, 3 + c0:3 + c1], [wg_, xT], psg, k == 0, k == 15)
                        sg = sgr.next()
                        C.act(sg[:, 0:n], psg[:, 0:n], AF.Sigmoid, [psg], [sg])
                        psy = self.psbig.next()
                        for k in range(16):
                            C.mm(psy[:, 0:n], wy[:, k, js], yT[:, k, c0:c1], [wy, yT], psy, k == 0, k == 15)
                        C.tt("dve", sg[:, 0:n], psy[:, 0:n], sg[:, 0:n], ALU.mult, [psy, sg], [sg])
                        parts.append(sg)
                    C.tt("dve", mT[:, f, c0:c1], parts[0][:, 0:n], parts[1][:, 0:n], ALU.add, parts, [mT])
        if "mT" in d:
            self.dump_fm(mT, "mT", NOWN, at=self.baseA)
        C.barrier()
        A = Arena(self.big, 0, mT_lo)
        x1 = A.t((128, 9, 2048), F32)
        base2 = A.p
        wring = Ring([T(A.ap((128, 4096), BF16)) for _ in range(4)])
        xtr = A.ring(3, (128, 512), F32)
        lng = A.t((128, 2048), F32)
        lnb = A.t((128, 2048), F32)
        x1b = A.ring(2, (128, 2048), BF16)
        tmpr = A.ring(2, (128, 512), F32)
        stats = A.t((128, 4, 6), F32)
        mv = A.t((128, 2), F32)
        rstd = A.t((128, 1), F32)
        C.dma("sp", lng.ap, d["ln1_g"], writes=[lng])
        C.dma("sp", lnb.ap, d["ln1_b"], writes=[lnb])
        for n in range(4):
            ns = slice(n * 512, (n + 1) * 512)
            wk = [self.wload(wring, d["w_o"][kh * 1024:(kh + 1) * 1024, ns], 512, k=8) for kh in range(2)]
            for t in range(9):
                nt = 128 if t < 8 else 16
                xt = xtr.next()
                C.dma("sp", xt[0:nt, :], d["xtok"][t * 128:t * 128 + nt, ns], writes=[xt])
                ps = self.psbig.next()
                for k in range(16):
                    C.mm(ps[0:nt, :], mT[:, k, t * 128:t * 128 + nt], wk[k // 8][:, k % 8, :], [mT, wk[k // 8]], ps,
                         k == 0, k == 15)
                C.stt("dve", x1[0:nt, t, ns], xt[0:nt, :], ALPHA, ps[0:nt, :], ALU.mult, ALU.add, [xt, ps], [x1])
        x1T = T(mT.ap, mT.b)
        for t in range(9):
            nt = 128 if t < 8 else 16
            xb = x1b.next()
            x1row = T(x1[:, t, :], x1.b)

            def put(q, tmp, t=t, nt=nt, xb=xb):
                qs = slice(q * 512, (q + 1) * 512)
                C.tt("dve", x1[0:nt, t, qs], tmp[0:nt, :], lnb[0:nt, qs], ALU.add, [tmp, lnb], [x1])
                C.cp("act", xb[0:nt, qs], x1[0:nt, t, qs], [x1], [xb])
            self.layernorm_rows(A, x1row, nt, lng, lnb, put, stats, mv, rstd, tmpr)
            for f8 in range(2):
                pb = self.pstr.next()
                pv = pb.ap.bitcast(BF16).rearrange("p (f n) -> p f n", f=8)
                for i in range(8):
                    f = f8 * 8 + i
                    C.tr(pv[:, i, 0:nt], xb[0:nt, f * 128:(f + 1) * 128], self.ident_b[0:nt, 0:nt], [xb, self.ident_b], pb)
                C.cp("act", x1T[:, f8 * 8:f8 * 8 + 8, t * 128:t * 128 + nt], pv[:, :, 0:nt], [pb], [x1T])
        if "x1" in d:
            for t in range(9):
                nt = 128 if t < 8 else 16
                C.dma("sp", d["x1"][t * 128:t * 128 + nt, :], x1[0:nt, t, :], reads=[x1])
        C.barrier()
        A = Arena(self.big, base2, mT_lo)
        wring = Ring([T(A.ap((128, 4096), BF16)) for _ in range(4)])
        hT = A.t((128, 8, NOWN), BF16)
        rtmp = A.ring(3, (128, 512), F32)
        lng = A.t((128, 2048), F32)
        lnb = A.t((128, 2048), F32)
        outr = A.ring(3, (128, 512), F32)
        tmpr = A.ring(2, (128, 512), F32)
        stats = A.t((128, 4, 6), F32)
        mv = A.t((128, 2), F32)
        rstd = A.t((128, 1), F32)
        C.dma("sp", lng.ap, d["ln2_g"], writes=[lng])
        C.dma("sp", lnb.ap, d["ln2_b"], writes=[lnb])
        for e8 in range(8):
            for q in range(4):
                wt = self.wload(wring, d["w_up"][:, e8 * 1024 + q * 256:e8 * 1024 + (q + 1) * 256], 256)
                for j in range(2):
                    for (c0, c1) in blocks:
                        n = c1 - c0
                        ps = self.psbig.next()
                        for k in range(16):
                            C.mm(ps[:, 0:n], wt[:, k, j * 128:(j + 1) * 128], x1T[:, k, c0:c1], [wt, x1T], ps, k == 0, k == 15)
                        r = rtmp.next()
                        C.act(r[:, 0:n], ps[:, 0:n], AF.Relu, [ps], [r])
                        C.tt("dve", hT[:, q * 2 + j, c0:c1], r[:, 0:n], r[:, 0:n], ALU.mult, [r], [hT])
            for n in range(4):
                ns = slice(n * 512, (n + 1) * 512)
                wd = self.wload(wring, d["w_dn"][e8 * 1024:(e8 + 1) * 1024, ns], 512, k=8)
                for t in range(9):
                    nt = 128 if t < 8 else 16
                    ps = self.psbig.next()
                    for k in range(8):
                        C.mm(ps[0:nt, :], hT[:, k, t * 128:t * 128 + nt], wd[:, k, :], [hT, wd], ps, k == 0, k == 7)
                    if e8 == 0:
                        C.stt("dve", x1[0:nt, t, ns], x1[0:nt, t, ns], ALPHA, ps[0:nt, :], ALU.mult, ALU.add, [x1, ps], [x1])
                    else:
                        C.tt("dve", x1[0:nt, t, ns], x1[0:nt, t, ns], ps[0:nt, :], ALU.add, [x1, ps], [x1])
        for t in range(9):
            nt = 128 if t < 8 else 16
            x1row = T(x1[:, t, :], x1.b)

            def put2(q, tmp, t=t, nt=nt):
                qs = slice(q * 512, (q + 1) * 512)
                o = outr.next()
                C.tt("dve", o[0:nt, :], tmp[0:nt, :], lnb[0:nt, qs], ALU.add, [tmp, lnb], [o])
                C.dma("sp", d["y_out"][t * 128:t * 128 + nt, qs], o[0:nt, :], reads=[o])
            self.layernorm_rows(A, x1row, nt, lng, lnb, put2, stats, mv, rstd, tmpr)
        C.barrier()

    def rsqrt_act(self, out, in_, R, W, scale=1.0):
        C = self.C
        n = out.shape[0]
        C.act(out, in_, AF.Ln, list(R) + [self.eps_rms], [W], bias=self.eps_rms[0:n, 0:1], scale=scale)
        C.act(out, out, AF.Exp, [W], [W], scale=-0.5)

    def mixerB(self, own):
        C, d = self.C, self.d
        C.barrier()
        NC_ = NCO if own else NCP
        NTK = NOWN if own else 1024
        A = Arena(self.big, self.baseB, self.topB)
        xT = A.t((128, 16, NC_), BF16)
        C.dma("pool", xT.ap, d["xTo" if own else "xTp"].rearrange("(k p) n -> p k n", p=128), writes=[xT])
        wring = Ring([T(A.ap((128, 4096), BF16)) for _ in range(3)])
        psbig = Ring(self.bank[0:2])
        psc = Ring(self.bank[2:8])
        blocks = [(0, 348), (348, 696), (696, NC_)] if own else [(0, 343), (343, 686), (686, NC_)]
        tblocks = [(0, 347), (347, 694), (694, NTK)] if own else [(0, 512), (512, 1024)]
        ident_f, ident_b, ones_f, ones_b = self.ident_f, self.ident_b, self.ones_f, self.ones_b
        wba = A.t((128, 16, 32), BF16)
        C.dma("pool", wba.ap, d["w_in"][:, OBETA:OBETA + 32].rearrange("(k p) n -> p k n", p=128), writes=[wba])
        beta = A.t((128, 8, 16), F32)
        g = A.t((128, 8, 16), F32)
        gc = A.t((128, 8, 16), F32)
        eg = A.t((128, 8, 16), F32)
        ekd = A.t((128, 8, 16), F32)
        bg = A.t((128, 8, 16), F32)
        glb = A.t((128, 2, 8, 16), F32)
        for t in range(8):
            ps = psbig.next()
            for k in range(16):
                C.mm(ps[:, 0:32], xT[:, k, 3 + 128 * t:3 + 128 * t + 128], wba[:, k, :], [xT, wba], ps, k == 0, k == 15)
            C.act(beta[:, t, :], ps[:, 0:16], AF.Sigmoid, [ps], [beta])
            C.tt("dve", g[:, t, :], ps[:, 16:32], self.dtb.ap, ALU.add, [ps, self.dtb], [g])
        C.act(g.ap, g.ap, AF.Exp, [g], [g])
        C.act(g.ap, g.ap, AF.Ln, [g], [g], bias=1.0)
        C.tt("dve", g.ap, g.ap, bc(self.nA.ap, [128, 8, 16], 1), ALU.mult, [g, self.nA], [g])
        g2 = g.ap.rearrange("p t h -> p (t h)")
        ps = psc.next()
        C.mm(ps[:, 0:128], self.ucs.ap, g2, [self.ucs, g], ps)
        C.mm(ps[:, 128:256], self.bones.ap, g2, [self.bones, g], ps)
        C.mm(ps[:, 256:384], self.ci[:, 0, :], g2, [self.ci, g], ps)
        C.mm(ps[:, 384:512], self.ci[:, 1, :], g2, [self.ci, g], ps)
        C.cp("act", gc.ap.rearrange("p t h -> p (t h)"), ps[:, 0:128], [ps], [gc])
        C.tt("dve", ekd.ap.rearrange("p t h -> p (t h)"), ps[:, 128:256], gc.ap.rearrange("p t h -> p (t h)"), ALU.subtract,
             [ps, gc], [ekd])
        C.act(ekd.ap, ekd.ap, AF.Exp, [ekd], [ekd])
        C.act(eg.ap, gc.ap, AF.Exp, [gc], [eg])
        C.act(glb.ap.rearrange("p c t h -> p (c t h)"), ps[:, 256:512], AF.Exp, [ps], [glb])
        C.tt("dve", bg.ap, beta.ap, eg.ap, ALU.mult, [beta, eg], [bg])
        if own:
            betas = A.t((16, 16), F32)
            gs = A.t((16, 16), F32)
            egs = A.t((16, 16), F32)
            dgs = A.t((16, 16, 16), F32)
            glbs = A.t((128, 16, 16), F32)
            ps = psbig.next()
            for k in range(16):
                C.mm(ps[0:16, 0:32], xT[:, k, 1027:1043], wba[:, k, :], [xT, wba], ps, k == 0, k == 15)
            C.act(betas.ap, ps[0:16, 0:16], AF.Sigmoid, [ps], [betas])
            C.tt("dve", gs.ap, ps[0:16, 16:32], self.dtb[0:16, :], ALU.add, [ps, self.dtb], [gs])
            C.act(gs.ap, gs.ap, AF.Exp, [gs], [gs])
            C.act(gs.ap, gs.ap, AF.Ln, [gs], [gs], bias=1.0)
            C.tt("dve", gs.ap, gs.ap, self.nA[0:16, :], ALU.mult, [gs, self.nA], [gs])
            C.act(egs.ap, gs.ap, AF.Exp, [gs], [egs])
            C.tt("dve", dgs.ap, bc(ident_f[0:16, 0:16], [16, 16, 16], 2), bc(gs.ap, [16, 16, 16], 1), ALU.mult,
                 [ident_f, gs], [dgs])
            ps = psc.next()
            C.mm(ps[:, 0:256], ones_f[0:16, :], dgs.ap.rearrange("p s h -> p (s h)"), [ones_f, dgs], ps)
            C.act(glbs.ap.rearrange("p s h -> p (s h)"), ps[:, 0:256], AF.Exp, [ps], [glbs])
        ngc = A.t((128, 8, 16), F32)
        C.ts("dve", ngc.ap, gc.ap, -1.0, None, ALU.mult, None, [gc], [ngc])
        rawr = A.ring(2, (128, NC_), F32)
        accr = A.ring(2, (128, NTK), F32)
        sqr = A.ring(2, (128, NTK), BF16)
        rsr = A.ring(2, (128, 512), F32)
        knT = A.t((128, 2, NTK), BF16)
        vT = A.t((128, 2, NTK), BF16)
        if own:
            qnT = A.t((128, 2, NTK), BF16)
            szT = A.t((128, 2, NTK), BF16)
            sctr = A.ring(2, (128, 3, 16), F32)
            ncsr = A.ring(2, (128, 3, 16), F32)
        NSLOT = int(os.environ.get("KB_NSLOT", "3"))

        class Slot:
            pass
        slots = []
        for i in range(NSLOT):
            sl = Slot()
            for nm in ["dg"] + (["EA"] if own else []):
                setattr(sl, nm, A.t((128, 2, 128), F32))
            sl.EN = sl.dg
            sl.bank = Ring([self.bank[2 + 2 * i], self.bank[3 + 2 * i]])
            sl.AB = [A.t((128, 4, 128), F32) for _ in range(2)]
            sl.P = [A.t((128, 2, 128), F32) for _ in range(2)]
            sl.w = sl.dg
            sl.N0 = sl.P[1]
            if own:
                sl.o1 = sl.EA
            for nm in ["kbg", "kd", "vb", "R", "kcd", "vn"] + (["attnT", "on"] if own else []):
                setattr(sl, nm, A.t((128, 2, 128), BF16))
            C.op("dve", lambda e: e.memset(sl.vn.ap, 0.0), [], [sl.vn])
            sl.ss = A.t((128, 2), F32)
            slots.append(sl)
        junk = A.t((128, 128), F32)
        if own:
            Ss = A.t((128, 4, 128), F32)
            Sn = A.t((128, 4, 128), F32)
            kmr = A.ring(2, (128, 16, 16), F32)
            vmr = A.ring(3, (16, 128), F32)
            tokr = A.ring(8, (16, 128), F32)
            smr = A.ring(4, (16, 1), F32)
            onr = A.ring(2, (128, 128), BF16)
        print("mixerB arena used", A.p - A.lo, "free", A.hi - A.p)
        v3 = lambda ap, a=2: ap.rearrange("p (a b) -> p a b", a=a)

        def run_rr(gens, width):
            active = []
            it = iter(gens)
            while True:
                while len(active) < width:
                    g_ = next(it, None)
                    if g_ is None:
                        break
                    active.append(g_)
                if not active:
                    break
                for g_ in list(active):
                    try:
                        next(g_)
                    except StopIteration:
                        active.remove(g_)

        def qkv_chunk(kind, hh, wt, h0):
            cidx = {"q": 0, "k": 16, "v": 32}[kind] + h0 + hh
            raw = rawr.next()
            for (c0, c1) in blocks:
                n = c1 - c0
                ps = psbig.next()
                for k in range(16):
                    C.mm(ps[:, 0:n], wt[:, k, hh * 128:(hh + 1) * 128], xT[:, k, c0:c1], [wt, xT], ps, k == 0, k == 15)
                C.cp("act", raw[:, c0:c1], ps[:, 0:n], [ps], [raw])
                yield
            acc = accr.next()
            wc = self.wconv
            C.act(acc[:, 0:1024], raw[:, 0:1024], AF.Copy, [raw, wc], [acc], scale=wc[:, cidx, 0:1])
            for j in range(1, 4):
                C.stt("dve", acc[:, 0:1024], raw[:, j:j + 1024], wc[:, cidx, j:j + 1], acc[:, 0:1024], ALU.mult, ALU.add,
                      [raw, wc, acc], [acc])
            if own:
                sct = sctr.next()
                C.dma("sp", sct.ap, d["scT"][:, cidx], writes=[sct])
                C.ts("dve", acc[:, 1024:1040], raw[:, 1027:1043], wc[:, cidx, 3:4], None, ALU.mult, None, [raw, wc], [acc])
                for j in range(3):
                    C.stt("dve", acc[:, 1024:1040], sct[:, j, :], wc[:, cidx, j:j + 1], acc[:, 1024:1040], ALU.mult, ALU.add,
                          [sct, wc, acc], [acc])
                ncs = ncsr.next()
                C.cp("dve", ncs[:, 0:2, :], sct[:, 1:3, :], [sct], [ncs])
                C.cp("dve", ncs[:, 2, :], raw[:, 1027:1043], [raw], [ncs])
                C.dma("sp", d["ncs"][:, cidx], ncs.ap, reads=[ncs])
                C.dma("sp", d["ncp"][:, cidx, :], raw[:, 1024:1027], reads=[raw])
            yield
            if kind == "v":
                C.act(vT[:, hh, :], acc.ap, AF.Silu, [acc], [vT])
                return
            C.act(acc.ap, acc.ap, AF.Silu, [acc], [acc])
            sq = sqr.next()
            C.act(sq.ap, acc.ap, AF.Square, [acc], [sq])
            yield
            dstT = knT if kind == "k" else qnT
            for (c0, c1) in tblocks:
                n = c1 - c0
                ps = psbig.next()
                C.mm(ps[:, 0:n], ones_b.ap, sq[:, c0:c1], [ones_b, sq], ps)
                yield
                rs = rsr.next()
                self.rsqrt_act(rs[:, 0:n], ps[:, 0:n], [ps], rs)
                yield
                if kind == "q":
                    C.stt("dve", dstT[:, hh, c0:c1], acc[:, c0:c1], 128.0 ** -0.5, rs[:, 0:n], ALU.mult, ALU.mult, [acc, rs], [dstT])
                else:
                    C.tt("dve", dstT[:, hh, c0:c1], acc[:, c0:c1], rs[:, 0:n], ALU.mult, [acc, rs], [dstT])

        def z_chunk(hh, wz):
            for (c0, c1) in tblocks:
                n = c1 - c0
                ps = psbig.next()
                for k in range(16):
                    C.mm(ps[:, 0:n], wz[:, k, hh * 128:(hh + 1) * 128], xT[:, k, 3 + c0:3 + c1], [wz, xT], ps, k == 0, k == 15)
                C.act(szT[:, hh, c0:c1], ps[:, 0:n], AF.Silu, [ps], [szT])
                yield

        TS = int(os.environ.get("KB_TS", "9"))
        B3 = [128, 2, 128]

        def tile_scan(t, h0, sl):
            cols = slice(128 * t, 128 * t + 128)
            hs = slice(h0, h0 + 2)
            gch = gc[:, t, hs]
            betab = bc(beta[:, t, hs], B3, 2)
            C.tt("dve", sl.dg.ap, bc(ident_f.ap, B3, 1), bc(gch, B3, 2), ALU.mult, [ident_f, gc], [sl.dg])
            yield
            bk = sl.bank.next()
            C.mm(bk[:, 0:256], ones_f.ap, sl.dg.ap.rearrange("p a b -> p (a b)"), [ones_f, sl.dg], bk)
            yield
            C.tt("dve", sl.EN.ap, v3(bk[:, 0:256]), bc(self.maskNb.ap, B3, 1), ALU.add, [bk, self.maskNb], [sl.EN])
            if own:
                C.tt("dve", sl.EA.ap, v3(bk[:, 0:256]), bc(self.maskAb.ap, B3, 1), ALU.add, [bk, self.maskAb], [sl.EA])
            for hh in range(2):
                C.act(sl.EN[:, hh, :], sl.EN[:, hh, :], AF.Exp, [sl.EN, gc], [sl.EN], scale=-1.0, bias=gc[:, t, h0 + hh:h0 + hh + 1])
                if own:
                    C.act(sl.EA[:, hh, :], sl.EA[:, hh, :], AF.Exp, [sl.EA, ngc], [sl.EA], bias=ngc[:, t, h0 + hh:h0 + hh + 1])
            C.tt("dve", sl.EN.ap, sl.EN.ap, betab, ALU.mult, [sl.EN, beta], [sl.EN])
            yield
            b1 = sl.bank.next()
            for hh in range(2):
                kt = knT[:, hh, cols]
                C.mm(b1[:, hh * 128:(hh + 1) * 128], kt, kt, [knT], b1)
            if own:
                for hh in range(2):
                    C.mm(b1[:, 256 + hh * 128:256 + (hh + 1) * 128], knT[:, hh, cols], qnT[:, hh, cols], [knT, qnT], b1)
            b2 = sl.bank.next()
            b2b = b2.ap.bitcast(BF16)
            for hh in range(2):
                C.tr(b2b[:, hh * 128:(hh + 1) * 128], knT[:, hh, cols], ident_b.ap, [knT, ident_b], b2)
            for hh in range(2):
                C.tr(b2b[:, 256 + hh * 128:256 + (hh + 1) * 128], vT[:, hh, cols], ident_b.ap, [vT, ident_b], b2)
            yield
            C.tt("dve", sl.N0.ap, v3(b1[:, 0:256]), sl.EN.ap, ALU.mult, [b1, sl.EN], [sl.N0])
            if own:
                C.tt("dve", sl.attnT.ap, v3(b1[:, 256:512]), sl.EA.ap, ALU.mult, [b1, sl.EA], [sl.attnT])
            C.tt("dve", sl.kbg.ap, v3(b2b[:, 0:256]), bc(bg[:, t, hs], B3, 2), ALU.mult, [b2, bg], [sl.kbg])
            C.tt("dve", sl.kd.ap, v3(b2b[:, 0:256]), bc(ekd[:, t, hs], B3, 2), ALU.mult, [b2, ekd], [sl.kd])
            C.tt("dve", sl.vb.ap, v3(b2b[:, 256:512]), betab, ALU.mult, [b2, beta], [sl.vb])
            yield
            bI = sl.bank.next()
            for hh in range(2):
                C.mm(bI[:, hh * 128:(hh + 1) * 128], sl.N0[:, hh, :], ident_f.ap, [sl.N0, ident_f], bI)
            yield
            AB = sl.AB[0]
            C.cp("act", AB[:, 0:2, :], v3(bI[:, 0:256]), [bI], [AB])
            C.cp("act", AB[:, 2:4, :], sl.N0.ap, [sl.N0], [AB])
            P = sl.P[0]
            C.tt("dve", P.ap, bc(ident_f.ap, B3, 1), v3(bI[:, 0:256]), ALU.subtract, [ident_f, bI], [P])
            yield
            for l in range(1, 7):
                bX = sl.bank.next()
                bY = sl.bank.next() if l >= 2 else None
                for hh in range(2):
                    Bk, Ak = AB[:, hh, :], AB[:, 2 + hh, :]
                    if l <= 4:
                        C.mm(bX[:, hh * 128:(hh + 1) * 128], Ak, Bk, [AB], bX)
                    if l <= 5:
                        C.mm(bX[:, 256 + hh * 128:256 + (hh + 1) * 128], Bk, Ak, [AB], bX)
                    if l >= 2:
                        C.mm(bY[:, hh * 128:(hh + 1) * 128], Ak, P[:, hh, :], [AB, P], bY)
                yield
                if l <= 5:
                    AB2 = sl.AB[l % 2]
                    if l <= 4:
                        C.cp("act", AB2.ap, v3(bX.ap, 4), [bX], [AB2])
                    else:
                        C.cp("act", AB2[:, 2:4, :], v3(bX[:, 256:512]), [bX], [AB2])
                if l >= 2:
                    if l == 6:
                        C.tt("dve", sl.R.ap, P.ap, v3(bY[:, 0:256]), ALU.add, [P, bY], [sl.R])
                    else:
                        P2 = sl.P[(l + 1) % 2]
                        C.tt("dve", P2.ap, P.ap, v3(bY[:, 0:256]), ALU.add, [P, bY], [P2])
                        P = P2
                if l <= 5:
                    AB = AB2
                yield
            bW = sl.bank.next()
            for hh in range(2):
                C.mm(bW[:, hh * 128:(hh + 1) * 128], sl.R[:, hh, :], sl.vb[:, hh, :], [sl.R, sl.vb], bW)
            for hh in range(2):
                C.mm(bW[:, 256 + hh * 128:256 + (hh + 1) * 128], sl.kbg[:, hh, :], sl.R[:, hh, :], [sl.R, sl.kbg], bW)
            yield
            C.cp("act", sl.w.ap, v3(bW[:, 0:256]), [bW], [sl.w])
            C.cp("dve", sl.kcd.ap, v3(bW[:, 256:512]), [bW], [sl.kcd])

        def tile_chain(t, h0, sl):
            cols = slice(128 * t, 128 * t + 128)
            hs = slice(h0, h0 + 2)
            Sl = [self.Sh[h0], self.Sh[h0 + 1]]
            Sbl = [self.Sbh[h0], self.Sbh[h0 + 1]]
            for c in range(2):
                rows = slice(64 * c, 64 * c + 64)
                bS = sl.bank.next()
                for hh in range(2):
                    C.mm(bS[:, hh * 128:(hh + 1) * 128], sl.kcd[:, hh, :], Sbl[hh].ap, [sl.kcd, Sbl[hh]], bS)
                if own:
                    for hh in range(2):
                        C.mm(bS[:, 256 + hh * 128:256 + (hh + 1) * 128], qnT[:, hh, cols], Sbl[hh].ap, [qnT, Sbl[hh]], bS)
                yield
                C.tt("dve", sl.vn[rows, :, :], sl.w[rows, :, :], v3(bS[rows, 0:256]), ALU.subtract, [sl.w, bS], [sl.vn])
                if own:
                    C.tt("dve", sl.o1[rows, :, :], v3(bS[rows, 256:512]), bc(eg[rows, t, hs], [64, 2, 128], 2), ALU.mult,
                         [bS, eg], [sl.o1])
                yield
                bU = sl.bank.next()
                for hh in range(2):
                    C.mm(bU[:, hh * 128:(hh + 1) * 128], sl.kd[rows, hh, :], sl.vn[rows, hh, :], [sl.kd, sl.vn], bU)
                yield
                for hh in range(2):
                    C.stt("dve", Sl[hh].ap, Sl[hh].ap, glb[:, c, t, h0 + hh:h0 + hh + 1], bU[:, hh * 128:(hh + 1) * 128],
                          ALU.mult, ALU.add, [Sl[hh], glb, bU], [Sl[hh]])
                C.cp("act", self.Sb[:, hs, :], self.S[:, hs, :], Sl, Sbl)
                yield

        def tile_post(t, h0, sl):
            cols = slice(128 * t, 128 * t + 128)
            hs = slice(h0, h0 + 2)
            if own:
                bO = sl.bank.next()
                for hh in range(2):
                    C.mm(bO[:, hh * 128:(hh + 1) * 128], sl.attnT[:, hh, :], sl.vn[:, hh, :], [sl.attnT, sl.vn], bO)
                yield
                C.tt("dve", sl.o1.ap, sl.o1.ap, v3(bO[:, 0:256]), ALU.add, [sl.o1, bO], [sl.o1])
                for hh in range(2):
                    C.act(junk.ap, sl.o1[:, hh, :], AF.Square, [sl.o1], [junk, sl.ss], accum_out=sl.ss[:, hh:hh + 1])
                self.rsqrt_act(sl.ss.ap, sl.ss.ap, [sl.ss], sl.ss, scale=1.0 / 128.0)
                C.tt("dve", sl.on.ap, sl.o1.ap, bc(sl.ss.ap, B3, 2), ALU.mult, [sl.o1, sl.ss], [sl.on])
                yield
                bO2 = sl.bank.next()
                bO2b = bO2.ap.bitcast(BF16)
                for hh in range(2):
                    C.tr(bO2b[:, hh * 128:(hh + 1) * 128], sl.on[:, hh, :], ident_b.ap, [sl.on, ident_b], bO2)
                yield
                C.stt("dve", self.ybT[:, hs, cols], v3(bO2b[:, 0:256]), self.wonorm[:, 0:1], szT[:, :, cols], ALU.mult, ALU.mult,
                      [bO2, self.wonorm, szT], [self.ybT])

        def run_group(h0):
            nt = 8
            prep, post = {}, {}
            prep_done = [False] * nt
            slot_busy = [None] * NSLOT
            chain, chain_t, next_prep, fin = None, 0, 0, 0
            while fin < nt:
                while next_prep < nt and slot_busy[next_prep % NSLOT] is None:
                    slot_busy[next_prep % NSLOT] = next_prep
                    prep[next_prep] = tile_scan(next_prep, h0, slots[next_prep % NSLOT])
                    next_prep += 1
                for t in sorted(prep):
                    try:
                        next(prep[t])
                    except StopIteration:
                        del prep[t]
                        prep_done[t] = True
                if chain is None and chain_t < nt and prep_done[chain_t]:
                    chain = tile_chain(chain_t, h0, slots[chain_t % NSLOT])
                if chain is not None:
                    try:
                        next(chain)
                    except StopIteration:
                        chain = None
                        post[chain_t] = tile_post(chain_t, h0, slots[chain_t % NSLOT])
                        chain_t += 1
                for t in sorted(post):
                    try:
                        next(post[t])
                    except StopIteration:
                        del post[t]
                        slot_busy[t % NSLOT] = None
                        fin += 1

        def sample_scan(h0):
            cs = slice(1024, 1040)
            for hh in range(2):
                h = h0 + hh
                bT = psc.next()
                bTb = bT.ap.bitcast(BF16)
                C.tr(bTb[0:16, 0:128], knT[:, hh, cs], ident_b.ap, [knT, ident_b], bT)
                C.tr(bTb[0:16, 128:256], qnT[:, hh, cs], ident_b.ap, [qnT, ident_b], bT)
                C.tr(bTb[0:16, 256:384], vT[:, hh, cs], ident_b.ap, [vT, ident_b], bT)
                ktok, qtok, vtok = tokr.next(), tokr.next(), tokr.next()
                C.cp("act", ktok.ap, bTb[0:16, 0:128], [bT], [ktok])
                C.cp("act", qtok.ap, bTb[0:16, 128:256], [bT], [qtok])
                C.cp("act", vtok.ap, bTb[0:16, 256:384], [bT], [vtok])
                km, qm = kmr.next(), kmr.next()
                C.tt("dve", km.ap, bc(knT[:, hh, cs], [128, 16, 16], 1), self.i16b.ap, ALU.mult, [knT, self.i16b], [km])
                C.tt("dve", qm.ap, bc(qnT[:, hh, cs], [128, 16, 16], 1), self.i16b.ap, ALU.mult, [qnT, self.i16b], [qm])
                qk = smr.next()
                C.tt("dve", junk[0:16, :], qtok.ap, ktok.ap, ALU.mult, [qtok, ktok], [junk])
                C.op("dve", lambda e: e.reduce_sum(out=qk.ap, in_=junk[0:16, :], axis=AX.X), [junk], [qk])
                os_ = tokr.next()
                for s8 in range(4):
                    sl = slice(4 * s8, 4 * s8 + 4)
                    C.dma("sp", Ss.ap, d["ssm"][sl, h, :, :].rearrange("s d e -> d s e"), writes=[Ss])
                    bP = psc.next()
                    for s in range(4):
                        C.mm(bP[0:16, 0:128], km[:, 4 * s8 + s, :], Ss[:, s, :], [km, Ss], bP, s == 0, s == 3)
                    for s in range(4):
                        C.mm(bP[0:16, 128:256], qm[:, 4 * s8 + s, :], Ss[:, s, :], [qm, Ss], bP, s == 0, s == 3)
                    if s8 == 0:
                        kS, qS = tokr.next(), tokr.next()
                        C.cp("act", kS.ap, bP[0:16, 0:128], [bP], [kS])
                        C.cp("act", qS.ap, bP[0:16, 128:256], [bP], [qS])
                        Ss0 = None
                    else:
                        C.tt("dve", kS.ap, kS.ap, bP[0:16, 0:128], ALU.add, [kS, bP], [kS])
                        C.tt("dve", qS.ap, qS.ap, bP[0:16, 128:256], ALU.add, [qS, bP], [qS])
                    if s8 == 0:
                        continue
                vn = tokr.next()
                C.stt("dve", vn.ap, kS.ap, egs[:, h:h + 1], vtok.ap, ALU.mult, ALU.subtract, [kS, egs, vtok], [vn])
                C.ts("dve", vn.ap, vn.ap, betas[:, h:h + 1], -1.0, ALU.mult, ALU.mult, [vn, betas], [vn])
                C.act(os_.ap, qS.ap, AF.Copy, [qS, egs], [os_], scale=egs[:, h:h + 1])
                C.stt("dve", os_.ap, vn.ap, qk[:, 0:1], os_.ap, ALU.mult, ALU.add, [vn, qk, os_], [os_])
                for s8 in range(4):
                    sl = slice(4 * s8, 4 * s8 + 4)
                    C.dma("sp", Ss.ap, d["ssm"][sl, h, :, :].rearrange("s d e -> d s e"), writes=[Ss])
                    for s in range(4):
                        sg = 4 * s8 + s
                        vm = vmr.next()
                        C.ts("dve", vm.ap, vn.ap, ident_f[0:16, sg:sg + 1], None, ALU.mult, None, [vn, ident_f], [vm])
                        bU = psc.next()
                        C.mm(bU[:, 0:128], ktok.ap, vm.ap, [ktok, vm], bU)
                        C.stt("dve", Sn[:, s, :], Ss[:, s, :], glbs[:, sg, h:h + 1], bU[:, 0:128], ALU.mult, ALU.add,
                              [Ss, glbs, bU], [Sn])
                    C.dma("sp", d["ssm_s"][sl, h, :, :].rearrange("s d e -> d s e"), Sn.ap, reads=[Sn])
                ss = smr.next()
                C.act(junk[0:16, :], os_.ap, AF.Square, [os_], [junk, ss], accum_out=ss[:, 0:1])
                self.rsqrt_act(ss.ap, ss.ap, [ss], ss, scale=1.0 / 128.0)
                on = onr.next()
                C.act(on[0:16, :], os_.ap, AF.Copy, [os_, ss], [on], scale=ss[:, 0:1])
                bO = psc.next()
                bOb = bO.ap.bitcast(BF16)
                C.tr(bOb[:, 0:16], on[0:16, :], ident_b[0:16, 0:16], [on, ident_b], bO)
                C.stt("dve", self.ybT[:, h, cs], bOb[:, 0:16], self.wonorm[:, 0:1], szT[:, hh, cs], ALU.mult, ALU.mult,
                      [bO, self.wonorm, szT], [self.ybT])

        stop = os.environ.get("KB_STOP", "all")
        nhg = int(os.environ.get("KB_NHG", "8"))
        WIDTH_IP = int(os.environ.get("KB_WIP", "2"))
        for hg in range(nhg if stop != "pre" else 0):
            h0 = 2 * hg
            gens = []
            wk = self.wload(wring, d["w_in"][:, OK_ + hg * 256:OK_ + (hg + 1) * 256], 256)
            gens += [qkv_chunk("k", hh, wk, h0) for hh in range(2)]
            wv = self.wload(wring, d["w_in"][:, OV + hg * 256:OV + (hg + 1) * 256], 256)
            gens += [qkv_chunk("v", hh, wv, h0) for hh in range(2)]
            run_rr(gens, WIDTH_IP)
            if own:
                gens = []
                wq = self.wload(wring, d["w_in"][:, OQ + hg * 256:OQ + (hg + 1) * 256], 256)
                gens += [qkv_chunk("q", hh, wq, h0) for hh in range(2)]
                wz = self.wload(wring, d["w_in"][:, OZ + hg * 256:OZ + (hg + 1) * 256], 256)
                gens += [z_chunk(hh, wz) for hh in range(2)]
                run_rr(gens, WIDTH_IP)
            if stop in ("scan", "all"):
                run_group(h0)
            if own and stop == "all":
                sample_scan(h0)
        if own:
            allS = [self.Sh[h] for h in range(16)]
            C.dma("sp", d["ssm_p"], self.S.ap, reads=allS)
            if "ybT" in d:
                self.dump_fm(self.ybT, "ybT", NOWN, at=self.baseB)
        C.barrier()


def _consts():
    i = np.arange(128)
    same = (i[:, None] // 64) == (i[None, :] // 64)
    ident = np.eye(128, dtype=np.float32)
    maskNb = np.where(same & (i[None, :] < i[:, None]), 0.0, BIGMASK).astype(np.float32)
    maskAb = np.where(same & (i[None, :] >= i[:, None]), 0.0, -BIGMASK).astype(np.float32)
    ucs = (same & (i[:, None] <= i[None, :])).astype(np.float32)
    bones = same.astype(np.float32)
    ci = np.zeros((128, 2, 128), np.float32)
    ci[0:64, 0, :] = 1.0
    ci[64:128, 1, :] = 1.0
    tri = (i[None, :] >= i[:, None]).astype(np.float32)
    i16b = np.ascontiguousarray(np.broadcast_to(np.eye(16, dtype=np.float32)[None], (128, 16, 16)))
    return dict(ident=ident, maskNb=maskNb, maskAb=maskAb, ucs=ucs, bones=bones, ci=ci, tri=tri, i16b=i16b)


def host_inputs(inp, c):
    b, half = c // 2, c % 2
    f = np.float32
    xp = inp["x_prompt"]
    xs = inp["x_sample"][16 * c:16 * c + 16, 0]
    own = xp[b, half * 1024:(half + 1) * 1024]
    xTo = np.zeros((2048, NCO), f)
    if half:
        xTo[:, 0:3] = xp[b, 1021:1024].T
    xTo[:, 3:1027] = own.T
    xTo[:, 1027:1043] = xs.T
    xTp = np.zeros((2048, NCP), f)
    if half:
        xTp[:, 3:] = xp[b, 0:1024].T
    xtok = np.concatenate([own, xs], 0)
    w_s = inp["w_s"][0]
    m = dict(
        xTo=xTo, xTp=xTp, xtok=np.ascontiguousarray(xtok),
        w_in=inp["w_in"][0], w_pa=inp["w_proj_a"][0], w_pb=inp["w_proj_b"][0], w_o=inp["w_o"][0],
        w_up=inp["w_up"][0], w_dn=inp["w_down"][0],
        w_sT=np.ascontiguousarray(w_s.transpose(2, 0, 1)),
        bs_row=np.ascontiguousarray(inp["b_s"][0].reshape(1, 2048)),
        bs0=np.ascontiguousarray(np.broadcast_to(inp["b_s"][0][:, 0][None, :, None], (1, 16, 16))),
        w00=np.ascontiguousarray(np.broadcast_to(w_s[:, 0, 0][None, :], (16, 16))),
        wconv=np.ascontiguousarray(inp["w_conv"][0].reshape(4, 48, 128).transpose(2, 1, 0)),
        alog=np.ascontiguousarray(np.broadcast_to(inp["a_log"][0][None, :], (128, 16))),
        dtb=np.ascontiguousarray(np.broadcast_to(inp["dt_bias"][0][None, :], (128, 16))),
        wonorm=np.ascontiguousarray(inp["w_onorm"][0].reshape(128, 1)),
        scT=np.ascontiguousarray(inp["state_conv"][0, 16 * c:16 * c + 16].reshape(16, 3, 48, 128).transpose(3, 2, 1, 0)),
        ssm=np.ascontiguousarray(inp["state_ssm"][0, 16 * c:16 * c + 16]),
    )
    for nm, key in [("lnv_g", "ln_v_g"), ("lnv_b", "ln_v_b"), ("ln1_g", "ln1_g"), ("ln1_b", "ln1_b"),
                    ("ln2_g", "ln2_g"), ("ln2_b", "ln2_b")]:
        m[nm] = np.ascontiguousarray(np.broadcast_to(inp[key][0][None, :], (128, 2048)))
    m.update(_consts())
    return {k: np.ascontiguousarray(v, dtype=f) for k, v in m.items()}


_PROG = None


def _program():
    global _PROG
    if _PROG is None:
        _PROG = Builder()
    return _PROG


def kernel(**inputs):
    inp = {k: np.asarray(v) for k, v in inputs.items()}
    B = _program()
    in_maps = []
    for c in range(8):
        m = host_inputs(inp, c)
        in_maps.append({k: v for k, v in m.items() if k in B.d})
    res = run_bass_kernel_spmd(B.nc, in_maps, core_ids=list(range(8)))
    R = res.results
    f = np.float32
    y_prompt = np.zeros((4, 2048, 2048), f)
    y_sample = np.zeros((128, 1, 2048), f)
    ncp = np.zeros((1, 4, 3, 6144), f)
    ssp = np.zeros((1, 4, 16, 128, 128), f)
    vs = np.zeros((1, 128, 1, 2048), f)
    ncs = np.zeros((1, 128, 3, 6144), f)
    sss = np.zeros((1, 128, 16, 128, 128), f)
    for c in range(8):
        b, half = c // 2, c % 2
        r = R[c]
        y_prompt[b, half * 1024:(half + 1) * 1024] = r["y_out"][0:1024]
        y_sample[16 * c:16 * c + 16, 0] = r["y_out"][1024:1040]
        vs[0, 16 * c:16 * c + 16, 0] = r["vs"]
        ncs[0, 16 * c:16 * c + 16] = r["ncs"].transpose(3, 2, 1, 0).reshape(16, 3, 6144)
        sss[0, 16 * c:16 * c + 16] = r["ssm_s"]
        if half:
            ncp[0, b] = r["ncp"].transpose(2, 1, 0).reshape(3, 6144)
            ssp[0, b] = r["ssm_p"].transpose(1, 0, 2)
    return (y_prompt, y_sample, ncp, ssp, vs, ncs, sss)
```

```python
import os
import numpy as np
from contextlib import ExitStack
import concourse.bass as bass
import concourse.mybir as mybir
from concourse.bass_utils import run_bass_kernel_spmd

F32 = mybir.dt.float32
BF16 = mybir.dt.bfloat16
U8 = mybir.dt.uint8
AF = mybir.ActivationFunctionType
ALU = mybir.AluOpType
AX = mybir.AxisListType

ALPHA = 2.0 ** 0.25
LN_EPS = 1e-5
RMS_EPS = 1e-6
NOWN = 1040
NCO = 1043
NCP = 1027
BIGMASK = 30000.0
OU, OVA, OQ, OK_, OV, OZ, OBETA, OA, OGA, OGB = 0, 2048, 4096, 6144, 8192, 10240, 12288, 12304, 12320, 14368


class Buf:
    __slots__ = ("w", "r", "excl")

    def __init__(self):
        self.w = {}
        self.r = {}
        self.excl = False


class T:
    def __init__(self, ap, b=None):
        self.ap = ap
        self.b = b if b is not None else Buf()

    def __getitem__(self, k):
        return self.ap[k]


def _bufs(ts):
    return [t.b if isinstance(t, T) else t for t in ts]


class Ctx:
    NDS = {"sp": 8, "pool": 8}

    def __init__(self, nc, es):
        self.nc = nc
        self.E = {"pe": nc.tensor, "dve": nc.vector, "act": nc.scalar, "pool": nc.gpsimd, "sp": nc.sync}
        self.sems = {}
        self.cnt = {}
        self.seen = {k: {} for k in self.E}
        for k in self.E:
            self.sems[k] = es.enter_context(nc.semaphore("sm_" + k))
            self.cnt[k] = 0
        self.dq = {}
        for q, n in self.NDS.items():
            lst = []
            for i in range(n):
                key = "d_%s%d" % (q, i)
                self.sems[key] = es.enter_context(nc.semaphore(key))
                self.cnt[key] = 0
                lst.append(key)
            self.dq[q] = [lst, 0]
        self.nwait = 0
        self.nins = 0

    def wait(self, eng, key, val):
        if val <= 0:
            return
        if eng == "pe" and key == "pe":
            return
        if self.seen[eng].get(key, 0) >= val:
            return
        self.E[eng].wait_ge(self.sems[key], val)
        self.seen[eng][key] = val
        self.nwait += 1

    def _deps(self, eng, reads, writes):
        deps = {}
        for b in reads:
            for k, v in b.w.items():
                if deps.get(k, 0) < v:
                    deps[k] = v
            if b.excl:
                for k, v in b.r.items():
                    if k != eng and deps.get(k, 0) < v:
                        deps[k] = v
        for b in writes:
            for k, v in b.w.items():
                if deps.get(k, 0) < v:
                    deps[k] = v
            for k, v in b.r.items():
                if deps.get(k, 0) < v:
                    deps[k] = v
        for k, v in deps.items():
            self.wait(eng, k, v)

    def _mark(self, key, val, reads, writes):
        for b in reads:
            if b.r.get(key, 0) < val:
                b.r[key] = val
        for b in writes:
            b.w = {key: val}
            b.r = {}

    def op(self, eng, fn, reads=(), writes=(), inc=True):
        reads = _bufs(reads)
        writes = _bufs(writes)
        self._deps(eng, reads, writes)
        ins = fn(self.E[eng])
        val = self.cnt[eng] + 1
        if inc:
            ins.then_inc(self.sems[eng], 1)
            self.cnt[eng] = val
        self._mark(eng, val, reads, writes)
        self.nins += 1
        return ins

    def dma(self, q, out, in_, reads=(), writes=()):
        reads = _bufs(reads)
        writes = _bufs(writes)
        lst, i = self.dq[q]
        key = lst[i % len(lst)]
        self.dq[q][1] = i + 1
        self.wait(q, key, self.cnt[key])
        self._deps(q, reads, writes)
        ins = self.E[q].dma_start(out=out, in_=in_)
        val = self.cnt[key] + 16
        ins.then_inc(self.sems[key], 16)
        self.cnt[key] = val
        self._mark(key, val, reads, writes)
        self.nins += 1
        return ins

    def barrier(self):
        for e in self.E:
            for k in self.sems:
                self.wait(e, k, self.cnt[k])

    def finish(self):
        for k in self.sems:
            self.wait("sp", k, self.cnt[k])

    def mm(self, out, lhsT, rhs, R, W, start=True, stop=True):
        return self.op("pe", lambda e: e.matmul(out, lhsT=lhsT, rhs=rhs, start=start, stop=stop), R, [W], inc=stop)

    def tr(self, out, in_, ident, R, W):
        return self.op("pe", lambda e: e.transpose(out, in_, ident), R, [W])

    def act(self, out, in_, func, R, W, **kw):
        return self.op("act", lambda e: e.activation(out=out, in_=in_, func=func, **kw), R, W)

    def tt(self, eng, out, in0, in1, op, R, W):
        return self.op(eng, lambda e: e.tensor_tensor(out=out, in0=in0, in1=in1, op=op), R, W)

    def ts(self, eng, out, in0, s1, s2, op0, op1, R, W):
        if s2 is None:
            return self.op(eng, lambda e: e.tensor_scalar(out=out, in0=in0, scalar1=s1, scalar2=None, op0=op0), R, W)
        return self.op(eng, lambda e: e.tensor_scalar(out=out, in0=in0, scalar1=s1, scalar2=s2, op0=op0, op1=op1), R, W)

    def stt(self, eng, out, in0, scalar, in1, op0, op1, R, W):
        return self.op(eng, lambda e: e.scalar_tensor_tensor(out=out, in0=in0, scalar=scalar, in1=in1, op0=op0, op1=op1), R, W)

    def cp(self, eng, out, in_, R, W):
        if eng == "act":
            return self.act(out, in_, AF.Copy, R, W)
        return self.op(eng, lambda e: e.tensor_copy(out=out, in_=in_), R, W)


ESZ = {F32: 4, BF16: 2, U8: 1}


class Arena:
    def __init__(self, big, lo, hi):
        self.big = big
        self.lo = lo
        self.hi = hi
        self.p = lo

    def ap(self, shape, dtype):
        free = 1
        for s in shape[1:]:
            free *= s
        nb = free * ESZ[dtype]
        nba = (nb + 63) // 64 * 64
        assert self.p + nba <= self.hi, ("arena overflow", self.p, nba, self.hi)
        a = self.big[0:shape[0], self.p:self.p + nb].bitcast(dtype)
        if len(shape) > 2:
            names = ["a%d" % i for i in range(len(shape) - 1)]
            a = a.rearrange("p (%s) -> p %s" % (" ".join(names), " ".join(names)),
                            **{n: s for n, s in zip(names, shape[1:])})
        self.p += nba
        return a

    def t(self, shape, dtype):
        return T(self.ap(shape, dtype))

    def ring(self, n, shape, dtype):
        return Ring([self.t(shape, dtype) for _ in range(n)])


class Ring:
    def __init__(self, items):
        self.items = items
        self.i = 0

    def next(self):
        t = self.items[self.i % len(self.items)]
        self.i += 1
        return t


def bc(ap, shape, axis):
    return ap.unsqueeze(axis).to_broadcast(shape)


class Builder:
    def __init__(self, phases=("P", "B", "A", "M", "T"), dbg=()):
        self.phases = phases
        self.dbg = dbg
        self.nc = bass.Bass("TRN2", target_bir_lowering=False)
        self.es = ExitStack()
        self.build()

    def din(self, name, shape):
        return self.nc.dram_tensor(name, list(shape), F32, kind="ExternalInput").ap()

    def dout(self, name, shape):
        return self.nc.dram_tensor(name, list(shape), F32, kind="ExternalOutput").ap()

    def build(self):
        nc, es = self.nc, self.es
        d = self.d = {}
        for name, shape in [
            ("xTo", (2048, NCO)), ("xTp", (2048, NCP)), ("xtok", (NOWN, 2048)),
            ("w_in", (2048, 16416)), ("w_pa", (2048, 2048)), ("w_pb", (2048, 2048)), ("w_o", (2048, 2048)),
            ("w_up", (2048, 8192)), ("w_dn", (8192, 2048)),
            ("w_sT", (128, 16, 128)), ("bs_row", (1, 2048)), ("bs0", (1, 16, 16)), ("w00", (16, 16)),
            ("lnv_g", (128, 2048)), ("lnv_b", (128, 2048)), ("ln1_g", (128, 2048)), ("ln1_b", (128, 2048)),
            ("ln2_g", (128, 2048)), ("ln2_b", (128, 2048)),
            ("wconv", (128, 48, 4)), ("alog", (128, 16)), ("dtb", (128, 16)), ("wonorm", (128, 1)),
            ("scT", (128, 48, 3, 16)), ("ssm", (16, 16, 128, 128)),
            ("ident", (128, 128)), ("maskNb", (128, 128)), ("maskAb", (128, 128)), ("ucs", (128, 128)),
            ("bones", (128, 128)), ("ci", (128, 2, 128)), ("tri", (128, 128)), ("i16b", (128, 16, 16)),
        ]:
            d[name] = self.din(name, shape)
        for name, shape in [
            ("y_out", (NOWN, 2048)), ("ncp", (128, 48, 3)), ("ssm_p", (128, 16, 128)), ("vs", (16, 2048)),
            ("ncs", (128, 48, 3, 16)), ("ssm_s", (16, 16, 128, 128)),
        ]:
            d[name] = self.dout(name, shape)
        for name, shape in self.dbg:
            if name.startswith("in_"):
                d[name] = self.din(name, shape)
            else:
                d[name] = self.dout(name, shape)

        C = self.C = Ctx(nc, es)
        sb = lambda n, s, dt: es.enter_context(nc.sbuf_tensor("sb_" + n, list(s), dt))
        self.ident_f = T(sb("ident_f", (128, 128), F32)[:])
        self.ident_b = T(sb("ident_b", (128, 128), BF16)[:])
        self.ones_f = T(sb("ones_f", (128, 128), F32)[:])
        self.ones_b = T(sb("ones_b", (128, 128), BF16)[:])
        self.maskNb = T(sb("maskNb", (128, 128), F32)[:])
        self.maskAb = T(sb("maskAb", (128, 128), F32)[:])
        self.ucs = T(sb("ucs", (128, 128), F32)[:])
        self.bones = T(sb("bones", (128, 128), F32)[:])
        self.ci = T(sb("ci", (128, 2, 128), F32)[:])
        self.wconv = T(sb("wconv", (128, 48, 4), F32)[:])
        self.wonorm = T(sb("wonorm", (128, 1), F32)[:])
        self.nA = T(sb("nA", (128, 16), F32)[:])
        self.dtb = T(sb("dtb", (128, 16), F32)[:])
        self.eps_ln = T(sb("eps_ln", (128, 1), F32)[:])
        for t_, nm in [(self.ident_f, "ident"), (self.maskNb, "maskNb"), (self.maskAb, "maskAb"), (self.ucs, "ucs"),
                       (self.bones, "bones"), (self.ci, "ci"), (self.wconv, "wconv"), (self.wonorm, "wonorm"),
                       (self.dtb, "dtb"), (self.nA, "alog")]:
            C.dma("sp", t_.ap, d[nm], writes=[t_])
        C.cp("dve", self.ident_b.ap, self.ident_f.ap, [self.ident_f], [self.ident_b])
        C.op("dve", lambda e: e.memset(self.ones_f.ap, 1.0), [], [self.ones_f])
        C.op("dve", lambda e: e.memset(self.ones_b.ap, 1.0), [], [self.ones_b])
        C.act(self.nA.ap, self.nA.ap, AF.Exp, [self.nA], [self.nA])
        C.ts("dve", self.nA.ap, self.nA.ap, -1.0, None, ALU.mult, None, [self.nA], [self.nA])

        self.eps_rms = T(sb("eps_rms", (128, 1), F32)[:])
        C.op("dve", lambda e: e.memset(self.eps_rms.ap, RMS_EPS), [], [self.eps_rms])
        self.i16b = T(sb("i16b", (128, 16, 16), F32)[:])
        C.dma("sp", self.i16b.ap, d["i16b"], writes=[self.i16b])
        ps = lambda n, s, dt: es.enter_context(nc.psum_tensor(n, list(s), dt))
        self.bank = [T(ps("pb%d" % i, (128, 512), F32)[:]) for i in range(8)]
        for t_ in self.bank:
            t_.b.excl = True
        self.psbig = Ring(self.bank[0:5])
        self.pstr = Ring(self.bank[5:7])

        nbig = nc.sbuf_bytes_remaining - 256
        nbig = nbig // 64 * 64
        self.nbig = nbig
        self.big = sb("big", (128, nbig), U8)
        G = self.G = Arena(self.big, 0, nbig)
        self.ybT = G.t((128, 16, NOWN), BF16)
        self.baseB = G.p
        self.yaT = G.t((128, 16, NOWN), BF16)
        self.baseA = G.p
        self.topB = (nbig - 128 * 16 * 6 - 128) // 64 * 64
        GS = Arena(self.big, self.topB, nbig)
        self.S = GS.t((128, 16, 128), F32)
        self.Sb = GS.t((128, 16, 128), BF16)
        C.op("dve", lambda e: e.memset(self.S.ap, 0.0), [], [self.S])
        C.op("dve", lambda e: e.memset(self.Sb.ap, 0.0), [], [self.Sb])
        self.Sh = [T(self.S[:, h, :]) for h in range(16)]
        self.Sbh = [T(self.Sb[:, h, :]) for h in range(16)]
        for h in range(16):
            self.Sh[h].b.w = dict(self.S.b.w)
            self.Sbh[h].b.w = dict(self.Sb.b.w)

        if "P" in self.phases:
            self.mixerB(False)
        if "B" in self.phases:
            self.mixerB(True)
        elif "in_ybT" in d:
            self.load_dbg_bf(self.ybT, d["in_ybT"])
        if "A" in self.phases:
            self.mixerA()
        elif "in_yaT" in d:
            self.load_dbg_bf(self.yaT, d["in_yaT"])
        if "M" in self.phases:
            self.merge_tail()
        C.finish()
        es.close()

    def rsqrt(self, out, in_, eps, R, W, scale=1.0):
        C = self.C
        C.ts("dve", out, in_, scale, eps, ALU.mult, ALU.add, R, [W])
        C.act(out, out, AF.Sqrt, [W], [W])
        C.op("dve", lambda e: e.reciprocal(out=out, in_=out), [W], [W])

    def load_dbg_bf(self, t, src):
        self.C.dma("pool", t.ap, src.rearrange("(k p) n -> p k n", p=128), writes=[t])

    def dump_fm(self, t, name, ncols, at=None):
        C = self.C
        C.barrier()
        if at is None:
            at = self.nbig - 8 * 1024 - 64
        A = Arena(self.big, at, self.nbig)
        tmp = A.t((128, ncols), F32)
        dst = self.d[name].rearrange("(k p) n -> p k n", p=128)
        for k in range(16):
            C.cp("dve", tmp.ap, t[:, k, 0:ncols], [t], [tmp])
            C.dma("sp", dst[:, k, :], tmp.ap, reads=[tmp])

    def wload(self, ring, src2d, ncols, k=16):
        slot = ring.next()
        view = slot.ap[:, 0:k * ncols].rearrange("p (k n) -> p k n", k=k)
        self.C.dma("pool", view, src2d.rearrange("(k p) n -> p k n", p=128), writes=[slot])
        return T(view, slot.b)

    def mixerA(self):
        C, d = self.C, self.d
        C.barrier()
        A = Arena(self.big, self.baseA, self.nbig)
        xT = A.t((128, 16, NCO), BF16)
        C.dma("pool", xT.ap, d["xTo"].rearrange("(k p) n -> p k n", p=128), writes=[xT])
        wring = Ring([T(A.ap((128, 4096), BF16)) for _ in range(3)])
        va_p0 = A.p
        va = A.t((128, 9, 2048), BF16)
        va32 = A.t((16, 2048), F32)
        lng = A.t((128, 2048), F32)
        lnb = A.t((128, 2048), F32)
        wsm = A.t((128, 16, 128), BF16)
        bsr = A.t((1, 2048), F32)
        bs0 = A.t((1, 16, 16), F32)
        w00 = A.t((16, 16), F32)
        w00I = A.t((16, 16, 16), BF16)
        stats = A.t((128, 4, 6), F32)
        mv = A.t((128, 2), F32)
        rstd = A.t((128, 1), F32)
        C.dma("sp", lng.ap, d["lnv_g"], writes=[lng])
        C.dma("sp", lnb.ap, d["lnv_b"], writes=[lnb])
        C.dma("sp", bsr.ap, d["bs_row"], writes=[bsr])
        C.dma("sp", bs0.ap, d["bs0"], writes=[bs0])
        C.dma("sp", w00.ap, d["w00"], writes=[w00])
        A2 = Arena(self.big, va_p0, self.nbig)
        wtmp = T(A2.ap((128, 16, 128), F32), va.b)
        C.dma("sp", wtmp.ap, d["w_sT"], writes=[wtmp])
        tri = T(A2.ap((128, 128), F32), va.b)
        C.dma("sp", tri.ap, d["tri"], writes=[tri])
        C.tt("dve", wsm.ap, wtmp.ap, bc(tri.ap, [128, 16, 128], 1), ALU.mult, [wtmp, tri], [wsm])
        C.tt("dve", w00I.ap, bc(self.ident_f[0:16, 0:16], [16, 16, 16], 1), bc(w00.ap, [16, 16, 16], 2), ALU.mult,
             [self.ident_f, w00], [w00I])
        yaT = self.yaT
        blocks = [(3, 350), (350, 697), (697, 1043)]
        for g in range(8):
            wt = self.wload(wring, d["w_in"][:, OU + g * 256: OU + (g + 1) * 256], 256)
            for j in range(2):
                for (c0, c1) in blocks:
                    ps = self.psbig.next()
                    n = c1 - c0
                    for k in range(16):
                        C.mm(ps[:, 0:n], wt[:, k, j * 128:(j + 1) * 128], xT[:, k, c0:c1], [wt, xT], ps, k == 0, k == 15)
                    C.act(yaT[:, 2 * g + j, c0 - 3:c1 - 3], ps[:, 0:n], AF.Gelu, [ps], [yaT])
        for g in range(8):
            wt = self.wload(wring, d["w_in"][:, OVA + g * 256: OVA + (g + 1) * 256], 256)
            for t in range(9):
                nt = 128 if t < 8 else 16
                c0 = 3 + 128 * t
                ps = self.psbig.next()
                for k in range(16):
                    C.mm(ps[0:nt, 0:256], xT[:, k, c0:c0 + nt], wt[:, k, :], [wt, xT], ps, k == 0, k == 15)
                if t < 8:
                    C.act(va[:, t, g * 256:(g + 1) * 256], ps[:, 0:256], AF.Gelu, [ps], [va])
                else:
                    C.act(va32[:, g * 256:(g + 1) * 256], ps[0:16, 0:256], AF.Gelu, [ps], [va32])
        tmpn = T(A.ap((128, 512), F32))
        for t in range(9):
            nt = 128 if t < 8 else 16
            src = va[0:nt, t, :] if t < 8 else va32.ap
            srcT = va if t < 8 else va32
            for q in range(4):
                C.op("dve", lambda e: e.bn_stats(out=stats[0:nt, q, :], in_=src[:, q * 512:(q + 1) * 512]), [srcT], [stats])
            C.op("dve", lambda e: e.bn_aggr(out=mv[0:nt, :], in_=stats[0:nt, :, :]), [stats], [mv])
            self.rsqrt(rstd[0:nt, :], mv[0:nt, 1:2], LN_EPS, [mv], rstd)
            for q in range(4):
                qs = slice(q * 512, (q + 1) * 512)
                C.ts("dve", tmpn[0:nt, :], src[:, qs], mv[0:nt, 0:1], rstd[0:nt, 0:1], ALU.subtract, ALU.mult,
                     [srcT, mv, rstd], [tmpn])
                C.tt("dve", tmpn[0:nt, :], tmpn[0:nt, :], lng[0:nt, qs], ALU.mult, [tmpn, lng], [tmpn])
                if t < 8:
                    C.tt("dve", va[:, t, qs], tmpn.ap, lnb[:, qs], ALU.add, [tmpn, lnb], [va])
                else:
                    C.tt("dve", va32[:, qs], tmpn[0:16, :], lnb[0:16, qs], ALU.add, [tmpn, lnb], [va32])
            if t == 8:
                C.dma("sp", d["vs"], va32.ap, reads=[va32])
                C.cp("dve", va[0:16, 8, :], va32.ap, [va32], [va])
        for t in range(9):
            for g4 in range(4):
                ps = self.psbig.next()
                nt = 128 if t < 8 else 16
                for gi in range(4):
                    g = g4 * 4 + gi
                    if t < 8:
                        C.mm(ps[:, gi * 128:(gi + 1) * 128], va[:, t, g * 128:(g + 1) * 128], wsm[:, g, :], [va, wsm], ps, True, False)
                        C.mm(ps[:, gi * 128:(gi + 1) * 128], self.ones_f[0:1, :], bsr[0:1, g * 128:(g + 1) * 128],
                             [self.ones_f, bsr], ps, False, True)
                    else:
                        C.mm(ps[:, gi * 128:gi * 128 + 16], va[0:16, 8, g * 128:(g + 1) * 128], w00I[:, g, :], [va, w00I], ps, True, False)
                        C.mm(ps[:, gi * 128:gi * 128 + 16], self.ones_f[0:1, :], bs0[0:1, g, :], [self.ones_f, bs0], ps, False, True)
                dst = yaT[:, g4 * 4:g4 * 4 + 4, t * 128:t * 128 + nt]
                src = ps.ap.rearrange("p (g n) -> p g n", g=4)[:, :, 0:nt]
                C.tt("dve", dst, dst, src, ALU.mult, [yaT, ps], [yaT])
        if "yaT" in d:
            self.dump_fm(self.yaT, "yaT", NOWN)
        C.barrier()

    def layernorm_rows(self, A, x, nt, g, b, dst_fn, stats, mv, rstd, tmp_ring):
        C = self.C
        for q in range(4):
            C.op("dve", lambda e: e.bn_stats(out=stats[0:nt, q, :], in_=x.ap[0:nt, q * 512:(q + 1) * 512]), [x], [stats])
        C.op("dve", lambda e: e.bn_aggr(out=mv[0:nt, :], in_=stats[0:nt, :, :]), [stats], [mv])
        self.rsqrt(rstd[0:nt, :], mv[0:nt, 1:2], LN_EPS, [mv], rstd)
        for q in range(4):
            qs = slice(q * 512, (q + 1) * 512)
            tmp = tmp_ring.next()
            C.ts("dve", tmp[0:nt, :], x.ap[0:nt, qs], mv[0:nt, 0:1], rstd[0:nt, 0:1], ALU.subtract, ALU.mult,
                 [x, mv, rstd], [tmp])
            C.tt("dve", tmp[0:nt, :], tmp[0:nt, :], g[0:nt, qs], ALU.mult, [tmp, g], [tmp])
            dst_fn(q, tmp)

    def merge_tail(self):
        C, d = self.C, self.d
        C.barrier()
        mT_lo = (self.nbig - 16 * NOWN * 2) // 64 * 64
        mT = T(Arena(self.big, mT_lo, self.nbig).ap((128, 16, NOWN), BF16))
        A = Arena(self.big, self.baseA, mT_lo)
        xT = A.t((128, 16, NCO), BF16)
        C.dma("pool", xT.ap, d["xTo"].rearrange("(k p) n -> p k n", p=128), writes=[xT])
        wring = Ring([T(A.ap((128, 4096), BF16)) for _ in range(4)])
        sgr = A.ring(3, (128, 512), F32)
        t1 = A.t((128, 2, NOWN), F32)
        yaT, ybT = self.yaT, self.ybT
        blocks = [(0, 347), (347, 694), (694, NOWN)]
        for g in range(8):
            for br, (wsrc, gsrc, yT) in enumerate(((d["w_pa"], OGA, yaT), (d["w_pb"], OGB, ybT))):
                wy = self.wload(wring, wsrc[:, g * 256:(g + 1) * 256], 256)
                wg_ = self.wload(wring, d["w_in"][:, gsrc + g * 256:gsrc + (g + 1) * 256], 256)
                for j in range(2):
                    f = 2 * g + j
                    js = slice(j * 128, (j + 1) * 128)
                    for (c0, c1) in blocks:
                        n = c1 - c0
                        psg = self.psbig.next()
                        for k in range(16):
                            C.mm(psg[:, 0:n], wg_[:, k, js], xT[:, k, 3 + c0:3 + c1], [wg_, xT], psg, k == 0, k == 15)
                        sg = sgr.next()
                        C.act(sg[:, 0:n], psg[:, 0:n], AF.Sigmoid, [psg], [sg])
                        psy = self.psbig.next()
                        for k in range(16):
                            C.mm(psy[:, 0:n], wy[:, k, js], yT[:, k, c0:c1], [wy, yT], psy, k == 0, k == 15)
                        if br == 0:
                            C.tt("dve", t1[:, j, c0:c1], psy[:, 0:n], sg[:, 0:n], ALU.mult, [psy, sg], [t1])
                        else:
                            C.tt("dve", sg[:, 0:n], psy[:, 0:n], sg[:, 0:n], ALU.mult, [psy, sg], [sg])
                            C.tt("dve", mT[:, f, c0:c1], t1[:, j, c0:c1], sg[:, 0:n], ALU.add, [t1, sg], [mT])
        if "mT" in d:
            self.dump_fm(mT, "mT", NOWN, at=self.baseA)
        C.barrier()
        A = Arena(self.big, 0, mT_lo)
        x1 = A.t((128, 9, 2048), F32)
        base2 = A.p
        wring = Ring([T(A.ap((128, 4096), BF16)) for _ in range(4)])
        xtr = A.ring(3, (128, 512), F32)
        lng = A.t((128, 2048), F32)
        lnb = A.t((128, 2048), F32)
        x1b = A.ring(2, (128, 2048), BF16)
        tmpr = A.ring(2, (128, 512), F32)
        stats = A.t((128, 4, 6), F32)
        mv = A.t((128, 2), F32)
        rstd = A.t((128, 1), F32)
        C.dma("sp", lng.ap, d["ln1_g"], writes=[lng])
        C.dma("sp", lnb.ap, d["ln1_b"], writes=[lnb])
        for n in range(4):
            ns = slice(n * 512, (n + 1) * 512)
            wk = [self.wload(wring, d["w_o"][kh * 1024:(kh + 1) * 1024, ns], 512, k=8) for kh in range(2)]
            for t in range(9):
                nt = 128 if t < 8 else 16
                xt = xtr.next()
                C.dma("sp", xt[0:nt, :], d["xtok"][t * 128:t * 128 + nt, ns], writes=[xt])
                ps = self.psbig.next()
                for k in range(16):
                    C.mm(ps[0:nt, :], mT[:, k, t * 128:t * 128 + nt], wk[k // 8][:, k % 8, :], [mT, wk[k // 8]], ps,
                         k == 0, k == 15)
                C.stt("dve", x1[0:nt, t, ns], xt[0:nt, :], ALPHA, ps[0:nt, :], ALU.mult, ALU.add, [xt, ps], [x1])
        x1T = T(mT.ap, mT.b)
        for t in range(9):
            nt = 128 if t < 8 else 16
            xb = x1b.next()
            x1row = T(x1[:, t, :], x1.b)

            def put(q, tmp, t=t, nt=nt, xb=xb):
                qs = slice(q * 512, (q + 1) * 512)
                C.tt("dve", x1[0:nt, t, qs], tmp[0:nt, :], lnb[0:nt, qs], ALU.add, [tmp, lnb], [x1])
                C.cp("act", xb[0:nt, qs], x1[0:nt, t, qs], [x1], [xb])
            self.layernorm_rows(A, x1row, nt, lng, lnb, put, stats, mv, rstd, tmpr)
            for f8 in range(2):
                pb = self.pstr.next()
                pv = pb.ap.bitcast(BF16).rearrange("p (f n) -> p f n", f=8)
                for i in range(8):
                    f = f8 * 8 + i
                    C.tr(pv[:, i, 0:nt], xb[0:nt, f * 128:(f + 1) * 128], self.ident_b[0:nt, 0:nt], [xb, self.ident_b], pb)
                C.cp("act", x1T[:, f8 * 8:f8 * 8 + 8, t * 128:t * 128 + nt], pv[:, :, 0:nt], [pb], [x1T])
        if "x1" in d:
            for t in range(9):
                nt = 128 if t < 8 else 16
                C.dma("sp", d["x1"][t * 128:t * 128 + nt, :], x1[0:nt, t, :], reads=[x1])
        C.barrier()
        A = Arena(self.big, base2, mT_lo)
        wring = Ring([T(A.ap((128, 4096), BF16)) for _ in range(4)])
        hT = A.t((128, 8, NOWN), BF16)
        rtmp = A.ring(3, (128, 512), F32)
        lng = A.t((128, 2048), F32)
        lnb = A.t((128, 2048), F32)
        outr = A.ring(3, (128, 512), F32)
        tmpr = A.ring(2, (128, 512), F32)
        stats = A.t((128, 4, 6), F32)
        mv = A.t((128, 2), F32)
        rstd = A.t((128, 1), F32)
        C.dma("sp", lng.ap, d["ln2_g"], writes=[lng])
        C.dma("sp", lnb.ap, d["ln2_b"], writes=[lnb])
        for e8 in range(8):
            for q in range(4):
                wt = self.wload(wring, d["w_up"][:, e8 * 1024 + q * 256:e8 * 1024 + (q + 1) * 256], 256)
                for j in range(2):
                    for (c0, c1) in blocks:
                        n = c1 - c0
                        ps = self.psbig.next()
                        for k in range(16):
                            C.mm(ps[:, 0:n], wt[:, k, j * 128:(j + 1) * 128], x1T[:, k, c0:c1], [wt, x1T], ps, k == 0, k == 15)
                        r = rtmp.next()
                        C.act(r[:, 0:n], ps[:, 0:n], AF.Relu, [ps], [r])
                        C.tt("dve", hT[:, q * 2 + j, c0:c1], r[:, 0:n], r[:, 0:n], ALU.mult, [r], [hT])
            for n in range(4):
                ns = slice(n * 512, (n + 1) * 512)
                wd = self.wload(wring, d["w_dn"][e8 * 1024:(e8 + 1) * 1024, ns], 512, k=8)
                for t in range(9):
                    nt = 128 if t < 8 else 16
                    ps = self.psbig.next()
                    for k in range(8):
                        C.mm(ps[0:nt, :], hT[:, k, t * 128:t * 128 + nt], wd[:, k, :], [hT, wd], ps, k == 0, k == 7)
                    if e8 == 0:
                        C.stt("dve", x1[0:nt, t, ns], x1[0:nt, t, ns], ALPHA, ps[0:nt, :], ALU.mult, ALU.add, [x1, ps], [x1])
                    else:
                        C.tt("dve", x1[0:nt, t, ns], x1[0:nt, t, ns], ps[0:nt, :], ALU.add, [x1, ps], [x1])
        for t in range(9):
            nt = 128 if t < 8 else 16
            x1row = T(x1[:, t, :], x1.b)

            def put2(q, tmp, t=t, nt=nt):
                qs = slice(q * 512, (q + 1) * 512)
                o = outr.next()
                C.tt("dve", o[0:nt, :], tmp[0:nt, :], lnb[0:nt, qs], ALU.add, [tmp, lnb], [o])
                C.dma("sp", d["y_out"][t * 128:t * 128 + nt, qs], o[0:nt, :], reads=[o])
            self.layernorm_rows(A, x1row, nt, lng, lnb, put2, stats, mv, rstd, tmpr)
        C.barrier()

    def sample_phase(self, side, lo, betas, egs, glbs):
        C, d = self.C, self.d
        C.barrier()
        A = Arena(self.big, lo, self.topB)
        ident_f, ident_b = self.ident_f, self.ident_b
        NW = 4

        class Slot:
            pass
        slots = []
        for i in range(NW):
            sl = Slot()
            sl.bank = Ring([self.bank[2 * i], self.bank[2 * i + 1]])
            sl.Ss = A.t((128, 16, 128), F32)
            sl.Sn = A.t((128, 16, 128), F32)
            sl.km = A.t((128, 16, 16), F32)
            sl.qm = A.t((128, 16, 16), F32)
            for nm in ("ktok", "qtok", "vtok", "kS", "qS", "vn", "os", "junk"):
                setattr(sl, nm, A.t((16, 128), F32))
            sl.vm = A.ring(4, (16, 128), F32)
            sl.qk = A.t((16, 1), F32)
            sl.ss = A.t((16, 1), F32)
            sl.on = A.t((16, 128), BF16)
            slots.append(sl)

        def head(h, sl):
            C.dma("sp", sl.Ss.ap, d["ssm"][:, h, :, :].rearrange("s d e -> d s e"), writes=[sl.Ss])
            bT = sl.bank.next()
            bTb = bT.ap.bitcast(BF16)
            C.tr(bTb[0:16, 0:128], side["knT"][:, h, :], ident_b.ap, [side["knT"], ident_b], bT)
            C.tr(bTb[0:16, 128:256], side["qnT"][:, h, :], ident_b.ap, [side["qnT"], ident_b], bT)
            C.tr(bTb[0:16, 256:384], side["vT"][:, h, :], ident_b.ap, [side["vT"], ident_b], bT)
            C.tt("dve", sl.km.ap, bc(side["knT"][:, h, :], [128, 16, 16], 1), self.i16b.ap, ALU.mult, [side["knT"], self.i16b], [sl.km])
            C.tt("dve", sl.qm.ap, bc(side["qnT"][:, h, :], [128, 16, 16], 1), self.i16b.ap, ALU.mult, [side["qnT"], self.i16b], [sl.qm])
            yield
            C.cp("act", sl.ktok.ap, bTb[0:16, 0:128], [bT], [sl.ktok])
            C.cp("act", sl.qtok.ap, bTb[0:16, 128:256], [bT], [sl.qtok])
            C.cp("act", sl.vtok.ap, bTb[0:16, 256:384], [bT], [sl.vtok])
            C.tt("dve", sl.junk.ap, sl.qtok.ap, sl.ktok.ap, ALU.mult, [sl.qtok, sl.ktok], [sl.junk])
            C.op("dve", lambda e: e.reduce_sum(out=sl.qk.ap, in_=sl.junk.ap, axis=AX.X), [sl.junk], [sl.qk])
            yield
            bP = sl.bank.next()
            for s_ in range(16):
                C.mm(bP[0:16, 0:128], sl.km[:, s_, :], sl.Ss[:, s_, :], [sl.km, sl.Ss], bP, s_ == 0, s_ == 15)
            for s_ in range(16):
                C.mm(bP[0:16, 128:256], sl.qm[:, s_, :], sl.Ss[:, s_, :], [sl.qm, sl.Ss], bP, s_ == 0, s_ == 15)
            yield
            C.stt("dve", sl.vn.ap, bP[0:16, 0:128], egs[:, h:h + 1], sl.vtok.ap, ALU.mult, ALU.subtract, [bP, egs, sl.vtok], [sl.vn])
            C.ts("dve", sl.vn.ap, sl.vn.ap, betas[:, h:h + 1], -1.0, ALU.mult, ALU.mult, [sl.vn, betas], [sl.vn])
            C.act(sl.os.ap, bP[0:16, 128:256], AF.Copy, [bP, egs], [sl.os], scale=egs[:, h:h + 1])
            C.stt("dve", sl.os.ap, sl.vn.ap, sl.qk[:, 0:1], sl.os.ap, ALU.mult, ALU.add, [sl.vn, sl.qk, sl.os], [sl.os])
            yield
            for s4 in range(4):
                vms = []
                for s_ in range(4):
                    sg = 4 * s4 + s_
                    vm = sl.vm.next()
                    C.ts("dve", vm.ap, sl.vn.ap, ident_f[0:16, sg:sg + 1], None, ALU.mult, None, [sl.vn, ident_f], [vm])
                    vms.append(vm)
                yield
                bU = sl.bank.next()
                for s_ in range(4):
                    C.mm(bU[:, s_ * 128:(s_ + 1) * 128], sl.ktok.ap, vms[s_].ap, [sl.ktok, vms[s_]], bU)
                yield
                for s_ in range(4):
                    sg = 4 * s4 + s_
                    C.stt("dve", sl.Sn[:, sg, :], sl.Ss[:, sg, :], glbs[:, sg, h:h + 1], bU[:, s_ * 128:(s_ + 1) * 128],
                          ALU.mult, ALU.add, [sl.Ss, glbs, bU], [sl.Sn])
                yield
            C.dma("sp", d["ssm_s"][:, h, :, :].rearrange("s d e -> d s e"), sl.Sn.ap, reads=[sl.Sn])
            C.act(sl.junk.ap, sl.os.ap, AF.Square, [sl.os], [sl.junk, sl.ss], accum_out=sl.ss[:, 0:1])
            self.rsqrt_act(sl.ss.ap, sl.ss.ap, [sl.ss], sl.ss, scale=1.0 / 128.0)
            C.act(sl.on.ap, sl.os.ap, AF.Copy, [sl.os, sl.ss], [sl.on], scale=sl.ss[:, 0:1])
            yield
            bO = sl.bank.next()
            bOb = bO.ap.bitcast(BF16)
            C.tr(bOb[:, 0:16], sl.on.ap, ident_b[0:16, 0:16], [sl.on, ident_b], bO)
            yield
            C.stt("dve", self.ybT[:, h, 1024:1040], bOb[:, 0:16], self.wonorm[:, 0:1], side["szT"][:, h, :], ALU.mult, ALU.mult,
                  [bO, self.wonorm, side["szT"]], [self.ybT])

        active = []
        hs = iter(range(16))
        free = list(range(NW))
        while True:
            while free:
                h = next(hs, None)
                if h is None:
                    break
                i = free.pop(0)
                active.append((i, head(h, slots[i])))
            if not active:
                break
            for it_ in list(active):
                try:
                    next(it_[1])
                except StopIteration:
                    active.remove(it_)
                    free.append(it_[0])

    def rsqrt_act(self, out, in_, R, W, scale=1.0):
        C = self.C
        n = out.shape[0]
        C.act(out, in_, AF.Ln, list(R) + [self.eps_rms], [W], bias=self.eps_rms[0:n, 0:1], scale=scale)
        C.act(out, out, AF.Exp, [W], [W], scale=-0.5)

    def mixerB(self, own):
        C, d = self.C, self.d
        C.barrier()
        NC_ = NCO if own else NCP
        NTK = NOWN if own else 1024
        A = Arena(self.big, self.baseB, self.topB)
        if own:
            side = {nm: A.t((128, 16, 16), BF16) for nm in ("knT", "qnT", "vT", "szT")}
            betas = A.t((16, 16), F32)
            egs = A.t((16, 16), F32)
            glbs = A.t((128, 16, 16), F32)
            side_end = A.p
        xT = A.t((128, 16, NC_), BF16)
        C.dma("pool", xT.ap, d["xTo" if own else "xTp"].rearrange("(k p) n -> p k n", p=128), writes=[xT])
        wring = Ring([T(A.ap((128, 4096), BF16)) for _ in range(3)])
        psbig = Ring(self.bank[0:2])
        psc = Ring(self.bank[2:8])
        blocks = [(0, 348), (348, 696), (696, NC_)] if own else [(0, 343), (343, 686), (686, NC_)]
        tblocks = [(0, 347), (347, 694), (694, NTK)] if own else [(0, 512), (512, 1024)]
        ident_f, ident_b, ones_f, ones_b = self.ident_f, self.ident_b, self.ones_f, self.ones_b
        Atmp = Arena(self.big, self.topB - 4096, self.topB)
        wba = Atmp.t((128, 16, 32), BF16)
        C.dma("pool", wba.ap, d["w_in"][:, OBETA:OBETA + 32].rearrange("(k p) n -> p k n", p=128), writes=[wba])
        beta = A.t((128, 8, 16), F32)
        g = Atmp.t((128, 8, 16), F32)
        gc = A.t((128, 8, 16), F32)
        eg = A.t((128, 8, 16), F32)
        ekd = A.t((128, 8, 16), F32)
        bg = A.t((128, 8, 16), F32)
        glb = A.t((128, 2, 8, 16), F32)
        for t in range(8):
            ps = psbig.next()
            for k in range(16):
                C.mm(ps[:, 0:32], xT[:, k, 3 + 128 * t:3 + 128 * t + 128], wba[:, k, :], [xT, wba], ps, k == 0, k == 15)
            C.act(beta[:, t, :], ps[:, 0:16], AF.Sigmoid, [ps], [beta])
            C.tt("dve", g[:, t, :], ps[:, 16:32], self.dtb.ap, ALU.add, [ps, self.dtb], [g])
        C.act(g.ap, g.ap, AF.Exp, [g], [g])
        C.act(g.ap, g.ap, AF.Ln, [g], [g], bias=1.0)
        C.tt("dve", g.ap, g.ap, bc(self.nA.ap, [128, 8, 16], 1), ALU.mult, [g, self.nA], [g])
        g2 = g.ap.rearrange("p t h -> p (t h)")
        ps = psc.next()
        C.mm(ps[:, 0:128], self.ucs.ap, g2, [self.ucs, g], ps)
        C.mm(ps[:, 128:256], self.bones.ap, g2, [self.bones, g], ps)
        C.mm(ps[:, 256:384], self.ci[:, 0, :], g2, [self.ci, g], ps)
        C.mm(ps[:, 384:512], self.ci[:, 1, :], g2, [self.ci, g], ps)
        C.cp("act", gc.ap.rearrange("p t h -> p (t h)"), ps[:, 0:128], [ps], [gc])
        C.tt("dve", ekd.ap.rearrange("p t h -> p (t h)"), ps[:, 128:256], gc.ap.rearrange("p t h -> p (t h)"), ALU.subtract,
             [ps, gc], [ekd])
        C.act(ekd.ap, ekd.ap, AF.Exp, [ekd], [ekd])
        C.act(eg.ap, gc.ap, AF.Exp, [gc], [eg])
        C.act(glb.ap.rearrange("p c t h -> p (c t h)"), ps[:, 256:512], AF.Exp, [ps], [glb])
        C.tt("dve", bg.ap, beta.ap, eg.ap, ALU.mult, [beta, eg], [bg])
        if own:
            gs = Atmp.t((16, 16), F32)
            dgs = Atmp.t((16, 16, 16), F32)
            ps = psbig.next()
            for k in range(16):
                C.mm(ps[0:16, 0:32], xT[:, k, 1027:1043], wba[:, k, :], [xT, wba], ps, k == 0, k == 15)
            C.act(betas.ap, ps[0:16, 0:16], AF.Sigmoid, [ps], [betas])
            C.tt("dve", gs.ap, ps[0:16, 16:32], self.dtb[0:16, :], ALU.add, [ps, self.dtb], [gs])
            C.act(gs.ap, gs.ap, AF.Exp, [gs], [gs])
            C.act(gs.ap, gs.ap, AF.Ln, [gs], [gs], bias=1.0)
            C.tt("dve", gs.ap, gs.ap, self.nA[0:16, :], ALU.mult, [gs, self.nA], [gs])
            C.act(egs.ap, gs.ap, AF.Exp, [gs], [egs])
            C.tt("dve", dgs.ap, bc(ident_f[0:16, 0:16], [16, 16, 16], 2), bc(gs.ap, [16, 16, 16], 1), ALU.mult,
                 [ident_f, gs], [dgs])
            ps = psc.next()
            C.mm(ps[:, 0:256], ones_f[0:16, :], dgs.ap.rearrange("p s h -> p (s h)"), [ones_f, dgs], ps)
            C.act(glbs.ap.rearrange("p s h -> p (s h)"), ps[:, 0:256], AF.Exp, [ps], [glbs])
        C.barrier()
        ngc = A.t((128, 8, 16), F32)
        C.ts("dve", ngc.ap, gc.ap, -1.0, None, ALU.mult, None, [gc], [ngc])
        rawr = A.ring(2, (128, NC_), F32)
        accr = A.ring(2, (128, NTK), F32)
        TB = 348 if own else 512
        sqr = A.ring(2, (128, TB), BF16)
        rsr = A.ring(2, (128, TB), F32)

        class GT:
            pass
        gts = []
        for i in range(2):
            gt_ = GT()
            gt_.knT = A.t((128, 2, NTK), BF16)
            gt_.vT = A.t((128, 2, NTK), BF16)
            if own:
                gt_.qnT = A.t((128, 2, NTK), BF16)
                gt_.szT = A.t((128, 2, NTK), BF16)
            gts.append(gt_)
        if own:
            sctg = [A.t((128, 3, 2, 3, 16), F32) for _ in range(2)]
            ncsr = A.ring(2, (128, 3, 16), F32)
        NSLOT = int(os.environ.get("KB_NSLOT", "3"))

        class Slot:
            pass
        slots = []
        for i in range(NSLOT):
            sl = Slot()
            for nm in ["dg"] + (["EA"] if own else []):
                setattr(sl, nm, A.t((128, 2, 128), F32))
            sl.EN = sl.dg
            sl.bank = Ring([self.bank[2 + 2 * i], self.bank[3 + 2 * i]])
            sl.AB = [A.t((128, 4, 128), F32) for _ in range(2)]
            sl.P = [A.t((128, 2, 128), F32) for _ in range(2)]
            sl.w = sl.dg
            sl.N0 = sl.P[1]
            if own:
                sl.o1 = sl.EA
            for nm in ["kbg", "kd", "vb", "R", "kcd", "vn"] + (["attnT", "on"] if own else []):
                setattr(sl, nm, A.t((128, 2, 128), BF16))
            C.op("dve", lambda e: e.memset(sl.vn.ap, 0.0), [], [sl.vn])
            sl.ss = A.t((128, 2), F32)
            slots.append(sl)
        junk = A.t((128, 128), F32)
        print("mixerB arena used", A.p - A.lo, "free", A.hi - A.p)
        v3 = lambda ap, a=2: ap.rearrange("p (a b) -> p a b", a=a)

        def run_rr(gens, width):
            active = []
            it = iter(gens)
            while True:
                while len(active) < width:
                    g_ = next(it, None)
                    if g_ is None:
                        break
                    active.append(g_)
                if not active:
                    break
                for g_ in list(active):
                    try:
                        next(g_)
                    except StopIteration:
                        active.remove(g_)

        def qkv_chunk(kind, hh, getw, h0, gt):
            wt = getw()
            cidx = {"q": 0, "k": 16, "v": 32}[kind] + h0 + hh
            raw = rawr.next()
            for (c0, c1) in blocks:
                n = c1 - c0
                ps = psbig.next()
                for k in range(16):
                    C.mm(ps[:, 0:n], wt[:, k, hh * 128:(hh + 1) * 128], xT[:, k, c0:c1], [wt, xT], ps, k == 0, k == 15)
                C.cp("act", raw[:, c0:c1], ps[:, 0:n], [ps], [raw])
                yield
            acc = accr.next()
            wc = self.wconv
            C.act(acc[:, 0:1024], raw[:, 0:1024], AF.Copy, [raw, wc], [acc], scale=wc[:, cidx, 0:1])
            for j in range(1, 4):
                C.stt("dve", acc[:, 0:1024], raw[:, j:j + 1024], wc[:, cidx, j:j + 1], acc[:, 0:1024], ALU.mult, ALU.add,
                      [raw, wc, acc], [acc])
            if own:
                sctT = sctg[(h0 // 2) % 2]
                sct = T(sctT[:, {"q": 0, "k": 1, "v": 2}[kind], hh], sctT.b)
                C.ts("dve", acc[:, 1024:1040], raw[:, 1027:1043], wc[:, cidx, 3:4], None, ALU.mult, None, [raw, wc], [acc])
                for j in range(3):
                    C.stt("dve", acc[:, 1024:1040], sct[:, j, :], wc[:, cidx, j:j + 1], acc[:, 1024:1040], ALU.mult, ALU.add,
                          [sct, wc, acc], [acc])
                ncs = ncsr.next()
                C.cp("dve", ncs[:, 0:2, :], sct[:, 1:3, :], [sct], [ncs])
                C.cp("dve", ncs[:, 2, :], raw[:, 1027:1043], [raw], [ncs])
                C.dma("sp", d["ncs"][:, cidx], ncs.ap, reads=[ncs])
                C.dma("sp", d["ncp"][:, cidx, :], raw[:, 1024:1027], reads=[raw])
            yield
            if kind == "v":
                C.act(gt.vT[:, hh, :], acc.ap, AF.Silu, [acc], [gt.vT])
                return
            C.act(acc.ap, acc.ap, AF.Silu, [acc], [acc])
            yield
            dstT = gt.knT if kind == "k" else gt.qnT
            for (c0, c1) in tblocks:
                n = c1 - c0
                sq = sqr.next()
                C.act(sq[:, 0:n], acc[:, c0:c1], AF.Square, [acc], [sq])
                ps = psbig.next()
                C.mm(ps[:, 0:n], ones_b.ap, sq[:, 0:n], [ones_b, sq], ps)
                yield
                rs = rsr.next()
                self.rsqrt_act(rs[:, 0:n], ps[:, 0:n], [ps], rs)
                yield
                if kind == "q":
                    C.stt("dve", dstT[:, hh, c0:c1], acc[:, c0:c1], 128.0 ** -0.5, rs[:, 0:n], ALU.mult, ALU.mult, [acc, rs], [dstT])
                else:
                    C.tt("dve", dstT[:, hh, c0:c1], acc[:, c0:c1], rs[:, 0:n], ALU.mult, [acc, rs], [dstT])

        def z_chunk(hh, getw, gt):
            wz = getw()
            for (c0, c1) in tblocks:
                n = c1 - c0
                ps = psbig.next()
                for k in range(16):
                    C.mm(ps[:, 0:n], wz[:, k, hh * 128:(hh + 1) * 128], xT[:, k, 3 + c0:3 + c1], [wz, xT], ps, k == 0, k == 15)
                C.act(gt.szT[:, hh, c0:c1], ps[:, 0:n], AF.Silu, [ps], [gt.szT])
                yield

        TS = int(os.environ.get("KB_TS", "9"))
        B3 = [128, 2, 128]

        def tile_scan(t, h0, sl, gt):
            cols = slice(128 * t, 128 * t + 128)
            hs = slice(h0, h0 + 2)
            gch = gc[:, t, hs]
            betab = bc(beta[:, t, hs], B3, 2)
            C.tt("dve", sl.dg.ap, bc(ident_f.ap, B3, 1), bc(gch, B3, 2), ALU.mult, [ident_f, gc], [sl.dg])
            yield
            bk = sl.bank.next()
            C.mm(bk[:, 0:256], ones_f.ap, sl.dg.ap.rearrange("p a b -> p (a b)"), [ones_f, sl.dg], bk)
            yield
            C.tt("dve", sl.EN.ap, v3(bk[:, 0:256]), bc(self.maskNb.ap, B3, 1), ALU.add, [bk, self.maskNb], [sl.EN])
            if own:
                C.tt("dve", sl.EA.ap, v3(bk[:, 0:256]), bc(self.maskAb.ap, B3, 1), ALU.add, [bk, self.maskAb], [sl.EA])
            for hh in range(2):
                C.act(sl.EN[:, hh, :], sl.EN[:, hh, :], AF.Exp, [sl.EN, gc], [sl.EN], scale=-1.0, bias=gc[:, t, h0 + hh:h0 + hh + 1])
                if own:
                    C.act(sl.EA[:, hh, :], sl.EA[:, hh, :], AF.Exp, [sl.EA, ngc], [sl.EA], bias=ngc[:, t, h0 + hh:h0 + hh + 1])
            C.tt("dve", sl.EN.ap, sl.EN.ap, betab, ALU.mult, [sl.EN, beta], [sl.EN])
            yield
            b1 = sl.bank.next()
            for hh in range(2):
                kt = gt.knT[:, hh, cols]
                C.mm(b1[:, hh * 128:(hh + 1) * 128], kt, kt, [gt.knT], b1)
            if own:
                for hh in range(2):
                    C.mm(b1[:, 256 + hh * 128:256 + (hh + 1) * 128], gt.knT[:, hh, cols], gt.qnT[:, hh, cols], [gt.knT, gt.qnT], b1)
            b2 = sl.bank.next()
            b2b = b2.ap.bitcast(BF16)
            for hh in range(2):
                C.tr(b2b[:, hh * 128:(hh + 1) * 128], gt.knT[:, hh, cols], ident_b.ap, [gt.knT, ident_b], b2)
            for hh in range(2):
                C.tr(b2b[:, 256 + hh * 128:256 + (hh + 1) * 128], gt.vT[:, hh, cols], ident_b.ap, [gt.vT, ident_b], b2)
            yield
            C.tt("dve", sl.N0.ap, v3(b1[:, 0:256]), sl.EN.ap, ALU.mult, [b1, sl.EN], [sl.N0])
            if own:
                C.tt("dve", sl.attnT.ap, v3(b1[:, 256:512]), sl.EA.ap, ALU.mult, [b1, sl.EA], [sl.attnT])
            C.tt("dve", sl.kbg.ap, v3(b2b[:, 0:256]), bc(bg[:, t, hs], B3, 2), ALU.mult, [b2, bg], [sl.kbg])
            C.tt("dve", sl.kd.ap, v3(b2b[:, 0:256]), bc(ekd[:, t, hs], B3, 2), ALU.mult, [b2, ekd], [sl.kd])
            C.tt("dve", sl.vb.ap, v3(b2b[:, 256:512]), betab, ALU.mult, [b2, beta], [sl.vb])
            yield
            bI = sl.bank.next()
            for hh in range(2):
                C.tr(bI[:, hh * 128:(hh + 1) * 128], sl.N0[:, hh, :], ident_f.ap, [sl.N0, ident_f], bI)
            yield
            AB = sl.AB[0]
            C.cp("act", AB[:, 0:2, :], v3(bI[:, 0:256]), [bI], [AB])
            C.cp("act", AB[:, 2:4, :], sl.N0.ap, [sl.N0], [AB])
            P = sl.P[0]
            C.tt("dve", P.ap, bc(ident_f.ap, B3, 1), v3(bI[:, 0:256]), ALU.subtract, [ident_f, bI], [P])
            yield
            for l in range(1, 7):
                bX = sl.bank.next()
                bY = sl.bank.next() if l >= 2 else None
                for hh in range(2):
                    Bk, Ak = AB[:, hh, :], AB[:, 2 + hh, :]
                    if l <= 4:
                        C.mm(bX[:, hh * 128:(hh + 1) * 128], Ak, Bk, [AB], bX)
                    if l <= 5:
                        C.mm(bX[:, 256 + hh * 128:256 + (hh + 1) * 128], Bk, Ak, [AB], bX)
                    if l >= 2:
                        C.mm(bY[:, hh * 128:(hh + 1) * 128], Ak, P[:, hh, :], [AB, P], bY)
                yield
                if l <= 5:
                    AB2 = sl.AB[l % 2]
                    if l <= 4:
                        C.cp("act", AB2.ap, v3(bX.ap, 4), [bX], [AB2])
                    else:
                        C.cp("act", AB2[:, 2:4, :], v3(bX[:, 256:512]), [bX], [AB2])
                if l >= 2:
                    if l == 6:
                        C.tt("dve", sl.R.ap, P.ap, v3(bY[:, 0:256]), ALU.add, [P, bY], [sl.R])
                    else:
                        P2 = sl.P[(l + 1) % 2]
                        C.tt("dve", P2.ap, P.ap, v3(bY[:, 0:256]), ALU.add, [P, bY], [P2])
                        P = P2
                if l <= 5:
                    AB = AB2
                yield
            bW = sl.bank.next()
            for hh in range(2):
                C.mm(bW[:, hh * 128:(hh + 1) * 128], sl.R[:, hh, :], sl.vb[:, hh, :], [sl.R, sl.vb], bW)
            for hh in range(2):
                C.mm(bW[:, 256 + hh * 128:256 + (hh + 1) * 128], sl.kbg[:, hh, :], sl.R[:, hh, :], [sl.R, sl.kbg], bW)
            yield
            C.cp("act", sl.w.ap, v3(bW[:, 0:256]), [bW], [sl.w])
            C.cp("dve", sl.kcd.ap, v3(bW[:, 256:512]), [bW], [sl.kcd])

        def tile_chain(t, h0, sl, gt):
            cols = slice(128 * t, 128 * t + 128)
            hs = slice(h0, h0 + 2)
            Sl = [self.Sh[h0], self.Sh[h0 + 1]]
            Sbl = [self.Sbh[h0], self.Sbh[h0 + 1]]
            for c in range(2):
                rows = slice(64 * c, 64 * c + 64)
                bS = sl.bank.next()
                for hh in range(2):
                    C.mm(bS[:, hh * 128:(hh + 1) * 128], sl.kcd[:, hh, :], Sbl[hh].ap, [sl.kcd, Sbl[hh]], bS)
                if own:
                    for hh in range(2):
                        C.mm(bS[:, 256 + hh * 128:256 + (hh + 1) * 128], gt.qnT[:, hh, cols], Sbl[hh].ap, [gt.qnT, Sbl[hh]], bS)
                yield
                C.tt("dve", sl.vn[rows, :, :], sl.w[rows, :, :], v3(bS[rows, 0:256]), ALU.subtract, [sl.w, bS], [sl.vn])
                if own:
                    C.tt("dve", sl.o1[rows, :, :], v3(bS[rows, 256:512]), bc(eg[rows, t, hs], [64, 2, 128], 2), ALU.mult,
                         [bS, eg], [sl.o1])
                yield
                bU = sl.bank.next()
                for hh in range(2):
                    C.mm(bU[:, hh * 128:(hh + 1) * 128], sl.kd[rows, hh, :], sl.vn[rows, hh, :], [sl.kd, sl.vn], bU)
                yield
                for hh in range(2):
                    C.stt("dve", Sl[hh].ap, Sl[hh].ap, glb[:, c, t, h0 + hh:h0 + hh + 1], bU[:, hh * 128:(hh + 1) * 128],
                          ALU.mult, ALU.add, [Sl[hh], glb, bU], [Sl[hh]])
                C.cp("act", self.Sb[:, hs, :], self.S[:, hs, :], Sl, Sbl)
                yield

        def tile_post(t, h0, sl, gt):
            cols = slice(128 * t, 128 * t + 128)
            hs = slice(h0, h0 + 2)
            if own:
                bO = sl.bank.next()
                for hh in range(2):
                    C.mm(bO[:, hh * 128:(hh + 1) * 128], sl.attnT[:, hh, :], sl.vn[:, hh, :], [sl.attnT, sl.vn], bO)
                yield
                C.tt("dve", sl.o1.ap, sl.o1.ap, v3(bO[:, 0:256]), ALU.add, [sl.o1, bO], [sl.o1])
                for hh in range(2):
                    C.act(junk.ap, sl.o1[:, hh, :], AF.Square, [sl.o1], [junk, sl.ss], accum_out=sl.ss[:, hh:hh + 1])
                self.rsqrt_act(sl.ss.ap, sl.ss.ap, [sl.ss], sl.ss, scale=1.0 / 128.0)
                C.tt("dve", sl.on.ap, sl.o1.ap, bc(sl.ss.ap, B3, 2), ALU.mult, [sl.o1, sl.ss], [sl.on])
                yield
                bO2 = sl.bank.next()
                bO2b = bO2.ap.bitcast(BF16)
                for hh in range(2):
                    C.tr(bO2b[:, hh * 128:(hh + 1) * 128], sl.on[:, hh, :], ident_b.ap, [sl.on, ident_b], bO2)
                yield
                C.stt("dve", self.ybT[:, hs, cols], v3(bO2b[:, 0:256]), self.wonorm[:, 0:1], gt.szT[:, :, cols], ALU.mult, ALU.mult,
                      [bO2, self.wonorm, gt.szT], [self.ybT])

        def run_group(h0, gt):
            nt = 8
            prep, post = {}, {}
            prep_done = [False] * nt
            slot_busy = [None] * NSLOT
            chain, chain_t, next_prep, fin = None, 0, 0, 0
            while fin < nt:
                while next_prep < nt and slot_busy[next_prep % NSLOT] is None:
                    slot_busy[next_prep % NSLOT] = next_prep
                    prep[next_prep] = tile_scan(next_prep, h0, slots[next_prep % NSLOT], gt)
                    next_prep += 1
                for t in sorted(prep):
                    try:
                        next(prep[t])
                    except StopIteration:
                        del prep[t]
                        prep_done[t] = True
                if chain is None and chain_t < nt and prep_done[chain_t]:
                    chain = tile_chain(chain_t, h0, slots[chain_t % NSLOT], gt)
                if chain is not None:
                    try:
                        next(chain)
                    except StopIteration:
                        chain = None
                        post[chain_t] = tile_post(chain_t, h0, slots[chain_t % NSLOT], gt)
                        chain_t += 1
                for t in sorted(post):
                    try:
                        next(post[t])
                    except StopIteration:
                        del post[t]
                        slot_busy[t % NSLOT] = None
                        fin += 1
                yield

        stop = os.environ.get("KB_STOP", "all")
        nhg = int(os.environ.get("KB_NHG", "8"))
        WIDTH_IP = int(os.environ.get("KB_WIP", "2"))
        RATIO = int(os.environ.get("KB_RATIO", "1"))
        IPSTEPS = int(os.environ.get("KB_IPSTEPS", "1"))
        NFILL = int(os.environ.get("KB_NFILL", "0"))
        FILLN = int(os.environ.get("KB_FILLN", "256"))
        nhg = nhg if stop != "pre" else 0

        def inproj_gens(hg, gt):
            h0 = 2 * hg
            cache = {}

            def getw(off):
                def f():
                    if off not in cache:
                        cache[off] = self.wload(wring, d["w_in"][:, off + hg * 256:off + (hg + 1) * 256], 256)
                    return cache[off]
                return f
            if own:
                for ki in range(3):
                    C.dma("sp", sctg[hg % 2][:, ki], d["scT"][:, 16 * ki + h0:16 * ki + h0 + 2], writes=[sctg[hg % 2]])
            gens = [qkv_chunk("k", hh, getw(OK_), h0, gt) for hh in range(2)]
            gens += [qkv_chunk("v", hh, getw(OV), h0, gt) for hh in range(2)]
            if own:
                gens += [qkv_chunk("q", hh, getw(OQ), h0, gt) for hh in range(2)]
                gens += [z_chunk(hh, getw(OZ), gt) for hh in range(2)]
            return gens

        def rr_gen(gens, width):
            active = []
            it = iter(gens)
            while True:
                while len(active) < width:
                    g_ = next(it, None)
                    if g_ is None:
                        break
                    active.append(g_)
                if not active:
                    return
                for g_ in list(active):
                    try:
                        next(g_)
                    except StopIteration:
                        active.remove(g_)
                yield

        def group_scan(hg, gt):
            if stop in ("scan", "all"):
                yield from run_group(2 * hg, gt)

        for hg in range(nhg + 1):
            sg = group_scan(hg - 1, gts[(hg - 1) % 2]) if hg >= 1 else None
            ig = rr_gen(inproj_gens(hg, gts[hg % 2]), WIDTH_IP) if hg < nhg else None
            while sg is not None or ig is not None:
                for _ in range(NFILL):
                    C.mm(self.bank[7][:, 0:FILLN], ones_b.ap, xT[:, 0, 0:FILLN], [ones_b, xT], self.bank[7])
                if sg is not None:
                    for _ in range(RATIO):
                        try:
                            next(sg)
                        except StopIteration:
                            sg = None
                            break
                for _ in range(IPSTEPS if ig is not None else 0):
                    try:
                        next(ig)
                    except StopIteration:
                        ig = None
                        if own:
                            gt_ = gts[hg % 2]
                            for nm in ("knT", "qnT", "vT", "szT"):
                                C.cp("dve", side[nm][:, 2 * hg:2 * hg + 2, :], getattr(gt_, nm)[:, :, 1024:1040],
                                     [getattr(gt_, nm)], [side[nm]])
                        break
        if own:
            allS = [self.Sh[h] for h in range(16)]
            C.dma("sp", d["ssm_p"], self.S.ap, reads=allS)
            if stop == "all":
                self.sample_phase(side, side_end, betas, egs, glbs)
            if "ybT" in d:
                self.dump_fm(self.ybT, "ybT", NOWN, at=self.baseB)
        C.barrier()


def _consts():
    i = np.arange(128)
    same = (i[:, None] // 64) == (i[None, :] // 64)
    ident = np.eye(128, dtype=np.float32)
    maskNb = np.where(same & (i[None, :] < i[:, None]), 0.0, BIGMASK).astype(np.float32)
    maskAb = np.where(same & (i[None, :] >= i[:, None]), 0.0, -BIGMASK).astype(np.float32)
    ucs = (same & (i[:, None] <= i[None, :])).astype(np.float32)
    bones = same.astype(np.float32)
    ci = np.zeros((128, 2, 128), np.float32)
    ci[0:64, 0, :] = 1.0
    ci[64:128, 1, :] = 1.0
    tri = (i[None, :] >= i[:, None]).astype(np.float32)
    i16b = np.ascontiguousarray(np.broadcast_to(np.eye(16, dtype=np.float32)[None], (128, 16, 16)))
    return dict(ident=ident, maskNb=maskNb, maskAb=maskAb, ucs=ucs, bones=bones, ci=ci, tri=tri, i16b=i16b)


def host_inputs(inp, c):
    b, half = c // 2, c % 2
    f = np.float32
    xp = inp["x_prompt"]
    xs = inp["x_sample"][16 * c:16 * c + 16, 0]
    own = xp[b, half * 1024:(half + 1) * 1024]
    xTo = np.zeros((2048, NCO), f)
    if half:
        xTo[:, 0:3] = xp[b, 1021:1024].T
    xTo[:, 3:1027] = own.T
    xTo[:, 1027:1043] = xs.T
    xTp = np.zeros((2048, NCP), f)
    if half:
        xTp[:, 3:] = xp[b, 0:1024].T
    xtok = np.concatenate([own, xs], 0)
    w_s = inp["w_s"][0]
    m = dict(
        xTo=xTo, xTp=xTp, xtok=np.ascontiguousarray(xtok),
        w_in=inp["w_in"][0], w_pa=inp["w_proj_a"][0], w_pb=inp["w_proj_b"][0], w_o=inp["w_o"][0],
        w_up=inp["w_up"][0], w_dn=inp["w_down"][0],
        w_sT=np.ascontiguousarray(w_s.transpose(2, 0, 1)),
        bs_row=np.ascontiguousarray(inp["b_s"][0].reshape(1, 2048)),
        bs0=np.ascontiguousarray(np.broadcast_to(inp["b_s"][0][:, 0][None, :, None], (1, 16, 16))),
        w00=np.ascontiguousarray(np.broadcast_to(w_s[:, 0, 0][None, :], (16, 16))),
        wconv=np.ascontiguousarray(inp["w_conv"][0].reshape(4, 48, 128).transpose(2, 1, 0)),
        alog=np.ascontiguousarray(np.broadcast_to(inp["a_log"][0][None, :], (128, 16))),
        dtb=np.ascontiguousarray(np.broadcast_to(inp["dt_bias"][0][None, :], (128, 16))),
        wonorm=np.ascontiguousarray(inp["w_onorm"][0].reshape(128, 1)),
        scT=np.ascontiguousarray(inp["state_conv"][0, 16 * c:16 * c + 16].reshape(16, 3, 48, 128).transpose(3, 2, 1, 0)),
        ssm=np.ascontiguousarray(inp["state_ssm"][0, 16 * c:16 * c + 16]),
    )
    for nm, key in [("lnv_g", "ln_v_g"), ("lnv_b", "ln_v_b"), ("ln1_g", "ln1_g"), ("ln1_b", "ln1_b"),
                    ("ln2_g", "ln2_g"), ("ln2_b", "ln2_b")]:
        m[nm] = np.ascontiguousarray(np.broadcast_to(inp[key][0][None, :], (128, 2048)))
    m.update(_consts())
    return {k: np.ascontiguousarray(v, dtype=f) for k, v in m.items()}


_PROG = None


def _program():
    global _PROG
    if _PROG is None:
        _PROG = Builder()
    return _PROG


def kernel(**inputs):
    inp = {k: np.asarray(v) for k, v in inputs.items()}
    B = _program()
    in_maps = []
    for c in range(8):
        m = host_inputs(inp, c)
        in_maps.append({k: v for k, v in m.items() if k in B.d})
    res = run_bass_kernel_spmd(B.nc, in_maps, core_ids=list(range(8)))
    R = res.results
    f = np.float32
    y_prompt = np.zeros((4, 2048, 2048), f)
    y_sample = np.zeros((128, 1, 2048), f)
    ncp = np.zeros((1, 4, 3, 6144), f)
    ssp = np.zeros((1, 4, 16, 128, 128), f)
    vs = np.zeros((1, 128, 1, 2048), f)
    ncs = np.zeros((1, 128, 3, 6144), f)
    sss = np.zeros((1, 128, 16, 128, 128), f)
    for c in range(8):
        b, half = c // 2, c % 2
        r = R[c]
        y_prompt[b, half * 1024:(half + 1) * 1024] = r["y_out"][0:1024]
        y_sample[16 * c:16 * c + 16, 0] = r["y_out"][1024:1040]
        vs[0, 16 * c:16 * c + 16, 0] = r["vs"]
        ncs[0, 16 * c:16 * c + 16] = r["ncs"].transpose(3, 2, 1, 0).reshape(16, 3, 6144)
        sss[0, 16 * c:16 * c + 16] = r["ssm_s"]
        if half:
            ncp[0, b] = r["ncp"].transpose(2, 1, 0).reshape(3, 6144)
            ssp[0, b] = r["ssm_p"].transpose(1, 0, 2)
    return (y_prompt, y_sample, ncp, ssp, vs, ncs, sss)
```

```python
import os
import numpy as np
from contextlib import ExitStack
import concourse.bass as bass
import concourse.mybir as mybir
from concourse.bass_utils import run_bass_kernel_spmd

F32 = mybir.dt.float32
BF16 = mybir.dt.bfloat16
U8 = mybir.dt.uint8
AF = mybir.ActivationFunctionType
ALU = mybir.AluOpType
AX = mybir.AxisListType

ALPHA = 2.0 ** 0.25
LN_EPS = 1e-5
RMS_EPS = 1e-6
NOWN = 1040
NCO = 1043
NCP = 1027
BIGMASK = 30000.0
OU, OVA, OQ, OK_, OV, OZ, OBETA, OA, OGA, OGB = 0, 2048, 4096, 6144, 8192, 10240, 12288, 12304, 12320, 14368


class Buf:
    __slots__ = ("w", "r", "excl")

    def __init__(self):
        self.w = {}
        self.r = {}
        self.excl = False


class T:
    def __init__(self, ap, b=None):
        self.ap = ap
        self.b = b if b is not None else Buf()

    def __getitem__(self, k):
        return self.ap[k]


def _bufs(ts):
    return [t.b if isinstance(t, T) else t for t in ts]


class Ctx:
    NDS = {"sp": 8, "pool": 8}

    def __init__(self, nc, es):
        self.nc = nc
        self.E = {"pe": nc.tensor, "dve": nc.vector, "act": nc.scalar, "pool": nc.gpsimd, "sp": nc.sync}
        self.sems = {}
        self.cnt = {}
        self.seen = {k: {} for k in self.E}
        for k in self.E:
            self.sems[k] = es.enter_context(nc.semaphore("sm_" + k))
            self.cnt[k] = 0
        self.dq = {}
        for q, n in self.NDS.items():
            lst = []
            for i in range(n):
                key = "d_%s%d" % (q, i)
                self.sems[key] = es.enter_context(nc.semaphore(key))
                self.cnt[key] = 0
                lst.append(key)
            self.dq[q] = [lst, 0]
        self.nwait = 0
        self.nins = 0

    def wait(self, eng, key, val):
        if val <= 0:
            return
        if eng == "pe" and key == "pe":
            return
        if self.seen[eng].get(key, 0) >= val:
            return
        self.E[eng].wait_ge(self.sems[key], val)
        self.seen[eng][key] = val
        self.nwait += 1

    def _deps(self, eng, reads, writes):
        deps = {}
        for b in reads:
            for k, v in b.w.items():
                if deps.get(k, 0) < v:
                    deps[k] = v
            if b.excl:
                for k, v in b.r.items():
                    if k != eng and deps.get(k, 0) < v:
                        deps[k] = v
        for b in writes:
            for k, v in b.w.items():
                if deps.get(k, 0) < v:
                    deps[k] = v
            for k, v in b.r.items():
                if deps.get(k, 0) < v:
                    deps[k] = v
        for k, v in deps.items():
            self.wait(eng, k, v)

    def _mark(self, key, val, reads, writes):
        for b in reads:
            if b.r.get(key, 0) < val:
                b.r[key] = val
        for b in writes:
            b.w = {key: val}
            b.r = {}

    def op(self, eng, fn, reads=(), writes=(), inc=True):
        reads = _bufs(reads)
        writes = _bufs(writes)
        self._deps(eng, reads, writes)
        ins = fn(self.E[eng])
        val = self.cnt[eng] + 1
        if inc:
            ins.then_inc(self.sems[eng], 1)
            self.cnt[eng] = val
        self._mark(eng, val, reads, writes)
        self.nins += 1
        return ins

    def dma(self, q, out, in_, reads=(), writes=()):
        reads = _bufs(reads)
        writes = _bufs(writes)
        lst, i = self.dq[q]
        key = lst[i % len(lst)]
        self.dq[q][1] = i + 1
        self.wait(q, key, self.cnt[key])
        self._deps(q, reads, writes)
        ins = self.E[q].dma_start(out=out, in_=in_)
        val = self.cnt[key] + 16
        ins.then_inc(self.sems[key], 16)
        self.cnt[key] = val
        self._mark(key, val, reads, writes)
        self.nins += 1
        return ins

    def barrier(self):
        for e in self.E:
            for k in self.sems:
                self.wait(e, k, self.cnt[k])

    def finish(self):
        for k in self.sems:
            self.wait("sp", k, self.cnt[k])

    def mm(self, out, lhsT, rhs, R, W, start=True, stop=True):
        return self.op("pe", lambda e: e.matmul(out, lhsT=lhsT, rhs=rhs, start=start, stop=stop), R, [W], inc=stop)

    def tr(self, out, in_, ident, R, W):
        return self.op("pe", lambda e: e.transpose(out, in_, ident), R, [W])

    def act(self, out, in_, func, R, W, **kw):
        return self.op("act", lambda e: e.activation(out=out, in_=in_, func=func, **kw), R, W)

    def tt(self, eng, out, in0, in1, op, R, W):
        return self.op(eng, lambda e: e.tensor_tensor(out=out, in0=in0, in1=in1, op=op), R, W)

    def ts(self, eng, out, in0, s1, s2, op0, op1, R, W):
        if s2 is None:
            return self.op(eng, lambda e: e.tensor_scalar(out=out, in0=in0, scalar1=s1, scalar2=None, op0=op0), R, W)
        return self.op(eng, lambda e: e.tensor_scalar(out=out, in0=in0, scalar1=s1, scalar2=s2, op0=op0, op1=op1), R, W)

    def stt(self, eng, out, in0, scalar, in1, op0, op1, R, W):
        return self.op(eng, lambda e: e.scalar_tensor_tensor(out=out, in0=in0, scalar=scalar, in1=in1, op0=op0, op1=op1), R, W)

    def cp(self, eng, out, in_, R, W):
        if eng == "act":
            return self.act(out, in_, AF.Copy, R, W)
        return self.op(eng, lambda e: e.tensor_copy(out=out, in_=in_), R, W)


ESZ = {F32: 4, BF16: 2, U8: 1}


class Arena:
    def __init__(self, big, lo, hi):
        self.big = big
        self.lo = lo
        self.hi = hi
        self.p = lo

    def ap(self, shape, dtype):
        free = 1
        for s in shape[1:]:
            free *= s
        nb = free * ESZ[dtype]
        nba = (nb + 63) // 64 * 64
        assert self.p + nba <= self.hi, ("arena overflow", self.p, nba, self.hi)
        a = self.big[0:shape[0], self.p:self.p + nb].bitcast(dtype)
        if len(shape) > 2:
            names = ["a%d" % i for i in range(len(shape) - 1)]
            a = a.rearrange("p (%s) -> p %s" % (" ".join(names), " ".join(names)),
                            **{n: s for n, s in zip(names, shape[1:])})
        self.p += nba
        return a

    def t(self, shape, dtype):
        return T(self.ap(shape, dtype))

    def ring(self, n, shape, dtype):
        return Ring([self.t(shape, dtype) for _ in range(n)])


class Ring:
    def __init__(self, items):
        self.items = items
        self.i = 0

    def next(self):
        t = self.items[self.i % len(self.items)]
        self.i += 1
        return t


def bc(ap, shape, axis):
    return ap.unsqueeze(axis).to_broadcast(shape)


class Builder:
    def __init__(self, phases=("P", "B", "A", "M", "T"), dbg=()):
        self.phases = phases
        self.dbg = dbg
        self.nc = bass.Bass("TRN2", target_bir_lowering=False)
        self.es = ExitStack()
        self.build()

    def din(self, name, shape):
        return self.nc.dram_tensor(name, list(shape), F32, kind="ExternalInput").ap()

    def dout(self, name, shape):
        return self.nc.dram_tensor(name, list(shape), F32, kind="ExternalOutput").ap()

    def build(self):
        nc, es = self.nc, self.es
        d = self.d = {}
        for name, shape in [
            ("xTo", (2048, NCO)), ("xTp", (2048, NCP)), ("xtok", (NOWN, 2048)),
            ("w_in", (2048, 16416)), ("w_pa", (2048, 2048)), ("w_pb", (2048, 2048)), ("w_o", (2048, 2048)),
            ("w_up", (2048, 8192)), ("w_dn", (8192, 2048)),
            ("w_sT", (128, 16, 128)), ("bs_row", (1, 2048)), ("bs0", (1, 16, 16)), ("w00", (16, 16)),
            ("lnv_g", (128, 2048)), ("lnv_b", (128, 2048)), ("ln1_g", (128, 2048)), ("ln1_b", (128, 2048)),
            ("ln2_g", (128, 2048)), ("ln2_b", (128, 2048)),
            ("wconv", (128, 48, 4)), ("alog", (128, 16)), ("dtb", (128, 16)), ("wonorm", (128, 1)),
            ("scT", (128, 48, 3, 16)), ("ssm", (16, 16, 128, 128)),
            ("ident", (128, 128)), ("maskNb", (128, 128)), ("maskAb", (128, 128)), ("ucs", (128, 128)),
            ("bones", (128, 128)), ("ci", (128, 2, 128)), ("tri", (128, 128)), ("i16b", (128, 16, 16)),
        ]:
            d[name] = self.din(name, shape)
        for name, shape in [
            ("y_out", (NOWN, 2048)), ("ncp", (128, 48, 3)), ("ssm_p", (128, 16, 128)), ("vs", (16, 2048)),
            ("ncs", (128, 48, 3, 16)), ("ssm_s", (16, 16, 128, 128)),
        ]:
            d[name] = self.dout(name, shape)
        for name, shape in self.dbg:
            if name.startswith("in_"):
                d[name] = self.din(name, shape)
            else:
                d[name] = self.dout(name, shape)

        C = self.C = Ctx(nc, es)
        sb = lambda n, s, dt: es.enter_context(nc.sbuf_tensor("sb_" + n, list(s), dt))
        self.ident_f = T(sb("ident_f", (128, 128), F32)[:])
        self.ident_b = T(sb("ident_b", (128, 128), BF16)[:])
        self.ones_f = T(sb("ones_f", (128, 128), F32)[:])
        self.ones_b = T(sb("ones_b", (128, 128), BF16)[:])
        self.maskNb = T(sb("maskNb", (128, 128), F32)[:])
        self.maskAb = T(sb("maskAb", (128, 128), F32)[:])
        self.ucs = T(sb("ucs", (128, 128), F32)[:])
        self.bones = T(sb("bones", (128, 128), F32)[:])
        self.ci = T(sb("ci", (128, 2, 128), F32)[:])
        self.wconv = T(sb("wconv", (128, 48, 4), F32)[:])
        self.wonorm = T(sb("wonorm", (128, 1), F32)[:])
        self.nA = T(sb("nA", (128, 16), F32)[:])
        self.dtb = T(sb("dtb", (128, 16), F32)[:])
        self.eps_ln = T(sb("eps_ln", (128, 1), F32)[:])
        for t_, nm in [(self.ident_f, "ident"), (self.maskNb, "maskNb"), (self.maskAb, "maskAb"), (self.ucs, "ucs"),
                       (self.bones, "bones"), (self.ci, "ci"), (self.wconv, "wconv"), (self.wonorm, "wonorm"),
                       (self.dtb, "dtb"), (self.nA, "alog")]:
            C.dma("sp", t_.ap, d[nm], writes=[t_])
        C.cp("dve", self.ident_b.ap, self.ident_f.ap, [self.ident_f], [self.ident_b])
        C.op("dve", lambda e: e.memset(self.ones_f.ap, 1.0), [], [self.ones_f])
        C.op("dve", lambda e: e.memset(self.ones_b.ap, 1.0), [], [self.ones_b])
        C.act(self.nA.ap, self.nA.ap, AF.Exp, [self.nA], [self.nA])
        C.ts("dve", self.nA.ap, self.nA.ap, -1.0, None, ALU.mult, None, [self.nA], [self.nA])

        self.eps_rms = T(sb("eps_rms", (128, 1), F32)[:])
        C.op("dve", lambda e: e.memset(self.eps_rms.ap, RMS_EPS), [], [self.eps_rms])
        self.i16b = T(sb("i16b", (128, 16, 16), F32)[:])
        C.dma("sp", self.i16b.ap, d["i16b"], writes=[self.i16b])
        ps = lambda n, s, dt: es.enter_context(nc.psum_tensor(n, list(s), dt))
        self.bank = [T(ps("pb%d" % i, (128, 512), F32)[:]) for i in range(8)]
        for t_ in self.bank:
            t_.b.excl = True
        self.psbig = Ring(self.bank[0:5])
        self.pstr = Ring(self.bank[5:7])

        nbig = nc.sbuf_bytes_remaining - 256
        nbig = nbig // 64 * 64
        self.nbig = nbig
        self.big = sb("big", (128, nbig), U8)
        G = self.G = Arena(self.big, 0, nbig)
        self.ybT = G.t((128, 16, NOWN), BF16)
        self.baseB = G.p
        self.yaT = G.t((128, 16, NOWN), BF16)
        self.baseA = G.p
        self.topB = (nbig - 128 * 16 * 6 - 128) // 64 * 64
        GS = Arena(self.big, self.topB, nbig)
        self.S = GS.t((128, 16, 128), F32)
        self.Sb = GS.t((128, 16, 128), BF16)
        C.op("dve", lambda e: e.memset(self.S.ap, 0.0), [], [self.S])
        C.op("dve", lambda e: e.memset(self.Sb.ap, 0.0), [], [self.Sb])
        self.Sh = [T(self.S[:, h, :]) for h in range(16)]
        self.Sbh = [T(self.Sb[:, h, :]) for h in range(16)]
        for h in range(16):
            self.Sh[h].b.w = dict(self.S.b.w)
            self.Sbh[h].b.w = dict(self.Sb.b.w)

        if "P" in self.phases:
            self.mixerB(False)
        if "B" in self.phases:
            self.mixerB(True)
        elif "in_ybT" in d:
            self.load_dbg_bf(self.ybT, d["in_ybT"])
        if "A" in self.phases:
            self.mixerA()
        elif "in_yaT" in d:
            self.load_dbg_bf(self.yaT, d["in_yaT"])
        if "M" in self.phases:
            self.merge_tail()
        C.finish()
        es.close()

    def rsqrt(self, out, in_, eps, R, W, scale=1.0):
        C = self.C
        C.ts("dve", out, in_, scale, eps, ALU.mult, ALU.add, R, [W])
        C.act(out, out, AF.Sqrt, [W], [W])
        C.op("dve", lambda e: e.reciprocal(out=out, in_=out), [W], [W])

    def load_dbg_bf(self, t, src):
        self.C.dma("pool", t.ap, src.rearrange("(k p) n -> p k n", p=128), writes=[t])

    def dump_fm(self, t, name, ncols, at=None):
        C = self.C
        C.barrier()
        if at is None:
            at = self.nbig - 8 * 1024 - 64
        A = Arena(self.big, at, self.nbig)
        tmp = A.t((128, ncols), F32)
        dst = self.d[name].rearrange("(k p) n -> p k n", p=128)
        for k in range(16):
            C.cp("dve", tmp.ap, t[:, k, 0:ncols], [t], [tmp])
            C.dma("sp", dst[:, k, :], tmp.ap, reads=[tmp])

    def wload(self, ring, src2d, ncols, k=16):
        slot = ring.next()
        view = slot.ap[:, 0:k * ncols].rearrange("p (k n) -> p k n", k=k)
        self.C.dma("pool", view, src2d.rearrange("(k p) n -> p k n", p=128), writes=[slot])
        return T(view, slot.b)

    def mixerA(self):
        C, d = self.C, self.d
        C.barrier()
        A = Arena(self.big, self.baseA, self.nbig)
        xT = A.t((128, 16, NCO), BF16)
        C.dma("pool", xT.ap, d["xTo"].rearrange("(k p) n -> p k n", p=128), writes=[xT])
        wring = Ring([T(A.ap((128, 4096), BF16)) for _ in range(3)])
        va_p0 = A.p
        va = A.t((128, 9, 2048), BF16)
        va32 = A.t((16, 2048), F32)
        lng = A.t((128, 2048), F32)
        lnb = A.t((128, 2048), F32)
        wsm = A.t((128, 16, 128), BF16)
        bsr = A.t((1, 2048), F32)
        bs0 = A.t((1, 16, 16), F32)
        w00 = A.t((16, 16), F32)
        w00I = A.t((16, 16, 16), BF16)
        stats = A.t((128, 4, 6), F32)
        mv = A.t((128, 2), F32)
        rstd = A.t((128, 1), F32)
        C.dma("sp", lng.ap, d["lnv_g"], writes=[lng])
        C.dma("sp", lnb.ap, d["lnv_b"], writes=[lnb])
        C.dma("sp", bsr.ap, d["bs_row"], writes=[bsr])
        C.dma("sp", bs0.ap, d["bs0"], writes=[bs0])
        C.dma("sp", w00.ap, d["w00"], writes=[w00])
        A2 = Arena(self.big, va_p0, self.nbig)
        wtmp = T(A2.ap((128, 16, 128), F32), va.b)
        C.dma("sp", wtmp.ap, d["w_sT"], writes=[wtmp])
        tri = T(A2.ap((128, 128), F32), va.b)
        C.dma("sp", tri.ap, d["tri"], writes=[tri])
        C.tt("dve", wsm.ap, wtmp.ap, bc(tri.ap, [128, 16, 128], 1), ALU.mult, [wtmp, tri], [wsm])
        C.tt("dve", w00I.ap, bc(self.ident_f[0:16, 0:16], [16, 16, 16], 1), bc(w00.ap, [16, 16, 16], 2), ALU.mult,
             [self.ident_f, w00], [w00I])
        yaT = self.yaT
        blocks = [(3, 350), (350, 697), (697, 1043)]
        for g in range(8):
            wt = self.wload(wring, d["w_in"][:, OU + g * 256: OU + (g + 1) * 256], 256)
            for j in range(2):
                for (c0, c1) in blocks:
                    ps = self.psbig.next()
                    n = c1 - c0
                    for k in range(16):
                        C.mm(ps[:, 0:n], wt[:, k, j * 128:(j + 1) * 128], xT[:, k, c0:c1], [wt, xT], ps, k == 0, k == 15)
                    C.act(yaT[:, 2 * g + j, c0 - 3:c1 - 3], ps[:, 0:n], AF.Gelu, [ps], [yaT])
        for g in range(8):
            wt = self.wload(wring, d["w_in"][:, OVA + g * 256: OVA + (g + 1) * 256], 256)
            for t in range(9):
                nt = 128 if t < 8 else 16
                c0 = 3 + 128 * t
                ps = self.psbig.next()
                for k in range(16):
                    C.mm(ps[0:nt, 0:256], xT[:, k, c0:c0 + nt], wt[:, k, :], [wt, xT], ps, k == 0, k == 15)
                if t < 8:
                    C.act(va[:, t, g * 256:(g + 1) * 256], ps[:, 0:256], AF.Gelu, [ps], [va])
                else:
                    C.act(va32[:, g * 256:(g + 1) * 256], ps[0:16, 0:256], AF.Gelu, [ps], [va32])
        tmpn = T(A.ap((128, 512), F32))
        for t in range(9):
            nt = 128 if t < 8 else 16
            src = va[0:nt, t, :] if t < 8 else va32.ap
            srcT = va if t < 8 else va32
            for q in range(4):
                C.op("dve", lambda e: e.bn_stats(out=stats[0:nt, q, :], in_=src[:, q * 512:(q + 1) * 512]), [srcT], [stats])
            C.op("dve", lambda e: e.bn_aggr(out=mv[0:nt, :], in_=stats[0:nt, :, :]), [stats], [mv])
            self.rsqrt(rstd[0:nt, :], mv[0:nt, 1:2], LN_EPS, [mv], rstd)
            for q in range(4):
                qs = slice(q * 512, (q + 1) * 512)
                C.ts("dve", tmpn[0:nt, :], src[:, qs], mv[0:nt, 0:1], rstd[0:nt, 0:1], ALU.subtract, ALU.mult,
                     [srcT, mv, rstd], [tmpn])
                C.tt("dve", tmpn[0:nt, :], tmpn[0:nt, :], lng[0:nt, qs], ALU.mult, [tmpn, lng], [tmpn])
                if t < 8:
                    C.tt("dve", va[:, t, qs], tmpn.ap, lnb[:, qs], ALU.add, [tmpn, lnb], [va])
                else:
                    C.tt("dve", va32[:, qs], tmpn[0:16, :], lnb[0:16, qs], ALU.add, [tmpn, lnb], [va32])
            if t == 8:
                C.dma("sp", d["vs"], va32.ap, reads=[va32])
                C.cp("dve", va[0:16, 8, :], va32.ap, [va32], [va])
        for t in range(9):
            for g4 in range(4):
                ps = self.psbig.next()
                nt = 128 if t < 8 else 16
                for gi in range(4):
                    g = g4 * 4 + gi
                    if t < 8:
                        C.mm(ps[:, gi * 128:(gi + 1) * 128], va[:, t, g * 128:(g + 1) * 128], wsm[:, g, :], [va, wsm], ps, True, False)
                        C.mm(ps[:, gi * 128:(gi + 1) * 128], self.ones_f[0:1, :], bsr[0:1, g * 128:(g + 1) * 128],
                             [self.ones_f, bsr], ps, False, True)
                    else:
                        C.mm(ps[:, gi * 128:gi * 128 + 16], va[0:16, 8, g * 128:(g + 1) * 128], w00I[:, g, :], [va, w00I], ps, True, False)
                        C.mm(ps[:, gi * 128:gi * 128 + 16], self.ones_f[0:1, :], bs0[0:1, g, :], [self.ones_f, bs0], ps, False, True)
                dst = yaT[:, g4 * 4:g4 * 4 + 4, t * 128:t * 128 + nt]
                src = ps.ap.rearrange("p (g n) -> p g n", g=4)[:, :, 0:nt]
                C.tt("dve", dst, dst, src, ALU.mult, [yaT, ps], [yaT])
        if "yaT" in d:
            self.dump_fm(self.yaT, "yaT", NOWN)
        C.barrier()

    def layernorm_rows(self, A, x, nt, g, b, dst_fn, stats, mv, rstd, tmp_ring):
        C = self.C
        for q in range(4):
            C.op("dve", lambda e: e.bn_stats(out=stats[0:nt, q, :], in_=x.ap[0:nt, q * 512:(q + 1) * 512]), [x], [stats])
        C.op("dve", lambda e: e.bn_aggr(out=mv[0:nt, :], in_=stats[0:nt, :, :]), [stats], [mv])
        self.rsqrt(rstd[0:nt, :], mv[0:nt, 1:2], LN_EPS, [mv], rstd)
        for q in range(4):
            qs = slice(q * 512, (q + 1) * 512)
            tmp = tmp_ring.next()
            C.ts("dve", tmp[0:nt, :], x.ap[0:nt, qs], mv[0:nt, 0:1], rstd[0:nt, 0:1], ALU.subtract, ALU.mult,
                 [x, mv, rstd], [tmp])
            C.tt("dve", tmp[0:nt, :], tmp[0:nt, :], g[0:nt, qs], ALU.mult, [tmp, g], [tmp])
            dst_fn(q, tmp)

    def merge_tail(self):
        C, d = self.C, self.d
        C.barrier()
        mT_lo = (self.nbig - 16 * NOWN * 2) // 64 * 64
        mT = T(Arena(self.big, mT_lo, self.nbig).ap((128, 16, NOWN), BF16))
        A = Arena(self.big, self.baseA, mT_lo)
        xT = A.t((128, 16, NCO), BF16)
        C.dma("pool", xT.ap, d["xTo"].rearrange("(k p) n -> p k n", p=128), writes=[xT])
        wring = Ring([T(A.ap((128, 4096), BF16)) for _ in range(4)])
        sgr = A.ring(3, (128, 512), F32)
        t1 = A.t((128, 2, NOWN), F32)
        yaT, ybT = self.yaT, self.ybT
        blocks = [(0, 347), (347, 694), (694, NOWN)]
        for g in range(8):
            for br, (wsrc, gsrc, yT) in enumerate(((d["w_pa"], OGA, yaT), (d["w_pb"], OGB, ybT))):
                wy = self.wload(wring, wsrc[:, g * 256:(g + 1) * 256], 256)
                wg_ = self.wload(wring, d["w_in"][:, gsrc + g * 256:gsrc + (g + 1) * 256], 256)
                for j in range(2):
                    f = 2 * g + j
                    js = slice(j * 128, (j + 1) * 128)
                    for (c0, c1) in blocks:
                        n = c1 - c0
                        psg = self.psbig.next()
                        for k in range(16):
                            C.mm(psg[:, 0:n], wg_[:, k, js], xT[:, k, 3 + c0:3 + c1], [wg_, xT], psg, k == 0, k == 15)
                        sg = sgr.next()
                        C.act(sg[:, 0:n], psg[:, 0:n], AF.Sigmoid, [psg], [sg])
                        psy = self.psbig.next()
                        for k in range(16):
                            C.mm(psy[:, 0:n], wy[:, k, js], yT[:, k, c0:c1], [wy, yT], psy, k == 0, k == 15)
                        if br == 0:
                            C.tt("dve", t1[:, j, c0:c1], psy[:, 0:n], sg[:, 0:n], ALU.mult, [psy, sg], [t1])
                        else:
                            C.tt("dve", sg[:, 0:n], psy[:, 0:n], sg[:, 0:n], ALU.mult, [psy, sg], [sg])
                            C.tt("dve", mT[:, f, c0:c1], t1[:, j, c0:c1], sg[:, 0:n], ALU.add, [t1, sg], [mT])
        if "mT" in d:
            self.dump_fm(mT, "mT", NOWN, at=self.baseA)
        C.barrier()
        A = Arena(self.big, 0, mT_lo)
        x1 = A.t((128, 9, 2048), F32)
        base2 = A.p
        wring = Ring([T(A.ap((128, 4096), BF16)) for _ in range(4)])
        xtr = A.ring(3, (128, 512), F32)
        lng = A.t((128, 2048), F32)
        lnb = A.t((128, 2048), F32)
        x1b = A.ring(2, (128, 2048), BF16)
        tmpr = A.ring(2, (128, 512), F32)
        stats = A.t((128, 4, 6), F32)
        mv = A.t((128, 2), F32)
        rstd = A.t((128, 1), F32)
        C.dma("sp", lng.ap, d["ln1_g"], writes=[lng])
        C.dma("sp", lnb.ap, d["ln1_b"], writes=[lnb])
        for n in range(4):
            ns = slice(n * 512, (n + 1) * 512)
            wk = [self.wload(wring, d["w_o"][kh * 1024:(kh + 1) * 1024, ns], 512, k=8) for kh in range(2)]
            for t in range(9):
                nt = 128 if t < 8 else 16
                xt = xtr.next()
                C.dma("sp", xt[0:nt, :], d["xtok"][t * 128:t * 128 + nt, ns], writes=[xt])
                ps = self.psbig.next()
                for k in range(16):
                    C.mm(ps[0:nt, :], mT[:, k, t * 128:t * 128 + nt], wk[k // 8][:, k % 8, :], [mT, wk[k // 8]], ps,
                         k == 0, k == 15)
                C.stt("dve", x1[0:nt, t, ns], xt[0:nt, :], ALPHA, ps[0:nt, :], ALU.mult, ALU.add, [xt, ps], [x1])
        x1T = T(mT.ap, mT.b)
        for t in range(9):
            nt = 128 if t < 8 else 16
            xb = x1b.next()
            x1row = T(x1[:, t, :], x1.b)

            def put(q, tmp, t=t, nt=nt, xb=xb):
                qs = slice(q * 512, (q + 1) * 512)
                C.tt("dve", x1[0:nt, t, qs], tmp[0:nt, :], lnb[0:nt, qs], ALU.add, [tmp, lnb], [x1])
                C.cp("act", xb[0:nt, qs], x1[0:nt, t, qs], [x1], [xb])
            self.layernorm_rows(A, x1row, nt, lng, lnb, put, stats, mv, rstd, tmpr)
            for f8 in range(2):
                pb = self.pstr.next()
                pv = pb.ap.bitcast(BF16).rearrange("p (f n) -> p f n", f=8)
                for i in range(8):
                    f = f8 * 8 + i
                    C.tr(pv[:, i, 0:nt], xb[0:nt, f * 128:(f + 1) * 128], self.ident_b[0:nt, 0:nt], [xb, self.ident_b], pb)
                C.cp("act", x1T[:, f8 * 8:f8 * 8 + 8, t * 128:t * 128 + nt], pv[:, :, 0:nt], [pb], [x1T])
        if "x1" in d:
            for t in range(9):
                nt = 128 if t < 8 else 16
                C.dma("sp", d["x1"][t * 128:t * 128 + nt, :], x1[0:nt, t, :], reads=[x1])
        C.barrier()
        A = Arena(self.big, base2, mT_lo)
        wring = Ring([T(A.ap((128, 4096), BF16)) for _ in range(4)])
        hT = A.t((128, 8, NOWN), BF16)
        rtmp = A.ring(3, (128, 512), F32)
        lng = A.t((128, 2048), F32)
        lnb = A.t((128, 2048), F32)
        outr = A.ring(3, (128, 512), F32)
        tmpr = A.ring(2, (128, 512), F32)
        stats = A.t((128, 4, 6), F32)
        mv = A.t((128, 2), F32)
        rstd = A.t((128, 1), F32)
        C.dma("sp", lng.ap, d["ln2_g"], writes=[lng])
        C.dma("sp", lnb.ap, d["ln2_b"], writes=[lnb])
        for e8 in range(8):
            for q in range(4):
                wt = self.wload(wring, d["w_up"][:, e8 * 1024 + q * 256:e8 * 1024 + (q + 1) * 256], 256)
                for j in range(2):
                    for (c0, c1) in blocks:
                        n = c1 - c0
                        ps = self.psbig.next()
                        for k in range(16):
                            C.mm(ps[:, 0:n], wt[:, k, j * 128:(j + 1) * 128], x1T[:, k, c0:c1], [wt, x1T], ps, k == 0, k == 15)
                        r = rtmp.next()
                        C.act(r[:, 0:n], ps[:, 0:n], AF.Relu, [ps], [r])
                        C.tt("dve", hT[:, q * 2 + j, c0:c1], r[:, 0:n], r[:, 0:n], ALU.mult, [r], [hT])
            for n in range(4):
                ns = slice(n * 512, (n + 1) * 512)
                wd = self.wload(wring, d["w_dn"][e8 * 1024:(e8 + 1) * 1024, ns], 512, k=8)
                for t in range(9):
                    nt = 128 if t < 8 else 16
                    ps = self.psbig.next()
                    for k in range(8):
                        C.mm(ps[0:nt, :], hT[:, k, t * 128:t * 128 + nt], wd[:, k, :], [hT, wd], ps, k == 0, k == 7)
                    if e8 == 0:
                        C.stt("dve", x1[0:nt, t, ns], x1[0:nt, t, ns], ALPHA, ps[0:nt, :], ALU.mult, ALU.add, [x1, ps], [x1])
                    else:
                        C.tt("dve", x1[0:nt, t, ns], x1[0:nt, t, ns], ps[0:nt, :], ALU.add, [x1, ps], [x1])
        for t in range(9):
            nt = 128 if t < 8 else 16
            x1row = T(x1[:, t, :], x1.b)

            def put2(q, tmp, t=t, nt=nt):
                qs = slice(q * 512, (q + 1) * 512)
                o = outr.next()
                C.tt("dve", o[0:nt, :], tmp[0:nt, :], lnb[0:nt, qs], ALU.add, [tmp, lnb], [o])
                C.dma("sp", d["y_out"][t * 128:t * 128 + nt, qs], o[0:nt, :], reads=[o])
            self.layernorm_rows(A, x1row, nt, lng, lnb, put2, stats, mv, rstd, tmpr)
        C.barrier()

    def sample_phase(self, side, lo, betas, egs, glbs):
        C, d = self.C, self.d
        C.barrier()
        A = Arena(self.big, lo, self.topB)
        ident_f, ident_b = self.ident_f, self.ident_b
        NW = 4

        class Slot:
            pass
        slots = []
        for i in range(NW):
            sl = Slot()
            sl.bank = Ring([self.bank[2 * i], self.bank[2 * i + 1]])
            sl.Ss = A.t((128, 16, 128), F32)
            sl.Sn = A.t((128, 16, 128), F32)
            sl.km = A.t((128, 16, 16), F32)
            sl.qm = A.t((128, 16, 16), F32)
            for nm in ("ktok", "qtok", "vtok", "kS", "qS", "vn", "os", "junk"):
                setattr(sl, nm, A.t((16, 128), F32))
            sl.vm = A.ring(4, (16, 128), F32)
            sl.qk = A.t((16, 1), F32)
            sl.ss = A.t((16, 1), F32)
            sl.on = A.t((16, 128), BF16)
            slots.append(sl)

        def head(h, sl):
            C.dma("sp", sl.Ss.ap, d["ssm"][:, h, :, :].rearrange("s d e -> d s e"), writes=[sl.Ss])
            bT = sl.bank.next()
            bTb = bT.ap.bitcast(BF16)
            C.tr(bTb[0:16, 0:128], side["knT"][:, h, :], ident_b.ap, [side["knT"], ident_b], bT)
            C.tr(bTb[0:16, 128:256], side["qnT"][:, h, :], ident_b.ap, [side["qnT"], ident_b], bT)
            C.tr(bTb[0:16, 256:384], side["vT"][:, h, :], ident_b.ap, [side["vT"], ident_b], bT)
            C.tt("dve", sl.km.ap, bc(side["knT"][:, h, :], [128, 16, 16], 1), self.i16b.ap, ALU.mult, [side["knT"], self.i16b], [sl.km])
            C.tt("dve", sl.qm.ap, bc(side["qnT"][:, h, :], [128, 16, 16], 1), self.i16b.ap, ALU.mult, [side["qnT"], self.i16b], [sl.qm])
            yield
            C.cp("act", sl.ktok.ap, bTb[0:16, 0:128], [bT], [sl.ktok])
            C.cp("act", sl.qtok.ap, bTb[0:16, 128:256], [bT], [sl.qtok])
            C.cp("act", sl.vtok.ap, bTb[0:16, 256:384], [bT], [sl.vtok])
            C.tt("dve", sl.junk.ap, sl.qtok.ap, sl.ktok.ap, ALU.mult, [sl.qtok, sl.ktok], [sl.junk])
            C.op("dve", lambda e: e.reduce_sum(out=sl.qk.ap, in_=sl.junk.ap, axis=AX.X), [sl.junk], [sl.qk])
            yield
            bP = sl.bank.next()
            for s_ in range(16):
                C.mm(bP[0:16, 0:128], sl.km[:, s_, :], sl.Ss[:, s_, :], [sl.km, sl.Ss], bP, s_ == 0, s_ == 15)
            for s_ in range(16):
                C.mm(bP[0:16, 128:256], sl.qm[:, s_, :], sl.Ss[:, s_, :], [sl.qm, sl.Ss], bP, s_ == 0, s_ == 15)
            yield
            C.stt("dve", sl.vn.ap, bP[0:16, 0:128], egs[:, h:h + 1], sl.vtok.ap, ALU.mult, ALU.subtract, [bP, egs, sl.vtok], [sl.vn])
            C.ts("dve", sl.vn.ap, sl.vn.ap, betas[:, h:h + 1], -1.0, ALU.mult, ALU.mult, [sl.vn, betas], [sl.vn])
            C.act(sl.os.ap, bP[0:16, 128:256], AF.Copy, [bP, egs], [sl.os], scale=egs[:, h:h + 1])
            C.stt("dve", sl.os.ap, sl.vn.ap, sl.qk[:, 0:1], sl.os.ap, ALU.mult, ALU.add, [sl.vn, sl.qk, sl.os], [sl.os])
            yield
            for s4 in range(4):
                vms = []
                for s_ in range(4):
                    sg = 4 * s4 + s_
                    vm = sl.vm.next()
                    C.ts("dve", vm.ap, sl.vn.ap, ident_f[0:16, sg:sg + 1], None, ALU.mult, None, [sl.vn, ident_f], [vm])
                    vms.append(vm)
                yield
                bU = sl.bank.next()
                for s_ in range(4):
                    C.mm(bU[:, s_ * 128:(s_ + 1) * 128], sl.ktok.ap, vms[s_].ap, [sl.ktok, vms[s_]], bU)
                yield
                for s_ in range(4):
                    sg = 4 * s4 + s_
                    C.stt("dve", sl.Sn[:, sg, :], sl.Ss[:, sg, :], glbs[:, sg, h:h + 1], bU[:, s_ * 128:(s_ + 1) * 128],
                          ALU.mult, ALU.add, [sl.Ss, glbs, bU], [sl.Sn])
                yield
            C.dma("sp", d["ssm_s"][:, h, :, :].rearrange("s d e -> d s e"), sl.Sn.ap, reads=[sl.Sn])
            C.act(sl.junk.ap, sl.os.ap, AF.Square, [sl.os], [sl.junk, sl.ss], accum_out=sl.ss[:, 0:1])
            self.rsqrt_act(sl.ss.ap, sl.ss.ap, [sl.ss], sl.ss, scale=1.0 / 128.0)
            C.act(sl.on.ap, sl.os.ap, AF.Copy, [sl.os, sl.ss], [sl.on], scale=sl.ss[:, 0:1])
            yield
            bO = sl.bank.next()
            bOb = bO.ap.bitcast(BF16)
            C.tr(bOb[:, 0:16], sl.on.ap, ident_b[0:16, 0:16], [sl.on, ident_b], bO)
            yield
            C.stt("dve", self.ybT[:, h, 1024:1040], bOb[:, 0:16], self.wonorm[:, 0:1], side["szT"][:, h, :], ALU.mult, ALU.mult,
                  [bO, self.wonorm, side["szT"]], [self.ybT])

        active = []
        hs = iter(range(16))
        free = list(range(NW))
        while True:
            while free:
                h = next(hs, None)
                if h is None:
                    break
                i = free.pop(0)
                active.append((i, head(h, slots[i])))
            if not active:
                break
            for it_ in list(active):
                try:
                    next(it_[1])
                except StopIteration:
                    active.remove(it_)
                    free.append(it_[0])

    def rsqrt_act(self, out, in_, R, W, scale=1.0):
        C = self.C
        n = out.shape[0]
        C.act(out, in_, AF.Ln, list(R) + [self.eps_rms], [W], bias=self.eps_rms[0:n, 0:1], scale=scale)
        C.act(out, out, AF.Exp, [W], [W], scale=-0.5)

    def mixerB(self, own):
        C, d = self.C, self.d
        C.barrier()
        NC_ = NCO if own else NCP
        NTK = NOWN if own else 1024
        A = Arena(self.big, self.baseB, self.topB)
        if own:
            side = {nm: A.t((128, 16, 16), BF16) for nm in ("knT", "qnT", "vT", "szT")}
            betas = A.t((16, 16), F32)
            egs = A.t((16, 16), F32)
            glbs = A.t((128, 16, 16), F32)
            side_end = A.p
        xT = A.t((128, 16, NC_), BF16)
        C.dma("pool", xT.ap, d["xTo" if own else "xTp"].rearrange("(k p) n -> p k n", p=128), writes=[xT])
        wring = Ring([T(A.ap((128, 4096), BF16)) for _ in range(3)])
        psbig = Ring(self.bank[0:2])
        psc = Ring(self.bank[2:8])
        blocks = [(0, 348), (348, 696), (696, NC_)] if own else [(0, 343), (343, 686), (686, NC_)]
        tblocks = [(0, 347), (347, 694), (694, NTK)] if own else [(0, 512), (512, 1024)]
        ident_f, ident_b, ones_f, ones_b = self.ident_f, self.ident_b, self.ones_f, self.ones_b
        Atmp = Arena(self.big, self.topB - 4096, self.topB)
        wba = Atmp.t((128, 16, 32), BF16)
        C.dma("pool", wba.ap, d["w_in"][:, OBETA:OBETA + 32].rearrange("(k p) n -> p k n", p=128), writes=[wba])
        beta = A.t((128, 8, 16), F32)
        g = Atmp.t((128, 8, 16), F32)
        gc = A.t((128, 8, 16), F32)
        eg = A.t((128, 8, 16), F32)
        ekd = A.t((128, 8, 16), F32)
        bg = A.t((128, 8, 16), F32)
        glb = A.t((128, 2, 8, 16), F32)
        for t in range(8):
            ps = psbig.next()
            for k in range(16):
                C.mm(ps[:, 0:32], xT[:, k, 3 + 128 * t:3 + 128 * t + 128], wba[:, k, :], [xT, wba], ps, k == 0, k == 15)
            C.act(beta[:, t, :], ps[:, 0:16], AF.Sigmoid, [ps], [beta])
            C.tt("dve", g[:, t, :], ps[:, 16:32], self.dtb.ap, ALU.add, [ps, self.dtb], [g])
        C.act(g.ap, g.ap, AF.Exp, [g], [g])
        C.act(g.ap, g.ap, AF.Ln, [g], [g], bias=1.0)
        C.tt("dve", g.ap, g.ap, bc(self.nA.ap, [128, 8, 16], 1), ALU.mult, [g, self.nA], [g])
        g2 = g.ap.rearrange("p t h -> p (t h)")
        ps = psc.next()
        C.mm(ps[:, 0:128], self.ucs.ap, g2, [self.ucs, g], ps)
        C.mm(ps[:, 128:256], self.bones.ap, g2, [self.bones, g], ps)
        C.mm(ps[:, 256:384], self.ci[:, 0, :], g2, [self.ci, g], ps)
        C.mm(ps[:, 384:512], self.ci[:, 1, :], g2, [self.ci, g], ps)
        C.cp("act", gc.ap.rearrange("p t h -> p (t h)"), ps[:, 0:128], [ps], [gc])
        C.tt("dve", ekd.ap.rearrange("p t h -> p (t h)"), ps[:, 128:256], gc.ap.rearrange("p t h -> p (t h)"), ALU.subtract,
             [ps, gc], [ekd])
        C.act(ekd.ap, ekd.ap, AF.Exp, [ekd], [ekd])
        C.act(eg.ap, gc.ap, AF.Exp, [gc], [eg])
        C.act(glb.ap.rearrange("p c t h -> p (c t h)"), ps[:, 256:512], AF.Exp, [ps], [glb])
        C.tt("dve", bg.ap, beta.ap, eg.ap, ALU.mult, [beta, eg], [bg])
        if own:
            gs = Atmp.t((16, 16), F32)
            dgs = Atmp.t((16, 16, 16), F32)
            ps = psbig.next()
            for k in range(16):
                C.mm(ps[0:16, 0:32], xT[:, k, 1027:1043], wba[:, k, :], [xT, wba], ps, k == 0, k == 15)
            C.act(betas.ap, ps[0:16, 0:16], AF.Sigmoid, [ps], [betas])
            C.tt("dve", gs.ap, ps[0:16, 16:32], self.dtb[0:16, :], ALU.add, [ps, self.dtb], [gs])
            C.act(gs.ap, gs.ap, AF.Exp, [gs], [gs])
            C.act(gs.ap, gs.ap, AF.Ln, [gs], [gs], bias=1.0)
            C.tt("dve", gs.ap, gs.ap, self.nA[0:16, :], ALU.mult, [gs, self.nA], [gs])
            C.act(egs.ap, gs.ap, AF.Exp, [gs], [egs])
            C.tt("dve", dgs.ap, bc(ident_f[0:16, 0:16], [16, 16, 16], 2), bc(gs.ap, [16, 16, 16], 1), ALU.mult,
                 [ident_f, gs], [dgs])
            ps = psc.next()
            C.mm(ps[:, 0:256], ones_f[0:16, :], dgs.ap.rearrange("p s h -> p (s h)"), [ones_f, dgs], ps)
            C.act(glbs.ap.rearrange("p s h -> p (s h)"), ps[:, 0:256], AF.Exp, [ps], [glbs])
        C.barrier()
        ngc = A.t((128, 8, 16), F32)
        C.ts("dve", ngc.ap, gc.ap, -1.0, None, ALU.mult, None, [gc], [ngc])
        rawr = A.ring(2, (128, NC_), F32)
        accr = A.ring(2, (128, NTK), F32)
        TB = 348 if own else 512
        sqr = A.ring(2, (128, TB), BF16)
        rsr = A.ring(2, (128, TB), F32)

        class GT:
            pass
        gts = []
        for i in range(2):
            gt_ = GT()
            gt_.knT = A.t((128, 2, NTK), BF16)
            gt_.vT = A.t((128, 2, NTK), BF16)
            if own:
                gt_.qnT = A.t((128, 2, NTK), BF16)
                gt_.szT = A.t((128, 2, NTK), BF16)
            gts.append(gt_)
        if own:
            sctg = [A.t((128, 3, 2, 3, 16), F32) for _ in range(2)]
            ncsr = A.ring(2, (128, 3, 16), F32)
        NSLOT = int(os.environ.get("KB_NSLOT", "3"))

        class Slot:
            pass
        slots = []
        for i in range(NSLOT):
            sl = Slot()
            for nm in ["dg"] + (["EA"] if own else []):
                setattr(sl, nm, A.t((128, 2, 128), F32))
            sl.EN = sl.dg
            sl.bank = Ring([self.bank[2 + 2 * i], self.bank[3 + 2 * i]])
            sl.AB = [A.t((128, 4, 128), F32) for _ in range(2)]
            sl.P = [A.t((128, 2, 128), F32) for _ in range(2)]
            sl.w = sl.dg
            sl.N0 = sl.P[1]
            if own:
                sl.o1 = sl.EA
            for nm in ["kbg", "kd", "vb", "R", "kcd", "vn"] + (["attnT", "on"] if own else []):
                setattr(sl, nm, A.t((128, 2, 128), BF16))
            C.op("dve", lambda e: e.memset(sl.vn.ap, 0.0), [], [sl.vn])
            sl.ss = A.t((128, 2), F32)
            slots.append(sl)
        junk = A.t((128, 128), F32)
        print("mixerB arena used", A.p - A.lo, "free", A.hi - A.p)
        v3 = lambda ap, a=2: ap.rearrange("p (a b) -> p a b", a=a)

        def run_rr(gens, width):
            active = []
            it = iter(gens)
            while True:
                while len(active) < width:
                    g_ = next(it, None)
                    if g_ is None:
                        break
                    active.append(g_)
                if not active:
                    break
                for g_ in list(active):
                    try:
                        next(g_)
                    except StopIteration:
                        active.remove(g_)

        def qkv_chunk(kind, hh, getw, h0, gt):
            wt = getw()
            cidx = {"q": 0, "k": 16, "v": 32}[kind] + h0 + hh
            raw = rawr.next()
            for (c0, c1) in blocks:
                n = c1 - c0
                ps = psbig.next()
                for k in range(16):
                    C.mm(ps[:, 0:n], wt[:, k, hh * 128:(hh + 1) * 128], xT[:, k, c0:c1], [wt, xT], ps, k == 0, k == 15)
                C.cp("act", raw[:, c0:c1], ps[:, 0:n], [ps], [raw])
                yield
            acc = accr.next()
            wc = self.wconv
            C.act(acc[:, 0:1024], raw[:, 0:1024], AF.Copy, [raw, wc], [acc], scale=wc[:, cidx, 0:1])
            for j in range(1, 4):
                C.stt("dve", acc[:, 0:1024], raw[:, j:j + 1024], wc[:, cidx, j:j + 1], acc[:, 0:1024], ALU.mult, ALU.add,
                      [raw, wc, acc], [acc])
            if own:
                sctT = sctg[(h0 // 2) % 2]
                sct = T(sctT[:, {"q": 0, "k": 1, "v": 2}[kind], hh], sctT.b)
                C.ts("dve", acc[:, 1024:1040], raw[:, 1027:1043], wc[:, cidx, 3:4], None, ALU.mult, None, [raw, wc], [acc])
                for j in range(3):
                    C.stt("dve", acc[:, 1024:1040], sct[:, j, :], wc[:, cidx, j:j + 1], acc[:, 1024:1040], ALU.mult, ALU.add,
                          [sct, wc, acc], [acc])
                ncs = ncsr.next()
                C.cp("dve", ncs[:, 0:2, :], sct[:, 1:3, :], [sct], [ncs])
                C.cp("dve", ncs[:, 2, :], raw[:, 1027:1043], [raw], [ncs])
                C.dma("sp", d["ncs"][:, cidx], ncs.ap, reads=[ncs])
                C.dma("sp", d["ncp"][:, cidx, :], raw[:, 1024:1027], reads=[raw])
            yield
            if kind == "v":
                C.act(gt.vT[:, hh, :], acc.ap, AF.Silu, [acc], [gt.vT])
                return
            C.act(acc.ap, acc.ap, AF.Silu, [acc], [acc])
            yield
            dstT = gt.knT if kind == "k" else gt.qnT
            for (c0, c1) in tblocks:
                n = c1 - c0
                sq = sqr.next()
                C.act(sq[:, 0:n], acc[:, c0:c1], AF.Square, [acc], [sq])
                ps = psbig.next()
                C.mm(ps[:, 0:n], ones_b.ap, sq[:, 0:n], [ones_b, sq], ps)
                yield
                rs = rsr.next()
                self.rsqrt_act(rs[:, 0:n], ps[:, 0:n], [ps], rs)
                yield
                if kind == "q":
                    C.stt("dve", dstT[:, hh, c0:c1], acc[:, c0:c1], 128.0 ** -0.5, rs[:, 0:n], ALU.mult, ALU.mult, [acc, rs], [dstT])
                else:
                    C.tt("dve", dstT[:, hh, c0:c1], acc[:, c0:c1], rs[:, 0:n], ALU.mult, [acc, rs], [dstT])

        def z_chunk(hh, getw, gt):
            wz = getw()
            for (c0, c1) in tblocks:
                n = c1 - c0
                ps = psbig.next()
                for k in range(16):
                    C.mm(ps[:, 0:n], wz[:, k, hh * 128:(hh + 1) * 128], xT[:, k, 3 + c0:3 + c1], [wz, xT], ps, k == 0, k == 15)
                C.act(gt.szT[:, hh, c0:c1], ps[:, 0:n], AF.Silu, [ps], [gt.szT])
                yield

        TS = int(os.environ.get("KB_TS", "9"))
        B3 = [128, 2, 128]

        def tile_scan(t, h0, sl, gt):
            cols = slice(128 * t, 128 * t + 128)
            hs = slice(h0, h0 + 2)
            gch = gc[:, t, hs]
            betab = bc(beta[:, t, hs], B3, 2)
            C.tt("dve", sl.dg.ap, bc(ident_f.ap, B3, 1), bc(gch, B3, 2), ALU.mult, [ident_f, gc], [sl.dg])
            yield
            bk = sl.bank.next()
            C.mm(bk[:, 0:256], ones_f.ap, sl.dg.ap.rearrange("p a b -> p (a b)"), [ones_f, sl.dg], bk)
            yield
            C.tt("dve", sl.EN.ap, v3(bk[:, 0:256]), bc(self.maskNb.ap, B3, 1), ALU.add, [bk, self.maskNb], [sl.EN])
            if own:
                C.tt("dve", sl.EA.ap, v3(bk[:, 0:256]), bc(self.maskAb.ap, B3, 1), ALU.add, [bk, self.maskAb], [sl.EA])
            for hh in range(2):
                C.act(sl.EN[:, hh, :], sl.EN[:, hh, :], AF.Exp, [sl.EN, gc], [sl.EN], scale=-1.0, bias=gc[:, t, h0 + hh:h0 + hh + 1])
                if own:
                    C.act(sl.EA[:, hh, :], sl.EA[:, hh, :], AF.Exp, [sl.EA, ngc], [sl.EA], bias=ngc[:, t, h0 + hh:h0 + hh + 1])
            C.tt("dve", sl.EN.ap, sl.EN.ap, betab, ALU.mult, [sl.EN, beta], [sl.EN])
            yield
            b1 = sl.bank.next()
            for hh in range(2):
                kt = gt.knT[:, hh, cols]
                C.mm(b1[:, hh * 128:(hh + 1) * 128], kt, kt, [gt.knT], b1)
            if own:
                for hh in range(2):
                    C.mm(b1[:, 256 + hh * 128:256 + (hh + 1) * 128], gt.knT[:, hh, cols], gt.qnT[:, hh, cols], [gt.knT, gt.qnT], b1)
            b2 = sl.bank.next()
            b2b = b2.ap.bitcast(BF16)
            for hh in range(2):
                C.tr(b2b[:, hh * 128:(hh + 1) * 128], gt.knT[:, hh, cols], ident_b.ap, [gt.knT, ident_b], b2)
            for hh in range(2):
                C.tr(b2b[:, 256 + hh * 128:256 + (hh + 1) * 128], gt.vT[:, hh, cols], ident_b.ap, [gt.vT, ident_b], b2)
            yield
            C.tt("dve", sl.N0.ap, v3(b1[:, 0:256]), sl.EN.ap, ALU.mult, [b1, sl.EN], [sl.N0])
            if own:
                C.tt("dve", sl.attnT.ap, v3(b1[:, 256:512]), sl.EA.ap, ALU.mult, [b1, sl.EA], [sl.attnT])
            C.tt("dve", sl.kbg.ap, v3(b2b[:, 0:256]), bc(bg[:, t, hs], B3, 2), ALU.mult, [b2, bg], [sl.kbg])
            C.tt("dve", sl.kd.ap, v3(b2b[:, 0:256]), bc(ekd[:, t, hs], B3, 2), ALU.mult, [b2, ekd], [sl.kd])
            C.tt("dve", sl.vb.ap, v3(b2b[:, 256:512]), betab, ALU.mult, [b2, beta], [sl.vb])
            yield
            bI = sl.bank.next()
            for hh in range(2):
                C.tr(bI[:, hh * 128:(hh + 1) * 128], sl.N0[:, hh, :], ident_f.ap, [sl.N0, ident_f], bI)
            yield
            AB = sl.AB[0]
            C.cp("act", AB[:, 0:2, :], v3(bI[:, 0:256]), [bI], [AB])
            C.cp("act", AB[:, 2:4, :], sl.N0.ap, [sl.N0], [AB])
            P = sl.P[0]
            C.tt("dve", P.ap, bc(ident_f.ap, B3, 1), v3(bI[:, 0:256]), ALU.subtract, [ident_f, bI], [P])
            yield
            for l in range(1, 7):
                bX = sl.bank.next()
                bY = sl.bank.next() if l >= 2 else None
                for hh in range(2):
                    Bk, Ak = AB[:, hh, :], AB[:, 2 + hh, :]
                    if l <= 4:
                        C.mm(bX[:, hh * 128:(hh + 1) * 128], Ak, Bk, [AB], bX)
                    if l <= 5:
                        C.mm(bX[:, 256 + hh * 128:256 + (hh + 1) * 128], Bk, Ak, [AB], bX)
                    if l >= 2:
                        C.mm(bY[:, hh * 128:(hh + 1) * 128], Ak, P[:, hh, :], [AB, P], bY)
                yield
                if l <= 5:
                    AB2 = sl.AB[l % 2]
                    if l <= 4:
                        C.cp("act", AB2.ap, v3(bX.ap, 4), [bX], [AB2])
                    else:
                        C.cp("act", AB2[:, 2:4, :], v3(bX[:, 256:512]), [bX], [AB2])
                if l >= 2:
                    if l == 6:
                        C.tt("dve", sl.R.ap, P.ap, v3(bY[:, 0:256]), ALU.add, [P, bY], [sl.R])
                    else:
                        P2 = sl.P[(l + 1) % 2]
                        C.tt("dve", P2.ap, P.ap, v3(bY[:, 0:256]), ALU.add, [P, bY], [P2])
                        P = P2
                if l <= 5:
                    AB = AB2
                yield
            bW = sl.bank.next()
            for hh in range(2):
                C.mm(bW[:, hh * 128:(hh + 1) * 128], sl.R[:, hh, :], sl.vb[:, hh, :], [sl.R, sl.vb], bW)
            for hh in range(2):
                C.mm(bW[:, 256 + hh * 128:256 + (hh + 1) * 128], sl.kbg[:, hh, :], sl.R[:, hh, :], [sl.R, sl.kbg], bW)
            yield
            C.cp("act", sl.w.ap, v3(bW[:, 0:256]), [bW], [sl.w])
            C.cp("dve", sl.kcd.ap, v3(bW[:, 256:512]), [bW], [sl.kcd])

        def tile_chain(t, h0, sl, gt):
            cols = slice(128 * t, 128 * t + 128)
            hs = slice(h0, h0 + 2)
            Sl = [self.Sh[h0], self.Sh[h0 + 1]]
            Sbl = [self.Sbh[h0], self.Sbh[h0 + 1]]
            for c in range(2):
                rows = slice(64 * c, 64 * c + 64)
                bS = sl.bank.next()
                for hh in range(2):
                    C.mm(bS[:, hh * 128:(hh + 1) * 128], sl.kcd[:, hh, :], Sbl[hh].ap, [sl.kcd, Sbl[hh]], bS)
                if own:
                    for hh in range(2):
                        C.mm(bS[:, 256 + hh * 128:256 + (hh + 1) * 128], gt.qnT[:, hh, cols], Sbl[hh].ap, [gt.qnT, Sbl[hh]], bS)
                yield
                C.tt("dve", sl.vn[rows, :, :], sl.w[rows, :, :], v3(bS[rows, 0:256]), ALU.subtract, [sl.w, bS], [sl.vn])
                if own:
                    C.tt("dve", sl.o1[rows, :, :], v3(bS[rows, 256:512]), bc(eg[rows, t, hs], [64, 2, 128], 2), ALU.mult,
                         [bS, eg], [sl.o1])
                yield
                bU = sl.bank.next()
                for hh in range(2):
                    C.mm(bU[:, hh * 128:(hh + 1) * 128], sl.kd[rows, hh, :], sl.vn[rows, hh, :], [sl.kd, sl.vn], bU)
                yield
                for hh in range(2):
                    C.stt("dve", Sbl[hh].ap, Sl[hh].ap, glb[:, c, t, h0 + hh:h0 + hh + 1], bU[:, hh * 128:(hh + 1) * 128],
                          ALU.mult, ALU.add, [Sl[hh], glb, bU], [Sbl[hh]])
                for hh in range(2):
                    C.stt("dve", Sl[hh].ap, Sl[hh].ap, glb[:, c, t, h0 + hh:h0 + hh + 1], bU[:, hh * 128:(hh + 1) * 128],
                          ALU.mult, ALU.add, [Sl[hh], glb, bU], [Sl[hh]])
                yield

        def tile_post(t, h0, sl, gt):
            cols = slice(128 * t, 128 * t + 128)
            hs = slice(h0, h0 + 2)
            if own:
                bO = sl.bank.next()
                for hh in range(2):
                    C.mm(bO[:, hh * 128:(hh + 1) * 128], sl.attnT[:, hh, :], sl.vn[:, hh, :], [sl.attnT, sl.vn], bO)
                yield
                C.tt("dve", sl.o1.ap, sl.o1.ap, v3(bO[:, 0:256]), ALU.add, [sl.o1, bO], [sl.o1])
                for hh in range(2):
                    C.act(junk.ap, sl.o1[:, hh, :], AF.Square, [sl.o1], [junk, sl.ss], accum_out=sl.ss[:, hh:hh + 1])
                self.rsqrt_act(sl.ss.ap, sl.ss.ap, [sl.ss], sl.ss, scale=1.0 / 128.0)
                C.tt("dve", sl.on.ap, sl.o1.ap, bc(sl.ss.ap, B3, 2), ALU.mult, [sl.o1, sl.ss], [sl.on])
                yield
                bO2 = sl.bank.next()
                bO2b = bO2.ap.bitcast(BF16)
                for hh in range(2):
                    C.tr(bO2b[:, hh * 128:(hh + 1) * 128], sl.on[:, hh, :], ident_b.ap, [sl.on, ident_b], bO2)
                yield
                C.stt("dve", self.ybT[:, hs, cols], v3(bO2b[:, 0:256]), self.wonorm[:, 0:1], gt.szT[:, :, cols], ALU.mult, ALU.mult,
                      [bO2, self.wonorm, gt.szT], [self.ybT])

        def run_group(h0, gt):
            nt = 8
            prep, post = {}, {}
            prep_done = [False] * nt
            slot_busy = [None] * NSLOT
            chain, chain_t, next_prep, fin = None, 0, 0, 0
            while fin < nt:
                while next_prep < nt and slot_busy[next_prep % NSLOT] is None:
                    slot_busy[next_prep % NSLOT] = next_prep
                    prep[next_prep] = tile_scan(next_prep, h0, slots[next_prep % NSLOT], gt)
                    next_prep += 1
                for t in sorted(prep):
                    try:
                        next(prep[t])
                    except StopIteration:
                        del prep[t]
                        prep_done[t] = True
                if chain is None and chain_t < nt and prep_done[chain_t]:
                    chain = tile_chain(chain_t, h0, slots[chain_t % NSLOT], gt)
                if chain is not None:
                    try:
                        next(chain)
                    except StopIteration:
                        chain = None
                        post[chain_t] = tile_post(chain_t, h0, slots[chain_t % NSLOT], gt)
                        chain_t += 1
                for t in sorted(post):
                    try:
                        next(post[t])
                    except StopIteration:
                        del post[t]
                        slot_busy[t % NSLOT] = None
                        fin += 1
                yield

        stop = os.environ.get("KB_STOP", "all")
        nhg = int(os.environ.get("KB_NHG", "8"))
        WIDTH_IP = int(os.environ.get("KB_WIP", "2"))
        RATIO = int(os.environ.get("KB_RATIO", "1"))
        IPSTEPS = int(os.environ.get("KB_IPSTEPS", "1"))
        NFILL = int(os.environ.get("KB_NFILL", "0"))
        FILLN = int(os.environ.get("KB_FILLN", "256"))
        nhg = nhg if stop != "pre" else 0

        def inproj_gens(hg, gt):
            h0 = 2 * hg
            cache = {}

            def getw(off):
                def f():
                    if off not in cache:
                        cache[off] = self.wload(wring, d["w_in"][:, off + hg * 256:off + (hg + 1) * 256], 256)
                    return cache[off]
                return f
            if own:
                for ki in range(3):
                    C.dma("sp", sctg[hg % 2][:, ki], d["scT"][:, 16 * ki + h0:16 * ki + h0 + 2], writes=[sctg[hg % 2]])
            gens = [qkv_chunk("k", hh, getw(OK_), h0, gt) for hh in range(2)]
            gens += [qkv_chunk("v", hh, getw(OV), h0, gt) for hh in range(2)]
            if own:
                gens += [qkv_chunk("q", hh, getw(OQ), h0, gt) for hh in range(2)]
                gens += [z_chunk(hh, getw(OZ), gt) for hh in range(2)]
            return gens

        def rr_gen(gens, width):
            active = []
            it = iter(gens)
            while True:
                while len(active) < width:
                    g_ = next(it, None)
                    if g_ is None:
                        break
                    active.append(g_)
                if not active:
                    return
                for g_ in list(active):
                    try:
                        next(g_)
                    except StopIteration:
                        active.remove(g_)
                yield

        def group_scan(hg, gt):
            if stop in ("scan", "all"):
                yield from run_group(2 * hg, gt)

        for hg in range(nhg + 1):
            sg = group_scan(hg - 1, gts[(hg - 1) % 2]) if hg >= 1 else None
            ig = rr_gen(inproj_gens(hg, gts[hg % 2]), WIDTH_IP) if hg < nhg else None
            while sg is not None or ig is not None:
                for _ in range(NFILL):
                    C.mm(self.bank[7][:, 0:FILLN], ones_b.ap, xT[:, 0, 0:FILLN], [ones_b, xT], self.bank[7])
                if sg is not None:
                    for _ in range(RATIO):
                        try:
                            next(sg)
                        except StopIteration:
                            sg = None
                            break
                for _ in range(IPSTEPS if ig is not None else 0):
                    try:
                        next(ig)
                    except StopIteration:
                        ig = None
                        if own:
                            gt_ = gts[hg % 2]
                            for nm in ("knT", "qnT", "vT", "szT"):
                                C.cp("dve", side[nm][:, 2 * hg:2 * hg + 2, :], getattr(gt_, nm)[:, :, 1024:1040],
                                     [getattr(gt_, nm)], [side[nm]])
                        break
        if own:
            allS = [self.Sh[h] for h in range(16)]
            C.dma("sp", d["ssm_p"], self.S.ap, reads=allS)
            if stop == "all":
                self.sample_phase(side, side_end, betas, egs, glbs)
            if "ybT" in d:
                self.dump_fm(self.ybT, "ybT", NOWN, at=self.baseB)
        C.barrier()


def _consts():
    i = np.arange(128)
    same = (i[:, None] // 64) == (i[None, :] // 64)
    ident = np.eye(128, dtype=np.float32)
    maskNb = np.where(same & (i[None, :] < i[:, None]), 0.0, BIGMASK).astype(np.float32)
    maskAb = np.where(same & (i[None, :] >= i[:, None]), 0.0, -BIGMASK).astype(np.float32)
    ucs = (same & (i[:, None] <= i[None, :])).astype(np.float32)
    bones = same.astype(np.float32)
    ci = np.zeros((128, 2, 128), np.float32)
    ci[0:64, 0, :] = 1.0
    ci[64:128, 1, :] = 1.0
    tri = (i[None, :] >= i[:, None]).astype(np.float32)
    i16b = np.ascontiguousarray(np.broadcast_to(np.eye(16, dtype=np.float32)[None], (128, 16, 16)))
    return dict(ident=ident, maskNb=maskNb, maskAb=maskAb, ucs=ucs, bones=bones, ci=ci, tri=tri, i16b=i16b)


def host_inputs(inp, c):
    b, half = c // 2, c % 2
    f = np.float32
    xp = inp["x_prompt"]
    xs = inp["x_sample"][16 * c:16 * c + 16, 0]
    own = xp[b, half * 1024:(half + 1) * 1024]
    xTo = np.zeros((2048, NCO), f)
    if half:
        xTo[:, 0:3] = xp[b, 1021:1024].T
    xTo[:, 3:1027] = own.T
    xTo[:, 1027:1043] = xs.T
    xTp = np.zeros((2048, NCP), f)
    if half:
        xTp[:, 3:] = xp[b, 0:1024].T
    xtok = np.concatenate([own, xs], 0)
    w_s = inp["w_s"][0]
    m = dict(
        xTo=xTo, xTp=xTp, xtok=np.ascontiguousarray(xtok),
        w_in=inp["w_in"][0], w_pa=inp["w_proj_a"][0], w_pb=inp["w_proj_b"][0], w_o=inp["w_o"][0],
        w_up=inp["w_up"][0], w_dn=inp["w_down"][0],
        w_sT=np.ascontiguousarray(w_s.transpose(2, 0, 1)),
        bs_row=np.ascontiguousarray(inp["b_s"][0].reshape(1, 2048)),
        bs0=np.ascontiguousarray(np.broadcast_to(inp["b_s"][0][:, 0][None, :, None], (1, 16, 16))),
        w00=np.ascontiguousarray(np.broadcast_to(w_s[:, 0, 0][None, :], (16, 16))),
        wconv=np.ascontiguousarray(inp["w_conv"][0].reshape(4, 48, 128).transpose(2, 1, 0)),
        alog=np.ascontiguousarray(np.broadcast_to(inp["a_log"][0][None, :], (128, 16))),
        dtb=np.ascontiguousarray(np.broadcast_to(inp["dt_bias"][0][None, :], (128, 16))),
        wonorm=np.ascontiguousarray(inp["w_onorm"][0].reshape(128, 1)),
        scT=np.ascontiguousarray(inp["state_conv"][0, 16 * c:16 * c + 16].reshape(16, 3, 48, 128).transpose(3, 2, 1, 0)),
        ssm=np.ascontiguousarray(inp["state_ssm"][0, 16 * c:16 * c + 16]),
    )
    for nm, key in [("lnv_g", "ln_v_g"), ("lnv_b", "ln_v_b"), ("ln1_g", "ln1_g"), ("ln1_b", "ln1_b"),
                    ("ln2_g", "ln2_g"), ("ln2_b", "ln2_b")]:
        m[nm] = np.ascontiguousarray(np.broadcast_to(inp[key][0][None, :], (128, 2048)))
    m.update(_consts())
    return {k: np.ascontiguousarray(v, dtype=f) for k, v in m.items()}


_PROG = None


def _program():
    global _PROG
    if _PROG is None:
        _PROG = Builder()
    return _PROG


def kernel(**inputs):
    inp = {k: np.asarray(v) for k, v in inputs.items()}
    B = _program()
    in_maps = []
    for c in range(8):
        m = host_inputs(inp, c)
        in_maps.append({k: v for k, v in m.items() if k in B.d})
    res = run_bass_kernel_spmd(B.nc, in_maps, core_ids=list(range(8)))
    R = res.results
    f = np.float32
    y_prompt = np.zeros((4, 2048, 2048), f)
    y_sample = np.zeros((128, 1, 2048), f)
    ncp = np.zeros((1, 4, 3, 6144), f)
    ssp = np.zeros((1, 4, 16, 128, 128), f)
    vs = np.zeros((1, 128, 1, 2048), f)
    ncs = np.zeros((1, 128, 3, 6144), f)
    sss = np.zeros((1, 128, 16, 128, 128), f)
    for c in range(8):
        b, half = c // 2, c % 2
        r = R[c]
        y_prompt[b, half * 1024:(half + 1) * 1024] = r["y_out"][0:1024]
        y_sample[16 * c:16 * c + 16, 0] = r["y_out"][1024:1040]
        vs[0, 16 * c:16 * c + 16, 0] = r["vs"]
        ncs[0, 16 * c:16 * c + 16] = r["ncs"].transpose(3, 2, 1, 0).reshape(16, 3, 6144)
        sss[0, 16 * c:16 * c + 16] = r["ssm_s"]
        if half:
            ncp[0, b] = r["ncp"].transpose(2, 1, 0).reshape(3, 6144)
            ssp[0, b] = r["ssm_p"].transpose(1, 0, 2)
    return (y_prompt, y_sample, ncp, ssp, vs, ncs, sss)
```
